# Optimizing a Trainium2 kernel written in Bass

```python
import math
import jax, jax.numpy as jnp
from jax import lax
import numpy as np

D_MODEL = 2048
BATCH = 4
SEQ = 2048
DEPTH = 1
DEC_BATCH = 128
DEC_SEQ = 8
PAST_LEN = 16384
PAGE_SIZE = 128

MIX_WIDTH = 2 * D_MODEL
C_A = MIX_WIDTH // 2
C_B = MIX_WIDTH - C_A
CHUNK = 128
N_HEADS_A = 8
HEAD_DIM_A = C_A // N_HEADS_A
GROUP_B = 16
N_GROUPS_B = C_B // GROUP_B
STATE_P = 64
IN_WIDTH = 3 * C_A + 2 * C_B
EPS = 1e-6
DT_MIN = 1e-3
DT_MAX = 1e-1

kernel_name = "hymba_gmlp_s5_decode_step"


def rmsnorm(x, g):
    xf = x.astype(jnp.float32)
    return xf * lax.rsqrt(jnp.mean(xf * xf, axis=-1, keepdims=True) + EPS) * g.astype(jnp.float32)


def chunk_spatial_mix(v, w_s, b_s):
    bsz, seqlen, nh, dh = v.shape
    pad = (-seqlen) % CHUNK
    vp = jnp.pad(v, ((0, 0), (0, pad), (0, 0), (0, 0)))
    n_chunks = (seqlen + pad) // CHUNK
    vp = vp.reshape(bsz, n_chunks, CHUNK, nh, dh)
    mask = jnp.tril(jnp.ones((CHUNK, CHUNK), jnp.float32))
    w = w_s.astype(jnp.float32) * mask[None]
    out = jnp.einsum('hts,bnshd->bnthd', w, vp) + b_s.astype(jnp.float32).T[None, None, :, :, None]
    return out.reshape(bsz, n_chunks * CHUNK, nh, dh)[:, :seqlen]


def ssm_combine(left, right):
    a1, b1 = left
    a2, b2 = right
    return a1 * a2, a2 * b1 + b2


def s5_branch(xb, h0, a_re, a_im, log_dt, b_re, b_im, c_re, c_im, d_skip):
    bsz, seqlen, _ = xb.shape
    lam = lax.complex(a_re.astype(jnp.float32), a_im.astype(jnp.float32))
    dt = jnp.exp(log_dt.astype(jnp.float32))[:, None]
    lam_bar = jnp.exp(lam * dt)
    b_c = lax.complex(b_re.astype(jnp.float32), b_im.astype(jnp.float32))
    b_bar = ((lam_bar - 1.0) / lam)[:, :, None] * b_c
    u = xb.reshape(bsz, seqlen, N_GROUPS_B, GROUP_B)
    bu = jnp.einsum('gpc,blgc->blgp', b_bar, u.astype(jnp.complex64))
    bu = bu.at[:, 0].add(lam_bar[None] * h0)
    a = jnp.broadcast_to(lam_bar, bu.shape)
    _, h = lax.associative_scan(ssm_combine, (a, bu), axis=1)
    c_c = lax.complex(c_re.astype(jnp.float32), c_im.astype(jnp.float32))
    y = jnp.einsum('gcp,blgp->blgc', c_c, h).real + d_skip.astype(jnp.float32).reshape(N_GROUPS_B, GROUP_B) * u
    return y.reshape(bsz, seqlen, C_B), h[:, -1]


def hybrid_layer(x, h0, g_norm, w_in, g_v, w_s, b_s, a_re, a_im, log_dt,
                 b_re, b_im, c_re, c_im, d_skip, w_glu, b_glu, w_out):
    bsz, seqlen, _ = x.shape
    xn = rmsnorm(x, g_norm)
    z = xn @ w_in.astype(jnp.float32)
    u, v, gate_a, xb, gate_b = jnp.split(z, [C_A, 2 * C_A, 3 * C_A, 3 * C_A + C_B], axis=-1)
    u = jax.nn.gelu(u)
    v = jax.nn.gelu(v).reshape(bsz, seqlen, N_HEADS_A, HEAD_DIM_A)
    v = rmsnorm(v, g_v.reshape(N_HEADS_A, HEAD_DIM_A))
    mixed = chunk_spatial_mix(v, w_s, b_s).reshape(bsz, seqlen, C_A)
    out_a = u * mixed * jax.nn.silu(gate_a)
    y_b, h_last = s5_branch(xb, h0, a_re, a_im, log_dt, b_re, b_im, c_re, c_im, d_skip)
    y_b = jax.nn.gelu(y_b)
    y_b = y_b * jax.nn.sigmoid(y_b @ w_glu.astype(jnp.float32) + b_glu.astype(jnp.float32))
    out_b = y_b * jax.nn.silu(gate_b)
    mix = jnp.concatenate([out_a, out_b], axis=-1)
    x_new = x + mix @ w_out.astype(jnp.float32)
    return x_new, h_last, v.reshape(bsz, seqlen, C_A)


def setup_inputs(seed: int = 0) -> dict:
    key = jax.random.key(seed)
    ks = jax.random.split(key, 24)
    f32 = jnp.float32
    nrm = lambda k, shape, s: jax.random.normal(k, shape, f32) * s
    x_prompt = nrm(ks[0], (BATCH, SEQ, D_MODEL), 1.0)
    x_sample = nrm(ks[1], (DEC_BATCH, DEC_SEQ, D_MODEL), 1.0)
    state_ssm_re = nrm(ks[2], (DEPTH, DEC_BATCH, N_GROUPS_B, STATE_P), 0.3)
    state_ssm_im = nrm(ks[3], (DEPTH, DEC_BATCH, N_GROUPS_B, STATE_P), 0.3)
    g_norm = 1.0 + nrm(ks[4], (DEPTH, D_MODEL), 0.02)
    w_in = nrm(ks[5], (DEPTH, D_MODEL, IN_WIDTH), D_MODEL ** -0.5)
    g_v = 1.0 + nrm(ks[6], (DEPTH, C_A), 0.02)
    w_s = nrm(ks[7], (DEPTH, N_HEADS_A, CHUNK, CHUNK), CHUNK ** -0.5)
    b_s = 1.0 + nrm(ks[8], (DEPTH, N_HEADS_A, CHUNK), 0.02)
    n_idx = jnp.arange(STATE_P, dtype=f32)
    a_re = -0.5 + nrm(ks[9], (DEPTH, N_GROUPS_B, STATE_P), 0.01)
    a_im = math.pi * n_idx[None, None, :] + nrm(ks[10], (DEPTH, N_GROUPS_B, STATE_P), 0.01)
    log_dt = jax.random.uniform(ks[11], (DEPTH, N_GROUPS_B), f32, math.log(DT_MIN), math.log(DT_MAX))
    b_re = nrm(ks[12], (DEPTH, N_GROUPS_B, STATE_P, GROUP_B), (2.0 * GROUP_B) ** -0.5)
    b_im = nrm(ks[13], (DEPTH, N_GROUPS_B, STATE_P, GROUP_B), (2.0 * GROUP_B) ** -0.5)
    c_re = nrm(ks[14], (DEPTH, N_GROUPS_B, GROUP_B, STATE_P), (2.0 * STATE_P) ** -0.5)
    c_im = nrm(ks[15], (DEPTH, N_GROUPS_B, GROUP_B, STATE_P), (2.0 * STATE_P) ** -0.5)
    d_skip = nrm(ks[16], (DEPTH, C_B), 1.0)
    w_glu = nrm(ks[17], (DEPTH, C_B, C_B), C_B ** -0.5)
    b_glu = nrm(ks[18], (DEPTH, C_B), 0.01)
    w_out = nrm(ks[19], (DEPTH, MIX_WIDTH, D_MODEL), MIX_WIDTH ** -0.5)
    g_final = 1.0 + nrm(ks[20], (D_MODEL,), 0.02)
    return {"x_prompt": x_prompt, "x_sample": x_sample,
            "state_ssm_re": state_ssm_re, "state_ssm_im": state_ssm_im,
            "g_norm": g_norm, "w_in": w_in, "g_v": g_v, "w_s": w_s, "b_s": b_s,
            "a_re": a_re, "a_im": a_im, "log_dt": log_dt,
            "b_re": b_re, "b_im": b_im, "c_re": c_re, "c_im": c_im, "d_skip": d_skip,
            "w_glu": w_glu, "b_glu": b_glu, "w_out": w_out, "g_final": g_final}


def reference(x_prompt, x_sample, state_ssm_re, state_ssm_im, g_norm, w_in, g_v, w_s, b_s,
              a_re, a_im, log_dt, b_re, b_im, c_re, c_im, d_skip, w_glu, b_glu, w_out, g_final):
    hp = x_prompt.astype(jnp.float32)
    hs = x_sample.astype(jnp.float32)
    re_p, im_p, re_s, im_s, v_s = [], [], [], [], []
    for l in range(DEPTH):
        params = (g_norm[l], w_in[l], g_v[l], w_s[l], b_s[l], a_re[l], a_im[l], log_dt[l],
                  b_re[l], b_im[l], c_re[l], c_im[l], d_skip[l], w_glu[l], b_glu[l], w_out[l])
        h0_p = jnp.zeros((hp.shape[0], N_GROUPS_B, STATE_P), jnp.complex64)
        h0_s = lax.complex(state_ssm_re[l].astype(jnp.float32), state_ssm_im[l].astype(jnp.float32))
        hp, hl_p, _ = hybrid_layer(hp, h0_p, *params)
        hs, hl_s, vrows_s = hybrid_layer(hs, h0_s, *params)
        re_p.append(hl_p.real)
        im_p.append(hl_p.imag)
        re_s.append(hl_s.real)
        im_s.append(hl_s.imag)
        v_s.append(vrows_s)
    y_prompt = rmsnorm(hp, g_final).astype(x_prompt.dtype)
    y_sample = rmsnorm(hs, g_final).astype(x_sample.dtype)
    new_ssm_re_prompt = jnp.stack(re_p).astype(x_prompt.dtype)
    new_ssm_im_prompt = jnp.stack(im_p).astype(x_prompt.dtype)
    new_ssm_re_sample = jnp.stack(re_s).astype(x_sample.dtype)
    new_ssm_im_sample = jnp.stack(im_s).astype(x_sample.dtype)
    new_chunk_v_sample = jnp.stack(v_s).astype(x_sample.dtype)
    return (y_prompt, y_sample, new_ssm_re_prompt, new_ssm_im_prompt,
            new_ssm_re_sample, new_ssm_im_sample, new_chunk_v_sample)
```

```python
import math
from contextlib import ExitStack
import numpy as np
import concourse.bass as bass
import concourse.mybir as mybir
from concourse.bass_utils import run_bass_kernel_spmd

F32 = mybir.dt.float32
BF16 = mybir.dt.bfloat16
I32 = mybir.dt.int32
ALU = mybir.AluOpType
AF = mybir.ActivationFunctionType

NCORES = 8
D = 2048
NT = 1152
NPR = 1024
EPS = 1e-6
TWO_PI = 2.0 * math.pi


class Ctx:
    def __init__(self, nc, stack):
        self.nc = nc
        self.stack = stack
        self.eng = {'pe': nc.tensor, 'act': nc.scalar, 'dve': nc.vector, 'pool': nc.gpsimd, 'sp': nc.sync}
        self.sems = {}
        self.cnt = {}
        self.seen = {e: {} for e in self.eng}
        self.last_w = {}
        self.readers = {}
        self.dead = False
        self.nops = 0
        self.max_ops = None
        for e in ['pe', 'act', 'dve', 'pool']:
            self.sems[e] = stack.enter_context(nc.semaphore("prog_" + e))
            self.cnt[e] = 0

    def chan(self, name):
        if name not in self.sems:
            self.sems[name] = self.stack.enter_context(self.nc.semaphore("ch_" + name))
            self.cnt[name] = 0
        return name

    def _deps(self, reads, writes):
        deps = []
        for r in reads:
            if r in self.last_w:
                deps.append(self.last_w[r])
        for w in writes:
            if w in self.last_w:
                deps.append(self.last_w[w])
            deps.extend(self.readers.get(w, []))
        return deps

    def _wait(self, e, deps):
        best = {}
        for (k, v) in deps:
            if e == 'pe' and k == 'pe':
                continue
            if v > best.get(k, 0):
                best[k] = v
        for k, v in best.items():
            if self.seen[e].get(k, 0) >= v:
                continue
            self.eng[e].wait_ge(self.sems[k], v)
            self.seen[e][k] = v

    def _record(self, key, val, reads, writes):
        for w in writes:
            self.last_w[w] = (key, val)
            self.readers[w] = []
        for r in reads:
            self.readers.setdefault(r, []).append((key, val))

    def _tick(self):
        self.nops += 1
        if self.max_ops is not None and self.nops > self.max_ops:
            self.dead = True
        return self.dead

    def op(self, e, fn, reads=(), writes=()):
        if self._tick():
            return None
        self._wait(e, self._deps(reads, writes))
        ins = fn()
        self.cnt[e] += 1
        ins.then_inc(self.sems[e], 1)
        self._record(e, self.cnt[e], reads, writes)
        return ins

    def dma(self, q, ch, out, in_, reads=(), writes=(), nowait=False):
        if self._tick():
            return None
        ch = self.chan(ch)
        if not nowait:
            self._wait(q, self._deps(reads, writes))
        ins = self.eng[q].dma_start(out=out, in_=in_)
        self.cnt[ch] += 16
        ins.then_inc(self.sems[ch], 16)
        self._record(ch, self.cnt[ch], reads, writes)
        return ins

    def fence(self, e, include_self=False):
        if self.dead:
            return
        for k, sm in self.sems.items():
            v = self.cnt[k]
            if v > 0 and self.seen[e].get(k, 0) < v and (include_self or k != e):
                self.eng[e].wait_ge(sm, v)
                self.seen[e][k] = v

    def barrier(self):
        if self.dead:
            return
        for e in self.eng:
            for k, s in self.sems.items():
                v = self.cnt[k]
                if v > 0 and self.seen[e].get(k, 0) < v:
                    self.eng[e].wait_ge(s, v)
                    self.seen[e][k] = v
        self.last_w = {}
        self.readers = {}


class _StopBuild(Exception):
    pass


def build_nc(stop=99, max_ops=None):
    nc = bass.Bass("TRN2", target_bir_lowering=False)

    def din(name, shape, dt=F32):
        return nc.dram_tensor(name, shape, dt, kind="ExternalInput").ap()

    def dout(name, shape, dt=F32):
        return nc.dram_tensor(name, shape, dt, kind="ExternalOutput").ap()

    xm = din("xm", [NT, D])
    xp = din("xp", [NPR, D])
    h0 = din("h0", [128, 64, 2, 16])
    w_in = din("w_in", [D, 10240])
    w_glu = din("w_glu", [D, D])
    w_out = din("w_out", [2 * D, D])
    gnb = din("gnb", [128, D])
    gncol = din("gncol", [128, 16])
    gvb = din("gvb", [128, D])
    gfb = din("gfb", [128, D])
    bglu = din("bglu", [128, 16])
    wsT = din("wsT", [128, 8, 128])
    wsS = din("wsS", [128, 8, 128])
    mask_ts = din("mask_ts", [128, 128])
    mask_blk = din("mask_blk", [128, 128])
    mask16 = din("mask16", [128, 128])
    bsrow = din("bsrow", [1, 8, 128])
    bsSrow = din("bsSrow", [1, 8, 128])
    are = din("are", [128, 64])
    aim = din("aim", [128, 64])
    ldt = din("ldt", [128, 64])
    bre = din("bre", [128, 64, 16])
    bim = din("bim", [128, 64, 16])
    cre = din("cre", [128, 64, 16])
    cim = din("cim", [128, 64, 16])
    dcol = din("dcol", [128, 128])
    qv = din("qv", [128, 24, 64])
    identf_in = din("identf_in", [128, 128])
    permI = din("permI", [128, 128])

    y_out = dout("y", [NT, D])
    hp_out = dout("hp", [128, 64, 2])
    hs_out = dout("hs", [128, 64, 2, 16])
    vns_out = dout("vns", [128, D])

    s_bptz = nc.dram_tensor("s_bptz", [128, 128, 2, 128], BF16).ap()
    s_cq1 = nc.dram_tensor("s_cq1", [128, 64, 2, 128], BF16).ap()
    s_msup = nc.dram_tensor("s_msup", [128, 128, 128], BF16).ap()

    with ExitStack() as st:
        c = Ctx(nc, st)
        c.max_ops = max_ops
        V, A, P, T = nc.vector, nc.scalar, nc.gpsimd, nc.tensor
        try:

            uid = [0]

            def sbt(stack, name, shape, dt):
                uid[0] += 1
                return stack.enter_context(nc.sbuf_tensor("%s_%d" % (name, uid[0]), shape, dt))

            def pst(stack, name, shape, dt):
                uid[0] += 1
                return stack.enter_context(nc.psum_tensor("%s_%d" % (name, uid[0]), shape, dt))

            def tt(e, out, in0, in1, op, R, W):
                eng = {'dve': V, 'pool': P}[e]
                return c.op(e, lambda: eng.tensor_tensor(out=out, in0=in0, in1=in1, op=op), R, W)

            def ts(e, out, in0, s1, s2, op0, op1, R, W):
                eng = {'dve': V, 'pool': P}[e]
                if op1 is None:
                    return c.op(e, lambda: eng.tensor_scalar(out=out, in0=in0, scalar1=s1, scalar2=None, op0=op0), R, W)
                return c.op(e, lambda: eng.tensor_scalar(out=out, in0=in0, scalar1=s1, scalar2=s2, op0=op0, op1=op1), R, W)

            def stt(out, in0, scalar, in1, op0, op1, R, W):
                return c.op('dve', lambda: V.scalar_tensor_tensor(out=out, in0=in0, scalar=scalar, in1=in1, op0=op0, op1=op1), R, W)

            def act(out, in_, func, R, W, **kw):
                return c.op('act', lambda: A.activation(out=out, in_=in_, func=func, **kw), R, W)

            def cp(e, out, in_, R, W):
                if e == 'act':
                    return c.op('act', lambda: A.copy(out=out, in_=in_), R, W)
                eng = {'dve': V, 'pool': P}[e]
                return c.op(e, lambda: eng.tensor_copy(out=out, in_=in_), R, W)

            def mm(out, lhsT, rhs, start, stop, R, W):
                return c.op('pe', lambda: T.matmul(out, lhsT=lhsT, rhs=rhs, start=start, stop=stop), R, W)

            def tr(out, in_, ident, R, W):
                return c.op('pe', lambda: T.transpose(out=out, in_=in_, identity=ident), R, W)

            def memset(e, ap, val, W):
                eng = {'dve': V, 'pool': P}[e]
                return c.op(e, lambda: eng.memset(ap, val), (), W)

            identf = sbt(st, "identf", [128, 128], F32)
            identb = sbt(st, "identb", [128, 128], BF16)
            A8 = sbt(st, "A8", [128, 2, 64], F32)
            Hmid = sbt(st, "Hmid", [128, 64, 2], F32)
            ones1 = sbt(st, "ones1", [1, 128], BF16)
            W8 = sbt(st, "W8", [128, 64, 4], F32)

            def sbt_r(stack, name, shape, dt):
                uid[0] += 1
                return stack.enter_context(nc.sbuf_tensor("%s_%d" % (name, uid[0]), shape, dt, side="right"))

            c.dma('sp', 'ldc', identf[:], identf_in[:, :], writes=['identf'])
            cp('dve', identb[:], identf[:], ['identf'], ['identb'])
            memset('dve', ones1[:], 1.0, ['ones1'])

            wslot = [0]

            def load_w(Wbufs, wname, src_ap, ncols_total_view):
                i = wslot[0] % len(Wbufs)
                wslot[0] += 1
                nm = "%s%d" % (wname, i)
                srcv = src_ap.rearrange("(k p) n -> p k n", p=128)
                for kq in range(4):
                    grp = 'a' if kq == 0 else 'b'
                    c.dma('pool', 'w_%s_%s' % (nm, grp), Wbufs[i][:, kq * 4:(kq + 1) * 4, 0:ncols_total_view], srcv[:, kq * 4:(kq + 1) * 4, :],
                          writes=['%s_%s' % (nm, grp)], nowait=(kq > 1))
                return Wbufs[i], nm

            def phaseA_gen(pa, x_dram, ntiles, xnT):
                gnb_t = sbt(pa, "gnb_tg", [128, D], F32)
                xl = [sbt(pa, "xlg%d" % i, [128, D], F32) for i in range(2)]
                xnb = [sbt(pa, "xnbg%d" % i, [128, D], BF16) for i in range(2)]
                ss = sbt(pa, "ssg", [128, 16], F32)
                rs = sbt(pa, "rsg", [128, 16], F32)
                psA = [pst(pa, "psAg%d" % i, [128, 1024], BF16) for i in range(4)]
                epsg = sbt(pa, "epsg", [128, 1], F32)
                c.dma('sp', 'ld_gnbg', gnb_t[:], gnb[:, :], writes=['gnbg'])
                memset('dve', ss[:], 0.0, ['ssg'])
                memset('dve', epsg[:], EPS, ['epsg'])

                def tposes(t):
                    b = t % 2
                    xbn = "xnbg%d" % b
                    for hb in range(2):
                        pa_ = psA[(t % 2) * 2 + hb]
                        pan = "psAg%d" % ((t % 2) * 2 + hb)
                        for k in range(8):
                            kk = hb * 8 + k
                            tr(pa_[:, k * 128:(k + 1) * 128], xnb[b][:, kk * 128:(kk + 1) * 128], identb[:, :],
                               [xbn, 'identb'], [pan])
                        cp('act' if hb == 0 else 'dve', xnT[:, hb * 8:(hb + 1) * 8, t * 128:(t + 1) * 128],
                           pa_[:].rearrange("p (k n) -> p k n", k=8), [pan], ['xnT'])

                for t in range(ntiles):
                    b = t % 2
                    xn_, xbn = "xlg%d" % b, "xnbg%d" % b
                    c.dma('sp', 'ld_' + xn_, xl[b][:], x_dram[t * 128:(t + 1) * 128, :], writes=[xn_])
                    act(xnb[b][:], xl[b][:], AF.Square, [xn_], [xbn, 'ssg'], accum_out=ss[:, t:t + 1])
                    act(rs[:, t:t + 1], ss[:, t:t + 1], AF.Ln, ['ssg', 'epsg'], ['rsg'], scale=1.0 / D, bias=epsg[:, 0:1])
                    act(rs[:, t:t + 1], rs[:, t:t + 1], AF.Exp, ['rsg'], ['rsg'], scale=-0.5)
                    stt(xnb[b][:], xl[b][:], rs[:, t:t + 1], gnb_t[:], ALU.mult, ALU.mult, [xn_, 'rsg', 'gnbg'], [xbn])
                    if t >= 1:
                        tposes(t - 1)
                    yield
                tposes(ntiles - 1)
                yield

            def phaseA(stack_parent, x_dram, ntiles, xnT):
                with ExitStack() as pa:
                    gnb_t = sbt(pa, "gnb_t", [128, D], F32)
                    xl = [sbt(pa, "xl%d" % i, [128, D], F32) for i in range(2)]
                    junk = sbt(pa, "junkA", [128, D], BF16)
                    xnb = [sbt(pa, "xnb%d" % i, [128, D], BF16) for i in range(2)]
                    ss = sbt(pa, "ssA", [128, 16], F32)
                    rs = sbt(pa, "rsA", [128, 16], F32)
                    psA = [pst(pa, "psA%d" % i, [128, 1024], BF16) for i in range(4)]
                    c.dma('sp', 'ld_gnb', gnb_t[:], gnb[:, :], writes=['gnb'])
                    memset('dve', ss[:], 0.0, ['ssA'])
                    for t in range(ntiles):
                        b = t % 2
                        xn_, xbn = "xl%d" % b, "xnb%d" % b
                        c.dma('sp', 'ld_' + xn_, xl[b][:], x_dram[t * 128:(t + 1) * 128, :], writes=[xn_])
                        act(junk[:], xl[b][:], AF.Square, [xn_], ['junkA', 'ssA'], accum_out=ss[:, t:t + 1])
                        ts('dve', rs[:, t:t + 1], ss[:, t:t + 1], 1.0 / D, EPS, ALU.mult, ALU.add, ['ssA'], ['rsA'])
                        act(rs[:, t:t + 1], rs[:, t:t + 1], AF.Sqrt, ['rsA'], ['rsA'])
                        c.op('dve', lambda: V.reciprocal(out=rs[:, t:t + 1], in_=rs[:, t:t + 1]), ['rsA'], ['rsA'])
                        stt(xnb[b][:], xl[b][:], rs[:, t:t + 1], gnb_t[:], ALU.mult, ALU.mult, [xn_, 'rsA', 'gnb'], [xbn])
                        for hb in range(2):
                            pa_ = psA[(t % 2) * 2 + hb]
                            pan = "psA%d" % ((t % 2) * 2 + hb)
                            for k in range(8):
                                kk = hb * 8 + k
                                tr(pa_[:, k * 128:(k + 1) * 128], xnb[b][:, kk * 128:(kk + 1) * 128], identb[:, :],
                                   [xbn, 'identb'], [pan])
                            cp('act' if hb == 0 else 'dve', xnT[:, hb * 8:(hb + 1) * 8, t * 128:(t + 1) * 128],
                               pa_[:].rearrange("p (k n) -> p k n", k=8), [pan], ['xnT'])
                c.barrier()

            pfx = ExitStack()
            X2p = sbt_r(pfx, "X2p", [128, D, 8], BF16)
            with ExitStack() as p0:
                LR = sbt(p0, "LR", [128, 24, 64], F32)
                LI = sbt(p0, "LI", [128, 24, 64], F32)
                Bbr = sbt(p0, "Bbr", [128, 64, 16], F32)
                Bbi = sbt(p0, "Bbi", [128, 64, 16], F32)
                cre_t = sbt(p0, "cre_t", [128, 64, 16], F32)
                cim_t = sbt(p0, "cim_t", [128, 64, 16], F32)
                dcol_t = sbt(p0, "dcol_t", [128, 128], F32)
                m16_t = sbt(p0, "m16_t", [128, 128], F32)
                permI_t = sbt(p0, "permI_t", [128, 128], F32)
                c.dma('sp', 'ld_permI', permI_t[:], permI[:, :], writes=['permI'])
                c.dma('sp', 'ld_dcol', dcol_t[:], dcol[:, :], writes=['dcol'])
                c.dma('sp', 'ld_m16', m16_t[:], mask16[:, :], writes=['m16'])
                c.dma('sp', 'ld_cre', cre_t[:], cre[:, :, :], writes=['cre'])
                c.dma('sp', 'ld_cim', cim_t[:], cim[:, :, :], writes=['cim'])
                with ExitStack() as pre:
                    are_t = sbt(pre, "are_t", [128, 64], F32)
                    aim_t = sbt(pre, "aim_t", [128, 64], F32)
                    ldt_t = sbt(pre, "ldt_t", [128, 64], F32)
                    qv_t = sbt(pre, "qv_t", [128, 24, 64], F32)
                    bre_t = sbt(pre, "bre_t", [128, 64, 16], F32)
                    bim_t = sbt(pre, "bim_t", [128, 64, 16], F32)
                    for (tl, src, nm) in [(are_t, are, 'are'), (aim_t, aim, 'aim'), (ldt_t, ldt, 'ldt')]:
                        c.dma('sp', 'ld_' + nm, tl[:], src[:, :], writes=[nm])
                    c.dma('sp', 'ld_qv', qv_t[:], qv[:, :, :], writes=['qv'])
                    for (tl, src, nm) in [(bre_t, bre, 'bre'), (bim_t, bim, 'bim')]:
                        c.dma('sp', 'ld_' + nm, tl[:], src[:, :, :], writes=[nm])
                    dt_t = sbt(pre, "dt_t", [128, 64], F32)
                    er_t = sbt(pre, "er_t", [128, 64], F32)
                    et_t = sbt(pre, "et_t", [128, 64], F32)
                    act(dt_t[:], ldt_t[:], AF.Exp, ['ldt'], ['dt'])
                    tt('dve', er_t[:], are_t[:], dt_t[:], ALU.mult, ['are', 'dt'], ['er'])
                    tt('dve', et_t[:], aim_t[:], dt_t[:], ALU.mult, ['aim', 'dt'], ['et'])
                    ts('dve', et_t[:], et_t[:], 1.0 / TWO_PI, None, ALU.mult, None, ['et'], ['et'])
                    MAG = sbt(pre, "MAG", [128, 24, 64], F32)
                    TH = sbt(pre, "TH", [128, 24, 64], F32)
                    TIi = sbt(pre, "TIi", [128, 24, 64], I32)
                    TF = sbt(pre, "TF", [128, 24, 64], F32)
                    er_b = er_t[:].unsqueeze(1).to_broadcast([128, 24, 64])
                    et_b = et_t[:].unsqueeze(1).to_broadcast([128, 24, 64])
                    tt('dve', MAG[:], qv_t[:], er_b, ALU.mult, ['qv', 'er'], ['MAG'])
                    act(MAG[:], MAG[:], AF.Exp, ['MAG'], ['MAG'])
                    tt('dve', TH[:], qv_t[:], et_b, ALU.mult, ['qv', 'et'], ['TH'])
                    cp('dve', TIi[:], TH[:], ['TH'], ['TIi'])
                    cp('dve', TF[:], TIi[:], ['TIi'], ['TF'])
                    tt('dve', TF[:], TH[:], TF[:], ALU.subtract, ['TH', 'TF'], ['TF'])
                    act(LI[:], TF[:], AF.Sin, ['TF'], ['LI'], scale=TWO_PI)
                    TH2 = sbt(pre, "TH2", [128, 24, 64], F32)
                    TIi2 = sbt(pre, "TIi2", [128, 24, 64], I32)
                    TF2 = sbt(pre, "TF2", [128, 24, 64], F32)
                    ts('pool', TH2[:], TH[:], 0.25, None, ALU.add, None, ['TH'], ['TH2'])
                    cp('dve', TIi2[:], TH2[:], ['TH2'], ['TIi2'])
                    cp('dve', TF2[:], TIi2[:], ['TIi2'], ['TF2'])
                    tt('pool', TF2[:], TH2[:], TF2[:], ALU.subtract, ['TH2', 'TF2'], ['TF2'])
                    act(LR[:], TF2[:], AF.Sin, ['TF2'], ['LR'], scale=TWO_PI)
                    tt('dve', LR[:], LR[:], MAG[:], ALU.mult, ['LR', 'MAG'], ['LR'])
                    tt('dve', LI[:], LI[:], MAG[:], ALU.mult, ['LI', 'MAG'], ['LI'])
                    cp('dve', A8[:, 0, :], LR[:, 23, :], ['LR'], ['A8'])
                    cp('dve', A8[:, 1, :], LI[:, 23, :], ['LI'], ['A8'])
                    cp('dve', W8[:, :, 0], LR[:, 23, :], ['LR'], ['W8'])
                    cp('dve', W8[:, :, 3], LR[:, 23, :], ['LR'], ['W8'])
                    cp('dve', W8[:, :, 2], LI[:, 23, :], ['LI'], ['W8'])
                    ts('dve', W8[:, :, 1], LI[:, 23, :], -1.0, None, ALU.mult, None, ['LI'], ['W8'])
                    nr = sbt(pre, "nr", [128, 64], F32)
                    den = sbt(pre, "den", [128, 64], F32)
                    t_a = sbt(pre, "t_a", [128, 64], F32)
                    t_b = sbt(pre, "t_b", [128, 64], F32)
                    cr = sbt(pre, "cr", [128, 64], F32)
                    ci = sbt(pre, "ci", [128, 64], F32)
                    ni = LI[:, 16, :]
                    ts('dve', nr[:], LR[:, 16, :], -1.0, None, ALU.add, None, ['LR'], ['nr'])
                    tt('dve', den[:], are_t[:], are_t[:], ALU.mult, ['are'], ['den'])
                    tt('dve', t_a[:], aim_t[:], aim_t[:], ALU.mult, ['aim'], ['t_a'])
                    tt('dve', den[:], den[:], t_a[:], ALU.add, ['den', 't_a'], ['den'])
                    c.op('dve', lambda: V.reciprocal(out=den[:], in_=den[:]), ['den'], ['den'])
                    tt('dve', t_a[:], nr[:], are_t[:], ALU.mult, ['nr', 'are'], ['t_a'])
                    tt('dve', t_b[:], ni, aim_t[:], ALU.mult, ['LI', 'aim'], ['t_b'])
                    tt('dve', t_a[:], t_a[:], t_b[:], ALU.add, ['t_a', 't_b'], ['t_a'])
                    tt('dve', cr[:], t_a[:], den[:], ALU.mult, ['t_a', 'den'], ['cr'])
                    tt('dve', t_a[:], ni, are_t[:], ALU.mult, ['LI', 'are'], ['t_a'])
                    tt('dve', t_b[:], nr[:], aim_t[:], ALU.mult, ['nr', 'aim'], ['t_b'])
                    tt('dve', t_a[:], t_a[:], t_b[:], ALU.subtract, ['t_a', 't_b'], ['t_a'])
                    tt('dve', ci[:], t_a[:], den[:], ALU.mult, ['t_a', 'den'], ['ci'])
                    tq1 = sbt(pre, "tq1", [128, 64, 16], F32)
                    tq2 = sbt(pre, "tq2", [128, 64, 16], F32)
                    cr_b = cr[:].unsqueeze(2).to_broadcast([128, 64, 16])
                    ci_b = ci[:].unsqueeze(2).to_broadcast([128, 64, 16])
                    tt('dve', Bbr[:], bre_t[:], cr_b, ALU.mult, ['bre', 'cr'], ['Bbr'])
                    tt('pool', tq1[:], bim_t[:], ci_b, ALU.mult, ['bim', 'ci'], ['tq1'])
                    tt('dve', Bbr[:], Bbr[:], tq1[:], ALU.subtract, ['Bbr', 'tq1'], ['Bbr'])
                    tt('dve', Bbi[:], bim_t[:], cr_b, ALU.mult, ['bim', 'cr'], ['Bbi'])
                    tt('pool', tq2[:], bre_t[:], ci_b, ALU.mult, ['bre', 'ci'], ['tq2'])
                    tt('dve', Bbi[:], Bbi[:], tq2[:], ALU.add, ['Bbi', 'tq2'], ['Bbi'])
                c.barrier()

                BPr = [sbt(p0, "BPr%d" % i, [128, 4, 128], BF16) for i in range(2)]
                BPi = [sbt(p0, "BPi%d" % i, [128, 4, 128], BF16) for i in range(2)]
                CQr = [sbt(p0, "CQr%d" % i, [128, 4, 256], BF16) for i in range(2)]
                CQn = [sbt(p0, "CQn%d" % i, [128, 4, 256], BF16) for i in range(2)]
                Dd = sbt(p0, "Dd", [128, 8, 128], BF16)
                Msc = [sbt(p0, "Msc%d" % i, [128, 8, 128], BF16) for i in range(2)]
                BPTz = sbt(p0, "BPTz", [128, 8, 2, 128], BF16)
                TA = [sbt(p0, "TA%d" % i, [128, 4, 128], F32) for i in range(4)]
                TC = [sbt(p0, "TC%d" % i, [128, 4, 256], F32) for i in range(4)]
                psM = [pst(p0, "psM%d" % i, [128, 512], F32) for i in range(4)]
                psTb = [pst(p0, "psTb%d" % i, [128, 1024], BF16) for i in range(2)]
                memset('pool', BPTz[:], 0.0, ['BPTz'])

                def p0_chunk(ch):
                    p0_ = ch * 4
                    g0 = ch * 8
                    db = ch % 2
                    bpr, bprn = BPr[db], "BPr%d" % db
                    bpi, bpin = BPi[db], "BPi%d" % db
                    cqr, cqrn = CQr[db], "CQr%d" % db
                    cqn, cqnn = CQn[db], "CQn%d" % db
                    msc, mscn = Msc[db], "Msc%d" % db
                    LRb = LR[:, 0:8, p0_:p0_ + 4].rearrange("p i r -> p r i").unsqueeze(3).to_broadcast([128, 4, 8, 16])
                    LIb = LI[:, 0:8, p0_:p0_ + 4].rearrange("p i r -> p r i").unsqueeze(3).to_broadcast([128, 4, 8, 16])
                    Bbr_b = Bbr[:, p0_:p0_ + 4, :].unsqueeze(2).to_broadcast([128, 4, 8, 16])
                    Bbi_b = Bbi[:, p0_:p0_ + 4, :].unsqueeze(2).to_broadcast([128, 4, 8, 16])
                    tav = [TA[i][:].rearrange("p r (c i) -> p r i c", i=8) for i in range(4)]
                    tt('pool', tav[0], LRb, Bbr_b, ALU.mult, ['LR', 'Bbr'], ['TA0'])
                    tt('pool', tav[1], LIb, Bbi_b, ALU.mult, ['LI', 'Bbi'], ['TA1'])
                    tt('dve', bpr[:], TA[0][:], TA[1][:], ALU.subtract, ['TA0', 'TA1'], [bprn])
                    tt('pool', tav[2], LRb, Bbi_b, ALU.mult, ['LR', 'Bbi'], ['TA2'])
                    tt('pool', tav[3], LIb, Bbr_b, ALU.mult, ['LI', 'Bbr'], ['TA3'])
                    tt('dve', bpi[:], TA[2][:], TA[3][:], ALU.add, ['TA2', 'TA3'], [bpin])
                    LRc = LR[:, 8:24, p0_:p0_ + 4].rearrange("p s r -> p r s").unsqueeze(3).to_broadcast([128, 4, 16, 16])
                    LIc = LI[:, 8:24, p0_:p0_ + 4].rearrange("p s r -> p r s").unsqueeze(3).to_broadcast([128, 4, 16, 16])
                    cre_b = cre_t[:, p0_:p0_ + 4, :].unsqueeze(2).to_broadcast([128, 4, 16, 16])
                    cim_b = cim_t[:, p0_:p0_ + 4, :].unsqueeze(2).to_broadcast([128, 4, 16, 16])
                    tcv = [TC[i][:].rearrange("p r (s c) -> p r s c", s=16) for i in range(4)]
                    cqrv = cqr[:].rearrange("p r (s c) -> p r s c", s=16)
                    cqnv = cqn[:].rearrange("p r (s c) -> p r s c", s=16)
                    tt('pool', tcv[0], LRc, cre_b, ALU.mult, ['LR', 'cre'], ['TC0'])
                    tt('pool', tcv[1], LIc, cim_b, ALU.mult, ['LI', 'cim'], ['TC1'])
                    tt('dve', cqrv, tcv[0], tcv[1], ALU.subtract, ['TC0', 'TC1'], [cqrn])
                    tt('pool', tcv[2], LRc, cim_b, ALU.mult, ['LR', 'cim'], ['TC2'])
                    tt('dve', tcv[3], LIc, cre_b, ALU.mult, ['LI', 'cre'], ['TC3'])
                    stt(cqnv, tcv[2], -1.0, tcv[3], ALU.mult, ALU.subtract, ['TC2', 'TC3'], [cqnn])
                    for gi in range(8):
                        ts('dve', Dd[:, gi, :], permI_t[:, :], dcol_t[:, g0 + gi:g0 + gi + 1], None, ALU.mult, None, ['permI', 'dcol'], ['Dd'])
                    for gq in range(2):
                        pm = psM[db * 2 + gq]
                        pmn = "psM%d" % (db * 2 + gq)
                        for gl in range(4):
                            gi = gq * 4 + gl
                            pl, g2 = gi // 2, gi % 2
                            hs = slice(g2 * 64, g2 * 64 + 64)
                            o_ = pm[:, gl * 128:(gl + 1) * 128]
                            mm(o_, bpr[hs, pl, :], cqr[hs, pl, 0:128], True, False, [bprn, cqrn], [pmn])
                            mm(o_, bpi[hs, pl, :], cqn[hs, pl, 0:128], False, False, [bpin, cqnn], [pmn])
                            mm(o_, identb[:, :], Dd[:, gi, :], False, True, ['identb', 'Dd'], [pmn])
                    ptn_ = "psTb%d" % db
                    ptv = psTb[db][:, 0:1024].rearrange("p (a r q) -> p a r q", a=4, r=2)
                    for a in range(4):
                        tr(ptv[:, a, 0, :], bpr[:, a, :], identb[:, :], [bprn, 'identb'], [ptn_])
                        tr(ptv[:, a, 1, :], bpi[:, a, :], identb[:, :], [bpin, 'identb'], [ptn_])

                def p0_finish(ch):
                    p0_ = ch * 4
                    g0 = ch * 8
                    db = ch % 2
                    cqr, cqrn = CQr[db], "CQr%d" % db
                    cqn, cqnn = CQn[db], "CQn%d" % db
                    msc, mscn = Msc[db], "Msc%d" % db
                    for gq in range(2):
                        pm = psM[db * 2 + gq]
                        pmn = "psM%d" % (db * 2 + gq)
                        tt('dve', msc[:, gq * 4:(gq + 1) * 4, :], pm[:].rearrange("p (g n) -> p g n", g=4),
                           m16_t[:].unsqueeze(1).to_broadcast([128, 4, 128]), ALU.mult, [pmn, 'm16'], [mscn])
                    ptn_ = "psTb%d" % db
                    ptv = psTb[db][:, 0:1024].rearrange("p (a r q) -> p a r q", a=4, r=2)
                    bzv = BPTz[:].rearrange("p (a g2) r q -> p a g2 r q", g2=2)
                    for g2 in range(2):
                        cp('dve', bzv[:, :, g2, :, g2 * 64:(g2 + 1) * 64], ptv[:, :, :, g2 * 64:(g2 + 1) * 64], [ptn_], ['BPTz'])
                    c.dma('sp', 'st_BPTz', s_bptz[:, g0:g0 + 8, :, :], BPTz[:], reads=['BPTz'])
                    c.dma('sp', 'st_' + cqrn, s_cq1[:, p0_:p0_ + 4, 0, :], cqr[:, :, 128:256], reads=[cqrn])
                    c.dma('sp', 'st_' + cqnn, s_cq1[:, p0_:p0_ + 4, 1, :], cqn[:, :, 128:256], reads=[cqnn])
                    c.dma('sp', 'st_' + mscn, s_msup[:, g0:g0 + 8, :], msc[:], reads=[mscn])

                xnTp = sbt(p0, "xnTp", [128, 16, NPR], BF16)

                def prefix_front():
                    gcol = sbt(p0, "gcol", [128, 16], F32)
                    epsb = sbt(p0, "epsb", [128, 1], F32)
                    c.dma('act', 'ld_gcol', gcol[:], gncol[:, :], writes=['gcol'])
                    with ExitStack() as pa:
                        xl = [sbt(pa, "xlp%d" % i, [128, D], F32) for i in range(2)]
                        xnb = [sbt(pa, "xnbp%d" % i, [128, D], BF16) for i in range(2)]
                        ss = sbt(pa, "ssp", [128, 16], F32)
                        rs = sbt(pa, "rsp", [128, 16], F32)
                        psA = [pst(pa, "psAp%d" % i, [128, 1024], BF16) for i in range(2)]
                        c.op('act', lambda: A.activation(out=epsb[:], in_=gcol[:, 0:1], func=AF.Copy, scale=0.0, bias=EPS),
                             ['gcol'], ['epsb'])
                        c.op('act', lambda: A.activation(out=ss[:], in_=gcol[:, :], func=AF.Copy, scale=0.0), ['gcol'], ['ssp'])

                        def tposes(t):
                            b = t % 2
                            xbn = "xnbp%d" % b
                            for hb in range(2):
                                pa_, pan = psA[hb], "psAp%d" % hb
                                for k in range(8):
                                    kk = hb * 8 + k
                                    tr(pa_[:, k * 128:(k + 1) * 128], xnb[b][:, kk * 128:(kk + 1) * 128], identb[:, :],
                                       [xbn, 'identb'], [pan])
                                for k in range(8):
                                    kk = hb * 8 + k
                                    act(xnTp[:, kk, t * 128:(t + 1) * 128], pa_[:, k * 128:(k + 1) * 128], AF.Copy,
                                        [pan, 'gcol'], ['xnTp'], scale=gcol[:, kk:kk + 1])

                        c.dma('act', 'ld_xlp0', xl[0][:], xp[0:128, :], writes=['xlp0'])
                        for t in range(8):
                            b = t % 2
                            xn_, xbn = "xlp%d" % b, "xnbp%d" % b
                            if t + 1 < 8:
                                nb_ = (t + 1) % 2
                                c.dma('act', 'ld_xlp%d' % nb_, xl[nb_][:], xp[(t + 1) * 128:(t + 2) * 128, :], writes=['xlp%d' % nb_])
                            act(xnb[b][:], xl[b][:], AF.Square, [xn_], [xbn, 'ssp'], accum_out=ss[:, t:t + 1])
                            act(rs[:, t:t + 1], ss[:, t:t + 1], AF.Ln, ['ssp', 'epsb'], ['rsp'], scale=1.0 / D, bias=epsb[:, 0:1])
                            act(rs[:, t:t + 1], rs[:, t:t + 1], AF.Exp, ['rsp'], ['rsp'], scale=-0.5)
                            act(xnb[b][:], xl[b][:], AF.Copy, [xn_, 'rsp'], [xbn], scale=rs[:, t:t + 1])
                            if t >= 1:
                                tposes(t - 1)
                            yield
                        tposes(7)
                        yield
                    with ExitStack() as pb:
                        Wb = [sbt(pb, "Wxp%d" % i, [128, 16, 256], BF16) for i in range(2)]
                        Wst = [sbt(pb, "Wst%d" % i, [128, 4, 256], F32) for i in range(3)]
                        psX = [pst(pb, "psXp%d" % i, [128, 512], F32) for i in range(2)]
                        pi = 0

                        def issue_piece(p):
                            if p >= 32:
                                return
                            cb_, kq_ = p // 4, p % 4
                            srcv = w_in[:, 6144 + cb_ * 256:6144 + (cb_ + 1) * 256].rearrange("(k p) n -> p k n", p=128)
                            c.dma('act', 'ld_Wst%d' % (p % 3), Wst[p % 3][:], srcv[:, kq_ * 4:(kq_ + 1) * 4, :], writes=['Wst%d' % (p % 3)])

                        for p in range(3):
                            issue_piece(p)
                        for cb in range(8):
                            Wt, wn = Wb[cb % 2], "Wxp%d" % (cb % 2)
                            for kq in range(4):
                                p = cb * 4 + kq
                                act(Wt[:, kq * 4:(kq + 1) * 4, :], Wst[p % 3][:], AF.Copy, ['Wst%d' % (p % 3)], ['%s_%d' % (wn, kq)])
                                issue_piece(p + 3)
                            for i in range(8):
                                px, pxn = psX[pi % 2], "psXp%d" % (pi % 2)
                                pi += 1
                                for k in range(16):
                                    mm(px[:, 0:256], xnTp[:, k, i:NPR:8], Wt[:, k, :], k == 0, k == 15,
                                       ['xnTp', '%s_%d' % (wn, k // 4)], [pxn])
                                cp('act', X2p[:, cb * 256:(cb + 1) * 256, i], px[:, 0:256], [pxn], ['X2p'])
                                yield

                ga = prefix_front()
                for ch in range(16):
                    p0_chunk(ch)
                    for _ in range(3 if ch < 3 else 5):
                        next(ga, None)
                    if ch >= 1:
                        p0_finish(ch - 1)
                p0_finish(15)
                for _ in ga:
                    pass
            c.barrier()

            def scan_chain(BuH, Pst, side_gen=None):
                W8v = W8[:].rearrange("p n (d c) -> p n d c", d=2)
                c.barrier()
                if side_gen is not None:
                    cp('dve', Pst[1][:, :, :, 2], BuH[:, :, :, 1], ['BuHin'], ['Pst1b'])
                for m in range(1, 129):
                    if side_gen is not None and m % 3 != 0:
                        next(side_gen, None)
                    Pt, ptn = Pst[m % 2], "Pst%d" % (m % 2)
                    Xp_ = BuH[:, :, :, m - 1]
                    Xc = BuH[:, :, :, m]
                    if side_gen is None:
                        cp('act', Pt[:, :, :, 2], Xc, ['BuHin'], [ptn + 'b'])
                    tt('dve', Pt[:, :, :, 0:2], Xp_.unsqueeze(2).to_broadcast([128, 64, 2, 2]), W8v, ALU.mult,
                       ['BuH', 'W8'], [ptn])
                    if side_gen is not None and m < 128:
                        Pn, pnn = Pst[(m + 1) % 2], "Pst%d" % ((m + 1) % 2)
                        cp('dve', Pn[:, :, :, 2], BuH[:, :, :, m + 1], ['BuHin'], [pnn + 'b'])
                    c.op('dve', lambda: V.tensor_reduce(out=Xc, in_=Pt[:], axis=mybir.AxisListType.X, op=ALU.add),
                         ['BuH', ptn, ptn + 'b'], ['BuH'])
                c.barrier()

            rmx = ExitStack()
            with ExitStack() as pc:
                BuHp = sbt(pc, "BuHp", [128, 64, 2, 129], F32)
                Pstp = [sbt(pc, "Pstp%d" % i, [128, 64, 2, 3], F32) for i in range(2)]
                with ExitStack() as pcu:
                    BPop = [sbt(pcu, "BPop%d" % i, [128, 8, 2, 128], BF16) for i in range(2)]
                    Uop = [sbt(pcu, "Uop%d" % i, [128, 8, 128], BF16) for i in range(2)]
                    psUp = [pst(pcu, "psUp%d" % i, [128, 1024], BF16) for i in range(2)]
                    psBp = [pst(pcu, "psBp%d" % i, [128, 512], F32) for i in range(4)]
                    memset('dve', BuHp[:, :, :, 0], 0.0, ['BuHp'])
                    pbi = 0
                    for o in range(16):
                        bpo, bpn = BPop[o % 2], "BPop%d" % (o % 2)
                        uo, uon = Uop[o % 2], "Uop%d" % (o % 2)
                        pu, pun = psUp[o % 2], "psUp%d" % (o % 2)
                        c.dma('sp', 'ld_' + bpn, bpo[:], s_bptz[:, o * 8:(o + 1) * 8, :, :], reads=['s_bptz'], writes=[bpn])
                        puv = pu[:].rearrange("p (g m) -> p g m", g=8)
                        for gl in range(8):
                            g = o * 8 + gl
                            tr(puv[:, gl, :], X2p[:, g * 16:(g + 1) * 16, :].rearrange("p c i -> p (c i)"), identb[:, :],
                               ['X2p', 'identb'], [pun])
                        cp('act' if o % 2 == 0 else 'dve', uo[:], puv, [pun], [uon])
                        for pl in range(4):
                            pb, pbn = psBp[pbi % 4], "psBp%d" % (pbi % 4)
                            pbi += 1
                            pbv = pb[:, 0:256].rearrange("p (r m) -> p r m", r=2)
                            for r in range(2):
                                for g2 in range(2):
                                    mm(pbv[:, r, :], bpo[:, pl * 2 + g2, r, :], uo[:, pl * 2 + g2, :], g2 == 0, g2 == 1, [bpn, uon], [pbn])
                            cp('act' if pl % 2 == 0 else 'dve', BuHp[:, o * 4 + pl, :, 1:129], pbv, [pbn], ['BuHp'])
                c.barrier()
                pfx.close()
                X2m = sbt_r(rmx, "X2m", [128, D, 8], BF16)
                X2sm = sbt_r(rmx, "X2sm", [128, 1024, 8], BF16)

                def main_front():
                    epsb2 = sbt(pc, "epsb2", [128, 1], F32)
                    xnTm = sbt(pc, "xnTm", [128, 16, NT], BF16)
                    with ExitStack() as pa:
                        gnbm = sbt(pa, "gnbm", [128, D], F32)
                        xl2 = [sbt(pa, "xlm%d" % i, [128, D], F32) for i in range(2)]
                        xs_ = [sbt(pa, "xsm%d" % i, [128, D], BF16) for i in range(2)]
                        xnb = [sbt(pa, "xnbm%d" % i, [128, D], BF16) for i in range(2)]
                        ss = sbt(pa, "ssm", [128, 16], F32)
                        rs = sbt(pa, "rsm", [128, 16], F32)
                        psA = [pst(pa, "psAm_%d" % i, [128, 1024], BF16) for i in range(2)]
                        c.dma('act', 'ld_gnbm', gnbm[:], gnb[:, :], writes=['gnbm'])
                        c.op('pool', lambda: P.memset(epsb2[:], EPS), (), ['epsb2'])
                        c.op('pool', lambda: P.memset(ss[:], 0.0), (), ['ssm'])

                        def tposes(t):
                            b = t % 2
                            xbn = "xnbm%d" % b
                            for hb in range(2):
                                pa_, pan = psA[hb], "psAm_%d" % hb
                                for k in range(8):
                                    kk = hb * 8 + k
                                    tr(pa_[:, k * 128:(k + 1) * 128], xnb[b][:, kk * 128:(kk + 1) * 128], identb[:, :],
                                       [xbn, 'identb'], [pan])
                                cp('act', xnTm[:, hb * 8:(hb + 1) * 8, t * 128:(t + 1) * 128],
                                   pa_[:].rearrange("p (k n) -> p k n", k=8), [pan], ['xnTm'])

                        c.dma('act', 'ld_xlm0', xl2[0][:], xm[0:128, :], writes=['xlm0'])
                        for t in range(9):
                            b = t % 2
                            xsn, xbn = "xsm%d" % b, "xnbm%d" % b
                            xl, xln = xl2[b], "xlm%d" % b
                            if t + 1 < 9:
                                nb_ = (t + 1) % 2
                                c.dma('act', 'ld_xlm%d' % nb_, xl2[nb_][:], xm[(t + 1) * 128:(t + 2) * 128, :], writes=['xlm%d' % nb_])
                            act(xs_[b][:], xl[:], AF.Square, [xln], [xsn, 'ssm'], accum_out=ss[:, t:t + 1])
                            act(rs[:, t:t + 1], ss[:, t:t + 1], AF.Ln, ['ssm', 'epsb2'], ['rsm'], scale=1.0 / D, bias=epsb2[:, 0:1])
                            act(rs[:, t:t + 1], rs[:, t:t + 1], AF.Exp, ['rsm'], ['rsm'], scale=-0.5)
                            act(xs_[b][:], xl[:], AF.Copy, [xln, 'rsm'], [xsn], scale=rs[:, t:t + 1])
                            tt('pool', xnb[b][:], xs_[b][:], gnbm[:], ALU.mult, [xsn, 'gnbm'], [xbn])
                            if t >= 1:
                                tposes(t - 1)
                            yield
                        tposes(8)
                        yield
                    c.fence('pool', include_self=True)
                    with ExitStack() as pb:
                        Wb = [sbt(pb, "Wxm%d" % i, [128, 16, 512], BF16) for i in range(2)]
                        XS = [sbt(pb, "XSm%d" % i, [128, 512], BF16) for i in range(2)]
                        XSr = sbt(pb, "XSrm", [128, 8, 512], BF16)
                        psX = [pst(pb, "psXm%d" % i, [128, 512], F32) for i in range(6)]
                        pi = 0
                        for cb in range(4):
                            Wt, wn = load_w(Wb, "Wxm", w_in[:, 6144 + cb * 512:6144 + (cb + 1) * 512], 512)
                            for i in range(9):
                                px, pxn = psX[pi % 6], "psXm%d" % (pi % 6)
                                pi += 1
                                for k in range(16):
                                    lt = xnTm[:, k, i:NPR:8] if i < 8 else xnTm[:, k, NPR:NT]
                                    mm(px[:], lt, Wt[:, k, :], k == 0, k == 15,
                                       ['xnTm', '%s_%s' % (wn, 'a' if k < 4 else 'b')], [pxn])
                                if i < 8:
                                    cp('act', X2m[:, cb * 512:(cb + 1) * 512, i], px[:], [pxn], ['X2m'])
                                else:
                                    xs, xsn = XS[cb % 2], "XSm%d" % (cb % 2)
                                    cp('act', xs[:], px[:], [pxn], [xsn])
                                    q = cb // 2
                                    c.dma('act', 'rgm', XSr[q * 64:q * 64 + 16, :, :], xs[:], reads=[xsn, 'XSrm'], writes=['XSrm'])
                                    cp('act', X2sm[q * 64:q * 64 + 16, (cb % 2) * 512:(cb % 2 + 1) * 512, :].rearrange("p n i -> p i n"),
                                       XSr[q * 64:q * 64 + 16, :, :], ['XSrm'], ['X2sm'])
                                yield

                gm = main_front()
                scan_chain(BuHp, Pstp, side_gen=gm)
                cp('dve', Hmid[:], BuHp[:, :, :, 128], ['BuHp'], ['Hmid'])
                for _ in gm:
                    pass
            c.barrier()
            if stop == 0:
                c.dead = True

            with ExitStack() as s5:
                Uall = sbt(s5, "Uall", [128, 128, 144], BF16)
                for is_main in (True,):
                    ntiles = 9 if is_main else 8
                    x_dram = xm if is_main else xp
                    ncol = 144 if is_main else 128
                    with ExitStack() as sx:
                        X2 = X2m
                        X2s = X2sm
                        with ExitStack() as su:
                            psU = [pst(su, "psU%d" % i, [128, 1024], BF16) for i in range(2)]
                            psUs = [pst(su, "psUs%d" % i, [128, 512], F32) for i in range(2)]
                            for o in range(16):
                                pu, pun = psU[o % 2], "psU%d" % (o % 2)
                                pus, pusn = psUs[o % 2], "psUs%d" % (o % 2)
                                puv = pu[:].rearrange("p (g m) -> p g m", g=8)
                                pusv = pus[:, 0:128].rearrange("p (g m) -> p g m", g=8)
                                for gl in range(8):
                                    g = o * 8 + gl
                                    tr(puv[:, gl, :], X2[:, g * 16:(g + 1) * 16, :].rearrange("p c i -> p (c i)"), identb[:, :],
                                       ['X2', 'identb'], [pun])
                                    if is_main:
                                        q = g // 64
                                        col = (g % 64) * 16
                                        mm(pusv[:, gl, :], X2s[q * 64:q * 64 + 16, col:col + 16, :].rearrange("p c i -> p (c i)"),
                                           identb[q * 64:q * 64 + 16, q * 64:q * 64 + 16], True, True, ['X2s', 'identb'], [pusn])
                                cp('act' if o % 2 == 0 else 'dve', Uall[:, o * 8:(o + 1) * 8, 0:128], puv, [pun], ['Uall'])
                                if is_main:
                                    cp('dve' if o % 2 == 0 else 'act', Uall[:, o * 8:(o + 1) * 8, 128:144], pusv, [pusn], ['Uall'])
                        c.barrier()
                    c.barrier()

                    rmx.close()
                    ybT = sbt_r(st, "ybT", [128, 16, NT], BF16)
                    with ExitStack() as sc:
                        BuH = sbt(sc, "BuH", [128, 64, 2, 145], F32)
                        Pst = [sbt(sc, "Pst%d" % i, [128, 64, 2, 3], F32) for i in range(2)]
                        BPo = [sbt(sc, "BPo%d" % i, [128, 8, 2, 128], BF16) for i in range(2)]
                        Hfin = sbt(sc, "Hfin", [128, 64, 2], F32)
                        if is_main:
                            HS0 = sbt(sc, "HS0", [128, 64, 2, 16], F32)
                            T1s = sbt(sc, "T1s", [128, 32, 2, 16], F32)
                            Pcs = sbt(sc, "Pcs", [128, 32, 2, 16], F32)
                            CQo = [sbt(sc, "CQo%d" % i, [128, 4, 2, 128], BF16) for i in range(2)]
                            MSo = [sbt(sc, "MSo%d" % i, [128, 8, 128], BF16) for i in range(2)]
                            Hbf = [sbt(sc, "Hbf%d" % i, [128, 4, 2, 144], BF16) for i in range(2)]
                            Y2o = [sbt(sc, "Y2o%d" % i, [128, 8, 128], BF16) for i in range(2)]
                            Y2so = [sbt(sc, "Y2so%d" % i, [16, 8, 128], BF16) for i in range(2)]
                            c.dma('sp', 'ld_h0', HS0[:], h0[:, :, :, :], writes=['HS0'])
                            cp('dve', BuH[:, :, :, 0], Hmid[:], ['Hmid'], ['BuH'])
                        else:
                            memset('dve', BuH[:, :, :, 0], 0.0, ['BuH'])
                        with ExitStack() as sbu:
                            psB = [pst(sbu, "psB%d" % i, [128, 512], F32) for i in range(4)]
                            pbi = 0
                            for o in range(16):
                                bpo = BPo[o % 2]
                                bpn = "BPo%d" % (o % 2)
                                c.dma('sp', 'ld_' + bpn, bpo[:], s_bptz[:, o * 8:(o + 1) * 8, :, :], reads=['s_bptz'], writes=[bpn])
                                for pl in range(4):
                                    pb = psB[pbi % 4]
                                    pbn = "psB%d" % (pbi % 4)
                                    pbi += 1
                                    pbv = pb[:, 0:288].rearrange("p (r m) -> p r m", r=2)
                                    for r in range(2):
                                        for g2 in range(2):
                                            g = o * 8 + pl * 2 + g2
                                            mm(pbv[:, r, 0:ncol], bpo[:, pl * 2 + g2, r, :], Uall[:, g, 0:ncol],
                                               g2 == 0, g2 == 1, [bpn, 'Uall'], [pbn])
                                    prl = o * 4 + pl
                                    cp('act' if pl % 2 == 0 else 'dve', BuH[:, prl, :, 1:1 + ncol], pbv[:, :, 0:ncol], [pbn], ['BuH'])
                        W8v = W8[:].rearrange("p n (d c) -> p n d c", d=2)
                        c.barrier()
                        for m in range(1, 129):
                            Pt, ptn = Pst[m % 2], "Pst%d" % (m % 2)
                            Xp_ = BuH[:, :, :, m - 1]
                            Xc = BuH[:, :, :, m]
                            cp('act', Pt[:, :, :, 2], Xc, ['BuHin'], [ptn + 'b'])
                            tt('dve', Pt[:, :, :, 0:2], Xp_.unsqueeze(2).to_broadcast([128, 64, 2, 2]), W8v, ALU.mult,
                               ['BuH', 'W8'], [ptn])
                            c.op('dve', lambda: V.tensor_reduce(out=Xc, in_=Pt[:], axis=mybir.AxisListType.X, op=ALU.add),
                                 ['BuH', ptn, ptn + 'b'], ['BuH'])
                        c.barrier()
                        if not is_main:
                            cp('dve', Hmid[:], BuH[:, :, :, 128], ['BuH'], ['Hmid'])
                        else:
                            cp('dve', Hfin[:], BuH[:, :, :, 128], ['BuH'], ['Hfin'])
                            c.dma('sp', 'st_hp', hp_out[:, :, :], Hfin[:], reads=['Hfin'])
                            for hf in range(2):
                                prs = slice(hf * 32, hf * 32 + 32)
                                for cc in range(2):
                                    tt('dve', Pcs[:], HS0[:, prs, cc, :].unsqueeze(2).to_broadcast([128, 32, 2, 16]),
                                       W8v[:, prs, :, cc].unsqueeze(3).to_broadcast([128, 32, 2, 16]), ALU.mult, ['HS0', 'W8'], ['Pcs'])
                                    if cc == 0:
                                        tt('dve', T1s[:], BuH[:, prs, :, 129:145], Pcs[:], ALU.add, ['BuH', 'Pcs'], ['T1s'])
                                    else:
                                        tt('dve', T1s[:], T1s[:], Pcs[:], ALU.add, ['T1s', 'Pcs'], ['T1s'])
                                c.dma('sp', 'st_hs', hs_out[:, prs, :, :], T1s[:], reads=['T1s'])
                            with ExitStack() as sy:
                                psY = [pst(sy, "psY%d" % i, [128, 512], F32) for i in range(2)]
                                psYs = [pst(sy, "psYs%d" % i, [128, 512], F32) for i in range(2)]
                                psT2 = [pst(sy, "psT2%d" % i, [128, 1024], BF16) for i in range(2)]
                                psT2s = [pst(sy, "psT2s%d" % i, [128, 512], F32) for i in range(2)]
                                def y_transposes(o):
                                    b2 = o % 2
                                    y2o, y2n = Y2o[b2], "Y2o%d" % b2
                                    y2so, y2sn = Y2so[b2], "Y2so%d" % b2
                                    p2, p2n = psT2[b2], "psT2%d" % b2
                                    p2s, p2sn = psT2s[b2], "psT2s%d" % b2
                                    p2v = p2[:].rearrange("p (j m) -> p j m", j=8)
                                    p2sv = p2s[:, 0:128].rearrange("p (j s) -> p j s", j=8)
                                    for j in range(8):
                                        tr(p2v[:, j, :], y2o[:, j, :], identb[:, :], [y2n, 'identb'], [p2n])
                                        mm(p2sv[:, j, :], y2so[0:16, j, :], identb[0:16, 0:16], True, True, [y2sn, 'identb'], [p2sn])
                                    ybv = ybT[:, o, 0:NPR].rearrange("p (m j) -> p j m", j=8)
                                    cp('dve', ybv[:, 0:4, :], p2v[:, 0:4, :], [p2n], ['ybT'])
                                    cp('act', ybv[:, 4:8, :], p2v[:, 4:8, :], [p2n], ['ybT'])
                                    cp('dve', ybT[:, o, NPR:NT].rearrange("p (s j) -> p j s", j=8), p2sv, [p2sn], ['ybT'])

                                qi = 0
                                for o in range(16):
                                    b2 = o % 2
                                    cqo, cqn = CQo[b2], "CQo%d" % b2
                                    mso, msn = MSo[b2], "MSo%d" % b2
                                    hbf, hbn = Hbf[b2], "Hbf%d" % b2
                                    y2o, y2n = Y2o[b2], "Y2o%d" % b2
                                    y2so, y2sn = Y2so[b2], "Y2so%d" % b2
                                    c.dma('sp', 'ld_' + cqn, cqo[:], s_cq1[:, o * 4:(o + 1) * 4, :, :], reads=['s_cq1'], writes=[cqn])
                                    c.dma('sp', 'ld_' + msn, mso[:], s_msup[:, o * 8:(o + 1) * 8, :], reads=['s_msup'], writes=[msn])
                                    cp('pool', hbf[:, :, :, 0:128], BuH[:, o * 4:(o + 1) * 4, :, 0:128], ['BuH'], [hbn])
                                    cp('pool', hbf[:, :, :, 128:144], HS0[:, o * 4:(o + 1) * 4, :, :], ['HS0'], [hbn])
                                    for quad in range(2):
                                        py, pyn = psY[qi % 2], "psY%d" % (qi % 2)
                                        pys, pysn = psYs[qi % 2], "psYs%d" % (qi % 2)
                                        qi += 1
                                        pyv = py[:].rearrange("p (g n) -> p g n", g=4)
                                        pysv = pys[:].rearrange("p (g n) -> p g n", g=4)
                                        for gl4 in range(4):
                                            gl = quad * 4 + gl4
                                            g = o * 8 + gl
                                            pl, g2 = gl // 2, gl % 2
                                            hs = slice(g2 * 64, g2 * 64 + 64)
                                            mm(pyv[:, gl4, :], hbf[hs, pl, 0, 0:128], cqo[hs, pl, 0, :], True, False, [hbn, cqn], [pyn])
                                            mm(pyv[:, gl4, :], hbf[hs, pl, 1, 0:128], cqo[hs, pl, 1, :], False, False, [hbn, cqn], [pyn])
                                            mm(pyv[:, gl4, :], Uall[:, g, 0:128], mso[:, gl, :], False, True, ['Uall', msn], [pyn])
                                            mm(pysv[0:16, gl4, :], hbf[hs, pl, 0, 128:144], cqo[hs, pl, 0, :], True, False, [hbn, cqn], [pysn])
                                            mm(pysv[0:16, gl4, :], hbf[hs, pl, 1, 128:144], cqo[hs, pl, 1, :], False, False, [hbn, cqn], [pysn])
                                            mm(pysv[0:16, gl4, :], Uall[:, g, 128:144], mso[:, gl, :], False, True, ['Uall', msn], [pysn])
                                        act(y2o[:, :, quad * 64:(quad + 1) * 64].rearrange("p j (g c) -> p g j c", g=4),
                                            pyv.rearrange("p g (j c) -> p g j c", j=8), AF.Gelu_apprx_tanh, [pyn], [y2n])
                                        act(y2so[:, :, quad * 64:(quad + 1) * 64].rearrange("p j (g c) -> p g j c", g=4),
                                            pysv[0:16].rearrange("p g (j c) -> p g j c", j=8), AF.Gelu_apprx_tanh, [pysn], [y2sn])
                                    if o >= 1:
                                        y_transposes(o - 1)
                                y_transposes(15)
                    c.barrier()
                    if (stop == 2 and not is_main) or (stop == 4 and is_main):
                        c.dead = True
            c.barrier()

            outbT = sbt(st, "outbT", [128, 16, NT], BF16)
            with ExitStack() as m_:
                xnT = sbt(m_, "xnT", [128, 16, NT], BF16)
                tblocks = [(0, 512), (512, 512), (1024, 128)]
                with ExitStack() as g_:
                    gpa = phaseA_gen(g_, xm, 9, xnT)
                    Wb = [sbt(g_, "Wg%d" % i, [128, 16, 512], BF16) for i in range(2)]
                    bglu_t = sbt(g_, "bglu_t", [128, 16], F32)
                    sg = [sbt(g_, "sg%d" % i, [128, 512], BF16) for i in range(2)]
                    psG = [pst(g_, "psG%d" % i, [128, 512], F32) for i in range(4)]
                    c.dma('sp', 'ld_bglu', bglu_t[:], bglu[:, :], writes=['bglu'])
                    pi = 0
                    for ob in range(4):
                        Wt, wn = load_w(Wb, "Wg", w_glu[:, ob * 512:(ob + 1) * 512], 512)
                        for mo in range(4):
                            oc = ob * 4 + mo
                            for (t0, tn) in tblocks:
                                pg, pgn = psG[pi % 4], "psG%d" % (pi % 4)
                                sgt, sgn = sg[pi % 2], "sg%d" % (pi % 2)
                                pi += 1
                                for k in range(16):
                                    mm(pg[:, 0:tn], Wt[:, k, mo * 128:(mo + 1) * 128], ybT[:, k, t0:t0 + tn], k == 0, k == 15,
                                       ['%s_%s' % (wn, 'a' if k < 4 else 'b'), 'ybT'], [pgn])
                                act(sgt[:, 0:tn], pg[:, 0:tn], AF.Sigmoid, [pgn, 'bglu'], [sgn], bias=bglu_t[:, oc:oc + 1])
                                tt('dve', outbT[:, oc, t0:t0 + tn], sgt[:, 0:tn], ybT[:, oc, t0:t0 + tn], ALU.mult,
                                   [sgn, 'ybT'], ['outb'])
                                if pi % 4 == 0:
                                    next(gpa, None)
                    for _ in gpa:
                        pass
                    for ob in range(4):
                        Wt, wn = load_w(Wb, "Wg", w_in[:, 8192 + ob * 512:8192 + (ob + 1) * 512], 512)
                        for mo in range(4):
                            oc = ob * 4 + mo
                            for (t0, tn) in tblocks:
                                pg, pgn = psG[pi % 4], "psG%d" % (pi % 4)
                                sgt, sgn = sg[pi % 2], "sg%d" % (pi % 2)
                                pi += 1
                                for k in range(16):
                                    mm(pg[:, 0:tn], Wt[:, k, mo * 128:(mo + 1) * 128], xnT[:, k, t0:t0 + tn], k == 0, k == 15,
                                       ['%s_%s' % (wn, 'a' if k < 4 else 'b'), 'xnT'], [pgn])
                                act(sgt[:, 0:tn], pg[:, 0:tn], AF.Silu, [pgn], [sgn])
                                tt('dve', outbT[:, oc, t0:t0 + tn], sgt[:, 0:tn], outbT[:, oc, t0:t0 + tn], ALU.mult,
                                   [sgn, 'outb'], ['outb'])
                c.barrier()
                if stop == 5:
                    c.dead = True

                with ExitStack() as a_:
                    gvb_t = sbt(a_, "gvb_t", [128, D], F32)
                    wsTm = sbt(a_, "wsTm", [128, 8, 128], BF16)
                    wsSm = sbt(a_, "wsSm", [128, 8, 128], BF16)
                    bs_b = sbt(a_, "bs_b", [1, 8, 128], BF16)
                    bsS_b = sbt(a_, "bsS_b", [1, 8, 128], BF16)
                    c.dma('sp', 'ld_gvb', gvb_t[:], gvb[:, :], writes=['gvb'])
                    with ExitStack() as tmp_:
                        wtmp = sbt(tmp_, "wtmp", [128, 8, 128], F32)
                        mtmp = sbt(tmp_, "mtmp", [128, 128], F32)
                        bs_t = sbt(tmp_, "bs_t", [1, 8, 128], F32)
                        bsS_t = sbt(tmp_, "bsS_t", [1, 8, 128], F32)
                        c.dma('sp', 'ld_wsT', wtmp[:], wsT[:, :, :], writes=['wtmp'])
                        c.dma('sp', 'ld_msk', mtmp[:], mask_ts[:, :], writes=['mtmp'])
                        tt('dve', wsTm[:], wtmp[:], mtmp[:].unsqueeze(1).to_broadcast([128, 8, 128]), ALU.mult, ['wtmp', 'mtmp'], ['wsTm'])
                        c.dma('sp', 'ld_wsT', wtmp[:], wsS[:, :, :], reads=['wtmp'], writes=['wtmp'])
                        c.dma('sp', 'ld_msk', mtmp[:], mask_blk[:, :], reads=['mtmp'], writes=['mtmp'])
                        tt('dve', wsSm[:], wtmp[:], mtmp[:].unsqueeze(1).to_broadcast([128, 8, 128]), ALU.mult, ['wtmp', 'mtmp'], ['wsSm'])
                        c.dma('sp', 'ld_bs', bs_t[:], bsrow[:, :, :], writes=['bs_t'])
                        c.dma('sp', 'ld_bsS', bsS_t[:], bsSrow[:, :, :], writes=['bsS_t'])
                        cp('dve', bs_b[:], bs_t[:], ['bs_t'], ['bs_b'])
                        cp('dve', bsS_b[:], bsS_t[:], ['bsS_t'], ['bsS_b'])
                    c.barrier()
                    Wh = [sbt(a_, "Wh%d" % i, [128, 16, 768], BF16) for i in range(2)]
                    uT = sbt(a_, "uT", [128, 2, NT], BF16)
                    gaT = sbt(a_, "gaT", [128, 2, NT], BF16)
                    vg9 = sbt(a_, "vg9", [128, 9, 256], F32)
                    junkv = sbt(a_, "junkv", [128, 256], BF16)
                    vnb = sbt(a_, "vnb", [128, 9, 256], BF16)
                    vnsh = [sbt(a_, "vnsh%d" % i, [128, 256], F32) for i in range(2)]
                    ssv = sbt(a_, "ssv", [128, 80], F32)
                    rsv = sbt(a_, "rsv", [128, 80], F32)
                    psA_ = [pst(a_, "psAm%d" % i, [128, 512], F32) for i in range(4)]
                    psV = [pst(a_, "psV%d" % i, [128, 512], F32) for i in range(2)]
                    psM_ = [pst(a_, "psMx%d" % i, [128, 512], F32) for i in range(2)]
                    memset('dve', ssv[:], 0.0, ['ssv'])
                    pi = 0
                    for h in range(8):
                        i = wslot[0] % 2
                        wslot[0] += 1
                        Wt, wn = Wh[i], "Wh%d" % i
                        for j, c0 in ((1, 2048 + h * 256), (0, h * 256), (2, 4096 + h * 256)):
                            srcv = w_in[:, c0:c0 + 256].rearrange("(k p) n -> p k n", p=128)
                            for kq in range(4):
                                grp = ('a' if kq == 0 else 'b') if j == 1 else ('c' if j == 0 else 'd')
                                c.dma('pool', 'w_%s_%s' % (wn, grp), Wt[:, kq * 4:(kq + 1) * 4, j * 256:(j + 1) * 256],
                                      srcv[:, kq * 4:(kq + 1) * 4, :], writes=['%s_%s' % (wn, grp)],
                                      nowait=((j == 1 and kq > 1) or (j != 1 and kq > 0)))
                        for t in range(9):
                            pv, pvn = psV[t % 2], "psV%d" % (t % 2)
                            for k in range(16):
                                mm(pv[:, 0:256], xnT[:, k, t * 128:(t + 1) * 128], Wt[:, k, 256:512], k == 0, k == 15, ['xnT', '%s_%s' % (wn, 'a' if k < 4 else 'b')], [pvn])
                            act(vg9[:, t, :], pv[:, 0:256], AF.Gelu_apprx_tanh, [pvn], ['vg9'])
                        for t in range(9):
                            col = h * 9 + t
                            act(junkv[:], vg9[:, t, :], AF.Square, ['vg9'], ['junkv', 'ssv'], accum_out=ssv[:, col:col + 1])
                        cs = slice(h * 9, h * 9 + 9)
                        ts('dve', rsv[:, cs], ssv[:, cs], 1.0 / 256, EPS, ALU.mult, ALU.add, ['ssv'], ['rsv'])
                        act(rsv[:, cs], rsv[:, cs], AF.Sqrt, ['rsv'], ['rsv'])
                        c.op('dve', lambda: V.reciprocal(out=rsv[:, cs], in_=rsv[:, cs]), ['rsv'], ['rsv'])
                        for t in range(9):
                            col = h * 9 + t
                            stt(vnb[:, t, :], vg9[:, t, :], rsv[:, col:col + 1], gvb_t[:, h * 256:(h + 1) * 256], ALU.mult, ALU.mult,
                                ['vg9', 'rsv', 'gvb'], ['vnb'])
                            if t == 8:
                                vsn = "vnsh%d" % (h % 2)
                                stt(vnsh[h % 2][:], vg9[:, t, :], rsv[:, col:col + 1], gvb_t[:, h * 256:(h + 1) * 256],
                                    ALU.mult, ALU.mult, ['vg9', 'rsv', 'gvb'], [vsn])
                                c.dma('sp', 'st_' + vsn, vns_out[:, h * 256:(h + 1) * 256], vnsh[h % 2][:], reads=[vsn])
                        for (j, dst, dn, fn) in ((0, uT, 'uT', AF.Gelu_apprx_tanh), (2, gaT, 'gaT', AF.Silu)):
                            for mo in range(2):
                                for (t0, tn) in tblocks:
                                    pg, pgn = psA_[pi % 4], "psAm%d" % (pi % 4)
                                    pi += 1
                                    for k in range(16):
                                        mm(pg[:, 0:tn], Wt[:, k, j * 256 + mo * 128:j * 256 + (mo + 1) * 128], xnT[:, k, t0:t0 + tn],
                                           k == 0, k == 15, ['%s_%s' % (wn, 'c' if j == 0 else 'd'), 'xnT'], [pgn])
                                    act(dst[:, mo, t0:t0 + tn], pg[:, 0:tn], fn, [pgn], [dn])
                                    if j == 2:
                                        tt('pool', uT[:, mo, t0:t0 + tn], uT[:, mo, t0:t0 + tn], gaT[:, mo, t0:t0 + tn], ALU.mult,
                                           ['uT', 'gaT'], ['uT'])
                        for mo in range(2):
                            for tb in range(3):
                                tiles = [0, 1, 2, 3] if tb == 0 else ([4, 5, 6, 7] if tb == 1 else [8])
                                pm, pmn = psM_[(mo * 3 + tb) % 2], "psMx%d" % ((mo * 3 + tb) % 2)
                                for ti, t in enumerate(tiles):
                                    wsm = wsTm if t < 8 else wsSm
                                    bsb = bs_b if t < 8 else bsS_b
                                    o_ = pm[:, ti * 128:(ti + 1) * 128]
                                    mm(o_, vnb[:, t, mo * 128:(mo + 1) * 128], wsm[:, h, :], True, False, ['vnb', 'wsTm', 'wsSm'], [pmn])
                                    mm(o_, ones1[0:1, :], bsb[0:1, h, :], False, True, ['ones1', 'bs_b', 'bsS_b'], [pmn])
                                t0 = tiles[0] * 128
                                tn = len(tiles) * 128
                                tt('dve', ybT[:, h * 2 + mo, t0:t0 + tn], pm[:, 0:tn], uT[:, mo, t0:t0 + tn], ALU.mult,
                                   [pmn, 'uT'], ['outa'])
                c.barrier()
            c.barrier()

            if stop == 6:
                c.dead = True
            with ExitStack() as o_s:
                Wo = [sbt(o_s, "Wo%d" % i, [128, 32, 512], BF16) for i in range(2)]
                gfb_t = sbt(o_s, "gfb_t", [128, D], F32)
                xnew = sbt(o_s, "xnew", [128, 5, D], F32)
                junko = sbt(o_s, "junko", [128, D], BF16)
                sso = sbt(o_s, "sso", [128, 16], F32)
                rso = sbt(o_s, "rso", [128, 16], F32)
                psO = [pst(o_s, "psO%d" % i, [128, 512], F32) for i in range(4)]
                c.dma('sp', 'ld_gfb', gfb_t[:], gfb[:, :], writes=['gfb'])
                memset('dve', sso[:], 0.0, ['sso'])
                pi = 0
                for (tiles) in ([0, 1, 2, 3, 4], [5, 6, 7, 8]):
                    for tl, t in enumerate(tiles):
                        c.dma('sp', 'ld_xr%d' % tl, xnew[:, tl, :], xm[t * 128:(t + 1) * 128, :], reads=['xnew%d' % tl], writes=['xnew%d' % tl])
                    for cb in range(4):
                        i = wslot[0] % 2
                        wslot[0] += 1
                        Wt, wn = Wo[i], "Wo%d" % i
                        srcv = w_out[:, cb * 512:(cb + 1) * 512].rearrange("(k p) n -> p k n", p=128)
                        for kq in range(8):
                            grp = 'a' if kq == 0 else 'b'
                            c.dma('pool', 'w_%s_%s' % (wn, grp), Wt[:, kq * 4:(kq + 1) * 4, :], srcv[:, kq * 4:(kq + 1) * 4, :],
                                  writes=['%s_%s' % (wn, grp)], nowait=(kq > 1))
                        for tl, t in enumerate(tiles):
                            po, pon = psO[pi % 4], "psO%d" % (pi % 4)
                            pi += 1
                            for k in range(32):
                                mm(po[:, :], (ybT if k < 16 else outbT)[:, k % 16, t * 128:(t + 1) * 128], Wt[:, k, :], k == 0, k == 31, ['mixT', '%s_%s' % (wn, 'a' if k < 4 else 'b')], [pon])
                            tt('dve', xnew[:, tl, cb * 512:(cb + 1) * 512], po[:, :], xnew[:, tl, cb * 512:(cb + 1) * 512], ALU.add,
                               [pon, 'xnew%d' % tl], ['xnew%d' % tl])
                    for tl, t in enumerate(tiles):
                        xn_ = 'xnew%d' % tl
                        act(junko[:], xnew[:, tl, :], AF.Square, [xn_], ['junko', 'sso'], accum_out=sso[:, t:t + 1])
                        ts('dve', rso[:, t:t + 1], sso[:, t:t + 1], 1.0 / D, EPS, ALU.mult, ALU.add, ['sso'], ['rso'])
                        act(rso[:, t:t + 1], rso[:, t:t + 1], AF.Sqrt, ['rso'], ['rso'])
                        c.op('dve', lambda: V.reciprocal(out=rso[:, t:t + 1], in_=rso[:, t:t + 1]), ['rso'], ['rso'])
                        stt(xnew[:, tl, :], xnew[:, tl, :], rso[:, t:t + 1], gfb_t[:], ALU.mult, ALU.mult, [xn_, 'rso', 'gfb'], [xn_])
                        c.dma('sp', 'st_y%d' % tl, y_out[t * 128:(t + 1) * 128, :], xnew[:, tl, :], reads=[xn_], writes=[])
        except _StopBuild:
            pass
        for k, s in c.sems.items():
            if k in ('pe', 'act', 'dve', 'pool'):
                continue
            if c.cnt[k] > 0:
                nc.sync.wait_ge(s, c.cnt[k])
    return nc


def _ls(a):
    return np.ascontiguousarray(a.reshape(64, 2, 64).transpose(1, 2, 0).reshape(128, 64))


def make_in_maps(x_prompt, x_sample, state_ssm_re, state_ssm_im, g_norm, w_in, g_v, w_s, b_s,
                 a_re, a_im, log_dt, b_re, b_im, c_re, c_im, d_skip, w_glu, b_glu, w_out, g_final):
    f = np.float32
    x_prompt = np.asarray(x_prompt, f)
    x_sample = np.asarray(x_sample, f)
    shared = {}
    shared["w_in"] = np.ascontiguousarray(np.asarray(w_in, f)[0])
    shared["w_glu"] = np.ascontiguousarray(np.asarray(w_glu, f)[0])
    shared["w_out"] = np.ascontiguousarray(np.asarray(w_out, f)[0])
    shared["gnb"] = np.ascontiguousarray(np.broadcast_to(np.asarray(g_norm, f)[0][None, :], (128, D)))
    shared["gncol"] = np.ascontiguousarray(np.asarray(g_norm, f)[0].reshape(16, 128).T)
    shared["gvb"] = np.ascontiguousarray(np.broadcast_to(np.asarray(g_v, f)[0][None, :], (128, D)))
    shared["gfb"] = np.ascontiguousarray(np.broadcast_to(np.asarray(g_final, f)[None, :], (128, D)))
    shared["bglu"] = np.ascontiguousarray(np.asarray(b_glu, f)[0].reshape(16, 128).T)
    ws = np.asarray(w_s, f)[0]
    shared["wsT"] = np.ascontiguousarray(ws.transpose(2, 0, 1))
    wss = np.zeros((16, 8, 8, 16, 8), f)
    w8 = ws[:, :8, :8]
    for s in range(16):
        wss[s, :, :, s, :] = w8.transpose(2, 0, 1)
    shared["wsS"] = np.ascontiguousarray(wss.reshape(128, 8, 128))
    shared["mask_ts"] = np.triu(np.ones((128, 128), f))
    mb = np.zeros((16, 8, 16, 8), f)
    for s in range(16):
        mb[s, :, s, :] = np.triu(np.ones((8, 8), f))
    shared["mask_blk"] = mb.reshape(128, 128)
    m16 = np.zeros((16, 8, 8, 16), f)
    pI = np.zeros((16, 8, 8, 16), f)
    for i in range(8):
        m16[:, i, i:, :] = 1.0
        for cc in range(16):
            pI[cc, i, i, cc] = 1.0
    shared["mask16"] = np.ascontiguousarray(m16.reshape(128, 128))
    shared["permI"] = np.ascontiguousarray(pI.reshape(128, 128))
    bs = np.asarray(b_s, f)[0]
    shared["bsrow"] = np.ascontiguousarray(bs[None, :, :])
    shared["bsSrow"] = np.ascontiguousarray(np.tile(bs[:, :8], (1, 16))[None, :, :])
    shared["are"] = _ls(np.asarray(a_re, f)[0])
    shared["aim"] = _ls(np.asarray(a_im, f)[0])
    shared["ldt"] = _ls(np.broadcast_to(np.asarray(log_dt, f)[0][:, None], (128, 64)))

    def _lb(a):
        return np.ascontiguousarray(a.reshape(64, 2, 64, 16).transpose(1, 2, 0, 3).reshape(128, 64, 16))
    shared["bre"] = _lb(np.asarray(b_re, f)[0])
    shared["bim"] = _lb(np.asarray(b_im, f)[0])
    shared["cre"] = _lb(np.asarray(c_re, f)[0].transpose(0, 2, 1))
    shared["cim"] = _lb(np.asarray(c_im, f)[0].transpose(0, 2, 1))
    shared["dcol"] = np.ascontiguousarray(np.repeat(np.asarray(d_skip, f)[0].reshape(128, 16).T, 8, axis=0))
    qv = np.zeros((128, 24, 64), f)
    for i in range(8):
        qv[:, i, :] = 7 - i
    for s in range(16):
        qv[:, 8 + s, :] = s - 7
    shared["qv"] = qv
    shared["identf_in"] = np.eye(128, dtype=f)

    sre = np.asarray(state_ssm_re, f)[0]
    sim = np.asarray(state_ssm_im, f)[0]
    in_maps = []
    for cid in range(NCORES):
        sq, half = cid // 2, cid % 2
        m = dict(shared)
        xs = x_sample[cid * 16:(cid + 1) * 16].reshape(128, D)
        m["xm"] = np.ascontiguousarray(np.concatenate([x_prompt[sq, half * NPR:(half + 1) * NPR], xs], axis=0))
        m["xp"] = np.ascontiguousarray(x_prompt[sq, 0:NPR]) if half == 1 else np.zeros((NPR, D), f)
        hh = np.stack([sre[cid * 16:(cid + 1) * 16], sim[cid * 16:(cid + 1) * 16]], axis=0)
        hh = hh.reshape(2, 16, 64, 2, 64).transpose(3, 4, 2, 0, 1)
        m["h0"] = np.ascontiguousarray(hh.reshape(128, 64, 2, 16))
        in_maps.append(m)
    return in_maps


def assemble(R):
    f = np.float32
    y_prompt = np.zeros((4, 2048, D), f)
    y_sample = np.zeros((128, 8, D), f)
    re_p = np.zeros((1, 4, 128, 64), f)
    im_p = np.zeros((1, 4, 128, 64), f)
    re_s = np.zeros((1, 128, 128, 64), f)
    im_s = np.zeros((1, 128, 128, 64), f)
    v_s = np.zeros((1, 128, 8, D), f)
    for cid in range(NCORES):
        sq, half = cid // 2, cid % 2
        y = R[cid]["y"]
        y_prompt[sq, half * NPR:(half + 1) * NPR] = y[:NPR]
        y_sample[cid * 16:(cid + 1) * 16] = y[NPR:].reshape(16, 8, D)
        if half == 1:
            hp = R[cid]["hp"].reshape(2, 64, 64, 2).transpose(2, 0, 1, 3).reshape(128, 64, 2)
            re_p[0, sq] = hp[..., 0]
            im_p[0, sq] = hp[..., 1]
        hs = R[cid]["hs"].reshape(2, 64, 64, 2, 16).transpose(4, 3, 2, 0, 1).reshape(16, 2, 128, 64)
        re_s[0, cid * 16:(cid + 1) * 16] = hs[:, 0]
        im_s[0, cid * 16:(cid + 1) * 16] = hs[:, 1]
        v_s[0, cid * 16:(cid + 1) * 16] = R[cid]["vns"].reshape(16, 8, D)
    return (y_prompt, y_sample, re_p, im_p, re_s, im_s, v_s)


def kernel(**inputs):
    in_maps = make_in_maps(**inputs)
    nc = build_nc()
    res = run_bass_kernel_spmd(nc, in_maps, core_ids=list(range(NCORES)))
    return assemble(res.results)
```

```python
import math
from contextlib import ExitStack
import numpy as np
import concourse.bass as bass
import concourse.mybir as mybir
from concourse.bass_utils import run_bass_kernel_spmd

F32 = mybir.dt.float32
BF16 = mybir.dt.bfloat16
I32 = mybir.dt.int32
ALU = mybir.AluOpType
AF = mybir.ActivationFunctionType

NCORES = 8
D = 2048
NT = 1152
NPR = 1024
EPS = 1e-6
TWO_PI = 2.0 * math.pi


class Ctx:
    def __init__(self, nc, stack):
        self.nc = nc
        self.stack = stack
        self.eng = {'pe': nc.tensor, 'act': nc.scalar, 'dve': nc.vector, 'pool': nc.gpsimd, 'sp': nc.sync}
        self.sems = {}
        self.cnt = {}
        self.seen = {e: {} for e in self.eng}
        self.last_w = {}
        self.readers = {}
        self.dead = False
        self.nops = 0
        self.max_ops = None
        for e in ['pe', 'act', 'dve', 'pool']:
            self.sems[e] = stack.enter_context(nc.semaphore("prog_" + e))
            self.cnt[e] = 0

    def chan(self, name):
        if name not in self.sems:
            self.sems[name] = self.stack.enter_context(self.nc.semaphore("ch_" + name))
            self.cnt[name] = 0
        return name

    def _deps(self, reads, writes):
        deps = []
        for r in reads:
            if r in self.last_w:
                deps.append(self.last_w[r])
        for w in writes:
            if w in self.last_w:
                deps.append(self.last_w[w])
            deps.extend(self.readers.get(w, []))
        return deps

    def _wait(self, e, deps):
        best = {}
        for (k, v) in deps:
            if e == 'pe' and k == 'pe':
                continue
            if v > best.get(k, 0):
                best[k] = v
        for k, v in best.items():
            if self.seen[e].get(k, 0) >= v:
                continue
            self.eng[e].wait_ge(self.sems[k], v)
            self.seen[e][k] = v

    def _record(self, key, val, reads, writes):
        for w in writes:
            self.last_w[w] = (key, val)
            self.readers[w] = []
        for r in reads:
            self.readers.setdefault(r, []).append((key, val))

    def _tick(self):
        self.nops += 1
        if self.max_ops is not None and self.nops > self.max_ops:
            self.dead = True
        return self.dead

    def op(self, e, fn, reads=(), writes=()):
        if self._tick():
            return None
        self._wait(e, self._deps(reads, writes))
        ins = fn()
        self.cnt[e] += 1
        ins.then_inc(self.sems[e], 1)
        self._record(e, self.cnt[e], reads, writes)
        return ins

    def dma(self, q, ch, out, in_, reads=(), writes=(), nowait=False):
        if self._tick():
            return None
        ch = self.chan(ch)
        if not nowait:
            self._wait(q, self._deps(reads, writes))
        ins = self.eng[q].dma_start(out=out, in_=in_)
        self.cnt[ch] += 16
        ins.then_inc(self.sems[ch], 16)
        self._record(ch, self.cnt[ch], reads, writes)
        return ins

    def fence(self, e, include_self=False):
        if self.dead:
            return
        for k, sm in self.sems.items():
            v = self.cnt[k]
            if v > 0 and self.seen[e].get(k, 0) < v and (include_self or k != e):
                self.eng[e].wait_ge(sm, v)
                self.seen[e][k] = v

    def barrier(self):
        if self.dead:
            return
        for e in self.eng:
            for k, s in self.sems.items():
                v = self.cnt[k]
                if v > 0 and self.seen[e].get(k, 0) < v:
                    self.eng[e].wait_ge(s, v)
                    self.seen[e][k] = v
        self.last_w = {}
        self.readers = {}


class _StopBuild(Exception):
    pass


def build_nc(stop=99, max_ops=None):
    nc = bass.Bass("TRN2", target_bir_lowering=False)

    def din(name, shape, dt=F32):
        return nc.dram_tensor(name, shape, dt, kind="ExternalInput").ap()

    def dout(name, shape, dt=F32):
        return nc.dram_tensor(name, shape, dt, kind="ExternalOutput").ap()

    xm = din("xm", [NT, D])
    xp = din("xp", [NPR, D])
    h0 = din("h0", [128, 64, 2, 16])
    w_in = din("w_in", [D, 10240])
    w_glu = din("w_glu", [D, D])
    w_out = din("w_out", [2 * D, D])
    gnb = din("gnb", [128, D])
    gncol = din("gncol", [128, 16])
    gvb = din("gvb", [128, D])
    gfb = din("gfb", [128, D])
    bglu = din("bglu", [128, 16])
    wsT = din("wsT", [128, 8, 128])
    wsS = din("wsS", [128, 8, 128])
    mask_ts = din("mask_ts", [128, 128])
    mask_blk = din("mask_blk", [128, 128])
    mask16 = din("mask16", [128, 128])
    bsrow = din("bsrow", [1, 8, 128])
    bsSrow = din("bsSrow", [1, 8, 128])
    are = din("are", [128, 64])
    aim = din("aim", [128, 64])
    ldt = din("ldt", [128, 64])
    bre = din("bre", [128, 64, 16])
    bim = din("bim", [128, 64, 16])
    cre = din("cre", [128, 64, 16])
    cim = din("cim", [128, 64, 16])
    dcol = din("dcol", [128, 128])
    qv = din("qv", [128, 24, 64])
    identf_in = din("identf_in", [128, 128])
    permI = din("permI", [128, 128])

    y_out = dout("y", [NT, D])
    hp_out = dout("hp", [128, 64, 2])
    hs_out = dout("hs", [128, 64, 2, 16])
    vns_out = dout("vns", [128, D])

    s_bptz = nc.dram_tensor("s_bptz", [128, 128, 2, 128], BF16).ap()
    s_cq1 = nc.dram_tensor("s_cq1", [128, 64, 2, 128], BF16).ap()
    s_msup = nc.dram_tensor("s_msup", [128, 128, 128], BF16).ap()

    with ExitStack() as st:
        c = Ctx(nc, st)
        c.max_ops = max_ops
        V, A, P, T = nc.vector, nc.scalar, nc.gpsimd, nc.tensor
        try:

            uid = [0]

            def sbt(stack, name, shape, dt):
                uid[0] += 1
                return stack.enter_context(nc.sbuf_tensor("%s_%d" % (name, uid[0]), shape, dt))

            def pst(stack, name, shape, dt):
                uid[0] += 1
                return stack.enter_context(nc.psum_tensor("%s_%d" % (name, uid[0]), shape, dt))

            def tt(e, out, in0, in1, op, R, W):
                eng = {'dve': V, 'pool': P}[e]
                return c.op(e, lambda: eng.tensor_tensor(out=out, in0=in0, in1=in1, op=op), R, W)

            def ts(e, out, in0, s1, s2, op0, op1, R, W):
                eng = {'dve': V, 'pool': P}[e]
                if op1 is None:
                    return c.op(e, lambda: eng.tensor_scalar(out=out, in0=in0, scalar1=s1, scalar2=None, op0=op0), R, W)
                return c.op(e, lambda: eng.tensor_scalar(out=out, in0=in0, scalar1=s1, scalar2=s2, op0=op0, op1=op1), R, W)

            def stt(out, in0, scalar, in1, op0, op1, R, W):
                return c.op('dve', lambda: V.scalar_tensor_tensor(out=out, in0=in0, scalar=scalar, in1=in1, op0=op0, op1=op1), R, W)

            def act(out, in_, func, R, W, **kw):
                return c.op('act', lambda: A.activation(out=out, in_=in_, func=func, **kw), R, W)

            def cp(e, out, in_, R, W):
                if e == 'act':
                    return c.op('act', lambda: A.copy(out=out, in_=in_), R, W)
                eng = {'dve': V, 'pool': P}[e]
                return c.op(e, lambda: eng.tensor_copy(out=out, in_=in_), R, W)

            def mm(out, lhsT, rhs, start, stop, R, W):
                return c.op('pe', lambda: T.matmul(out, lhsT=lhsT, rhs=rhs, start=start, stop=stop), R, W)

            def tr(out, in_, ident, R, W):
                return c.op('pe', lambda: T.transpose(out=out, in_=in_, identity=ident), R, W)

            def memset(e, ap, val, W):
                eng = {'dve': V, 'pool': P}[e]
                return c.op(e, lambda: eng.memset(ap, val), (), W)

            identf = sbt(st, "identf", [128, 128], F32)
            identb = sbt(st, "identb", [128, 128], BF16)
            A8 = sbt(st, "A8", [128, 2, 64], F32)
            Hmid = sbt(st, "Hmid", [128, 64, 2], F32)
            ones1 = sbt(st, "ones1", [1, 128], BF16)
            W8 = sbt(st, "W8", [128, 64, 4], F32)

            def sbt_r(stack, name, shape, dt):
                uid[0] += 1
                return stack.enter_context(nc.sbuf_tensor("%s_%d" % (name, uid[0]), shape, dt, side="right"))

            c.dma('sp', 'ldc', identf[:], identf_in[:, :], writes=['identf'])
            cp('dve', identb[:], identf[:], ['identf'], ['identb'])
            memset('dve', ones1[:], 1.0, ['ones1'])

            wslot = [0]

            def load_w(Wbufs, wname, src_ap, ncols_total_view):
                i = wslot[0] % len(Wbufs)
                wslot[0] += 1
                nm = "%s%d" % (wname, i)
                srcv = src_ap.rearrange("(k p) n -> p k n", p=128)
                for kq in range(4):
                    grp = 'a' if kq == 0 else 'b'
                    c.dma('pool', 'w_%s_%s' % (nm, grp), Wbufs[i][:, kq * 4:(kq + 1) * 4, 0:ncols_total_view], srcv[:, kq * 4:(kq + 1) * 4, :],
                          writes=['%s_%s' % (nm, grp)], nowait=(kq > 1))
                return Wbufs[i], nm

            def phaseA_gen(pa, x_dram, ntiles, xnT):
                gnb_t = sbt(pa, "gnb_tg", [128, D], F32)
                xl = [sbt(pa, "xlg%d" % i, [128, D], F32) for i in range(2)]
                xnb = [sbt(pa, "xnbg%d" % i, [128, D], BF16) for i in range(2)]
                ss = sbt(pa, "ssg", [128, 16], F32)
                rs = sbt(pa, "rsg", [128, 16], F32)
                psA = [pst(pa, "psAg%d" % i, [128, 1024], BF16) for i in range(4)]
                epsg = sbt(pa, "epsg", [128, 1], F32)
                c.dma('sp', 'ld_gnbg', gnb_t[:], gnb[:, :], writes=['gnbg'])
                memset('dve', ss[:], 0.0, ['ssg'])
                memset('dve', epsg[:], EPS, ['epsg'])

                def tposes(t):
                    b = t % 2
                    xbn = "xnbg%d" % b
                    for hb in range(2):
                        pa_ = psA[(t % 2) * 2 + hb]
                        pan = "psAg%d" % ((t % 2) * 2 + hb)
                        for k in range(8):
                            kk = hb * 8 + k
                            tr(pa_[:, k * 128:(k + 1) * 128], xnb[b][:, kk * 128:(kk + 1) * 128], identb[:, :],
                               [xbn, 'identb'], [pan])
                        cp('act' if hb == 0 else 'dve', xnT[:, hb * 8:(hb + 1) * 8, t * 128:(t + 1) * 128],
                           pa_[:].rearrange("p (k n) -> p k n", k=8), [pan], ['xnT'])

                for t in range(ntiles):
                    b = t % 2
                    xn_, xbn = "xlg%d" % b, "xnbg%d" % b
                    c.dma('sp', 'ld_' + xn_, xl[b][:], x_dram[t * 128:(t + 1) * 128, :], writes=[xn_])
                    act(xnb[b][:], xl[b][:], AF.Square, [xn_], [xbn, 'ssg'], accum_out=ss[:, t:t + 1])
                    act(rs[:, t:t + 1], ss[:, t:t + 1], AF.Ln, ['ssg', 'epsg'], ['rsg'], scale=1.0 / D, bias=epsg[:, 0:1])
                    act(rs[:, t:t + 1], rs[:, t:t + 1], AF.Exp, ['rsg'], ['rsg'], scale=-0.5)
                    stt(xnb[b][:], xl[b][:], rs[:, t:t + 1], gnb_t[:], ALU.mult, ALU.mult, [xn_, 'rsg', 'gnbg'], [xbn])
                    if t >= 1:
                        tposes(t - 1)
                    yield
                tposes(ntiles - 1)
                yield

            def phaseA(stack_parent, x_dram, ntiles, xnT):
                with ExitStack() as pa:
                    gnb_t = sbt(pa, "gnb_t", [128, D], F32)
                    xl = [sbt(pa, "xl%d" % i, [128, D], F32) for i in range(2)]
                    junk = sbt(pa, "junkA", [128, D], BF16)
                    xnb = [sbt(pa, "xnb%d" % i, [128, D], BF16) for i in range(2)]
                    ss = sbt(pa, "ssA", [128, 16], F32)
                    rs = sbt(pa, "rsA", [128, 16], F32)
                    psA = [pst(pa, "psA%d" % i, [128, 1024], BF16) for i in range(4)]
                    c.dma('sp', 'ld_gnb', gnb_t[:], gnb[:, :], writes=['gnb'])
                    memset('dve', ss[:], 0.0, ['ssA'])
                    for t in range(ntiles):
                        b = t % 2
                        xn_, xbn = "xl%d" % b, "xnb%d" % b
                        c.dma('sp', 'ld_' + xn_, xl[b][:], x_dram[t * 128:(t + 1) * 128, :], writes=[xn_])
                        act(junk[:], xl[b][:], AF.Square, [xn_], ['junkA', 'ssA'], accum_out=ss[:, t:t + 1])
                        ts('dve', rs[:, t:t + 1], ss[:, t:t + 1], 1.0 / D, EPS, ALU.mult, ALU.add, ['ssA'], ['rsA'])
                        act(rs[:, t:t + 1], rs[:, t:t + 1], AF.Sqrt, ['rsA'], ['rsA'])
                        c.op('dve', lambda: V.reciprocal(out=rs[:, t:t + 1], in_=rs[:, t:t + 1]), ['rsA'], ['rsA'])
                        stt(xnb[b][:], xl[b][:], rs[:, t:t + 1], gnb_t[:], ALU.mult, ALU.mult, [xn_, 'rsA', 'gnb'], [xbn])
                        for hb in range(2):
                            pa_ = psA[(t % 2) * 2 + hb]
                            pan = "psA%d" % ((t % 2) * 2 + hb)
                            for k in range(8):
                                kk = hb * 8 + k
                                tr(pa_[:, k * 128:(k + 1) * 128], xnb[b][:, kk * 128:(kk + 1) * 128], identb[:, :],
                                   [xbn, 'identb'], [pan])
                            cp('act' if hb == 0 else 'dve', xnT[:, hb * 8:(hb + 1) * 8, t * 128:(t + 1) * 128],
                               pa_[:].rearrange("p (k n) -> p k n", k=8), [pan], ['xnT'])
                c.barrier()

            pfx = ExitStack()
            X2p = sbt_r(pfx, "X2p", [128, D, 8], BF16)
            with ExitStack() as p0:
                LR = sbt(p0, "LR", [128, 24, 64], F32)
                LI = sbt(p0, "LI", [128, 24, 64], F32)
                Bbr = sbt(p0, "Bbr", [128, 64, 16], F32)
                Bbi = sbt(p0, "Bbi", [128, 64, 16], F32)
                cre_t = sbt(p0, "cre_t", [128, 64, 16], F32)
                cim_t = sbt(p0, "cim_t", [128, 64, 16], F32)
                dcol_t = sbt(p0, "dcol_t", [128, 128], F32)
                m16_t = sbt(p0, "m16_t", [128, 128], F32)
                permI_t = sbt(p0, "permI_t", [128, 128], F32)
                c.dma('sp', 'ld_permI', permI_t[:], permI[:, :], writes=['permI'])
                c.dma('sp', 'ld_dcol', dcol_t[:], dcol[:, :], writes=['dcol'])
                c.dma('sp', 'ld_m16', m16_t[:], mask16[:, :], writes=['m16'])
                c.dma('sp', 'ld_cre', cre_t[:], cre[:, :, :], writes=['cre'])
                c.dma('sp', 'ld_cim', cim_t[:], cim[:, :, :], writes=['cim'])
                with ExitStack() as pre:
                    are_t = sbt(pre, "are_t", [128, 64], F32)
                    aim_t = sbt(pre, "aim_t", [128, 64], F32)
                    ldt_t = sbt(pre, "ldt_t", [128, 64], F32)
                    qv_t = sbt(pre, "qv_t", [128, 24, 64], F32)
                    bre_t = sbt(pre, "bre_t", [128, 64, 16], F32)
                    bim_t = sbt(pre, "bim_t", [128, 64, 16], F32)
                    for (tl, src, nm) in [(are_t, are, 'are'), (aim_t, aim, 'aim'), (ldt_t, ldt, 'ldt')]:
                        c.dma('sp', 'ld_' + nm, tl[:], src[:, :], writes=[nm])
                    c.dma('sp', 'ld_qv', qv_t[:], qv[:, :, :], writes=['qv'])
                    for (tl, src, nm) in [(bre_t, bre, 'bre'), (bim_t, bim, 'bim')]:
                        c.dma('sp', 'ld_' + nm, tl[:], src[:, :, :], writes=[nm])
                    dt_t = sbt(pre, "dt_t", [128, 64], F32)
                    er_t = sbt(pre, "er_t", [128, 64], F32)
                    et_t = sbt(pre, "et_t", [128, 64], F32)
                    act(dt_t[:], ldt_t[:], AF.Exp, ['ldt'], ['dt'])
                    tt('dve', er_t[:], are_t[:], dt_t[:], ALU.mult, ['are', 'dt'], ['er'])
                    tt('dve', et_t[:], aim_t[:], dt_t[:], ALU.mult, ['aim', 'dt'], ['et'])
                    ts('dve', et_t[:], et_t[:], 1.0 / TWO_PI, None, ALU.mult, None, ['et'], ['et'])
                    MAG = sbt(pre, "MAG", [128, 24, 64], F32)
                    TH = sbt(pre, "TH", [128, 24, 64], F32)
                    TIi = sbt(pre, "TIi", [128, 24, 64], I32)
                    TF = sbt(pre, "TF", [128, 24, 64], F32)
                    er_b = er_t[:].unsqueeze(1).to_broadcast([128, 24, 64])
                    et_b = et_t[:].unsqueeze(1).to_broadcast([128, 24, 64])
                    tt('dve', MAG[:], qv_t[:], er_b, ALU.mult, ['qv', 'er'], ['MAG'])
                    act(MAG[:], MAG[:], AF.Exp, ['MAG'], ['MAG'])
                    tt('dve', TH[:], qv_t[:], et_b, ALU.mult, ['qv', 'et'], ['TH'])
                    cp('dve', TIi[:], TH[:], ['TH'], ['TIi'])
                    cp('dve', TF[:], TIi[:], ['TIi'], ['TF'])
                    tt('dve', TF[:], TH[:], TF[:], ALU.subtract, ['TH', 'TF'], ['TF'])
                    act(LI[:], TF[:], AF.Sin, ['TF'], ['LI'], scale=TWO_PI)
                    TH2 = sbt(pre, "TH2", [128, 24, 64], F32)
                    TIi2 = sbt(pre, "TIi2", [128, 24, 64], I32)
                    TF2 = sbt(pre, "TF2", [128, 24, 64], F32)
                    ts('pool', TH2[:], TH[:], 0.25, None, ALU.add, None, ['TH'], ['TH2'])
                    cp('dve', TIi2[:], TH2[:], ['TH2'], ['TIi2'])
                    cp('dve', TF2[:], TIi2[:], ['TIi2'], ['TF2'])
                    tt('pool', TF2[:], TH2[:], TF2[:], ALU.subtract, ['TH2', 'TF2'], ['TF2'])
                    act(LR[:], TF2[:], AF.Sin, ['TF2'], ['LR'], scale=TWO_PI)
                    tt('dve', LR[:], LR[:], MAG[:], ALU.mult, ['LR', 'MAG'], ['LR'])
                    tt('dve', LI[:], LI[:], MAG[:], ALU.mult, ['LI', 'MAG'], ['LI'])
                    cp('dve', A8[:, 0, :], LR[:, 23, :], ['LR'], ['A8'])
                    cp('dve', A8[:, 1, :], LI[:, 23, :], ['LI'], ['A8'])
                    cp('dve', W8[:, :, 0], LR[:, 23, :], ['LR'], ['W8'])
                    cp('dve', W8[:, :, 3], LR[:, 23, :], ['LR'], ['W8'])
                    cp('dve', W8[:, :, 2], LI[:, 23, :], ['LI'], ['W8'])
                    ts('dve', W8[:, :, 1], LI[:, 23, :], -1.0, None, ALU.mult, None, ['LI'], ['W8'])
                    nr = sbt(pre, "nr", [128, 64], F32)
                    den = sbt(pre, "den", [128, 64], F32)
                    t_a = sbt(pre, "t_a", [128, 64], F32)
                    t_b = sbt(pre, "t_b", [128, 64], F32)
                    cr = sbt(pre, "cr", [128, 64], F32)
                    ci = sbt(pre, "ci", [128, 64], F32)
                    ni = LI[:, 16, :]
                    ts('dve', nr[:], LR[:, 16, :], -1.0, None, ALU.add, None, ['LR'], ['nr'])
                    tt('dve', den[:], are_t[:], are_t[:], ALU.mult, ['are'], ['den'])
                    tt('dve', t_a[:], aim_t[:], aim_t[:], ALU.mult, ['aim'], ['t_a'])
                    tt('dve', den[:], den[:], t_a[:], ALU.add, ['den', 't_a'], ['den'])
                    c.op('dve', lambda: V.reciprocal(out=den[:], in_=den[:]), ['den'], ['den'])
                    tt('dve', t_a[:], nr[:], are_t[:], ALU.mult, ['nr', 'are'], ['t_a'])
                    tt('dve', t_b[:], ni, aim_t[:], ALU.mult, ['LI', 'aim'], ['t_b'])
                    tt('dve', t_a[:], t_a[:], t_b[:], ALU.add, ['t_a', 't_b'], ['t_a'])
                    tt('dve', cr[:], t_a[:], den[:], ALU.mult, ['t_a', 'den'], ['cr'])
                    tt('dve', t_a[:], ni, are_t[:], ALU.mult, ['LI', 'are'], ['t_a'])
                    tt('dve', t_b[:], nr[:], aim_t[:], ALU.mult, ['nr', 'aim'], ['t_b'])
                    tt('dve', t_a[:], t_a[:], t_b[:], ALU.subtract, ['t_a', 't_b'], ['t_a'])
                    tt('dve', ci[:], t_a[:], den[:], ALU.mult, ['t_a', 'den'], ['ci'])
                    tq1 = sbt(pre, "tq1", [128, 64, 16], F32)
                    tq2 = sbt(pre, "tq2", [128, 64, 16], F32)
                    cr_b = cr[:].unsqueeze(2).to_broadcast([128, 64, 16])
                    ci_b = ci[:].unsqueeze(2).to_broadcast([128, 64, 16])
                    tt('dve', Bbr[:], bre_t[:], cr_b, ALU.mult, ['bre', 'cr'], ['Bbr'])
                    tt('pool', tq1[:], bim_t[:], ci_b, ALU.mult, ['bim', 'ci'], ['tq1'])
                    tt('dve', Bbr[:], Bbr[:], tq1[:], ALU.subtract, ['Bbr', 'tq1'], ['Bbr'])
                    tt('dve', Bbi[:], bim_t[:], cr_b, ALU.mult, ['bim', 'cr'], ['Bbi'])
                    tt('pool', tq2[:], bre_t[:], ci_b, ALU.mult, ['bre', 'ci'], ['tq2'])
                    tt('dve', Bbi[:], Bbi[:], tq2[:], ALU.add, ['Bbi', 'tq2'], ['Bbi'])
                c.barrier()

                BPr = [sbt(p0, "BPr%d" % i, [128, 4, 128], BF16) for i in range(2)]
                BPi = [sbt(p0, "BPi%d" % i, [128, 4, 128], BF16) for i in range(2)]
                CQr = [sbt(p0, "CQr%d" % i, [128, 4, 256], BF16) for i in range(2)]
                CQn = [sbt(p0, "CQn%d" % i, [128, 4, 256], BF16) for i in range(2)]
                Dd = sbt(p0, "Dd", [128, 8, 128], BF16)
                Msc = [sbt(p0, "Msc%d" % i, [128, 8, 128], BF16) for i in range(2)]
                BPTz = sbt(p0, "BPTz", [128, 8, 2, 128], BF16)
                TA = [sbt(p0, "TA%d" % i, [128, 4, 128], F32) for i in range(4)]
                TC = [sbt(p0, "TC%d" % i, [128, 4, 256], F32) for i in range(4)]
                psM = [pst(p0, "psM%d" % i, [128, 512], F32) for i in range(4)]
                psTb = [pst(p0, "psTb%d" % i, [128, 1024], BF16) for i in range(2)]
                memset('pool', BPTz[:], 0.0, ['BPTz'])

                def p0_chunk(ch):
                    p0_ = ch * 4
                    g0 = ch * 8
                    db = ch % 2
                    bpr, bprn = BPr[db], "BPr%d" % db
                    bpi, bpin = BPi[db], "BPi%d" % db
                    cqr, cqrn = CQr[db], "CQr%d" % db
                    cqn, cqnn = CQn[db], "CQn%d" % db
                    msc, mscn = Msc[db], "Msc%d" % db
                    LRb = LR[:, 0:8, p0_:p0_ + 4].rearrange("p i r -> p r i").unsqueeze(3).to_broadcast([128, 4, 8, 16])
                    LIb = LI[:, 0:8, p0_:p0_ + 4].rearrange("p i r -> p r i").unsqueeze(3).to_broadcast([128, 4, 8, 16])
                    Bbr_b = Bbr[:, p0_:p0_ + 4, :].unsqueeze(2).to_broadcast([128, 4, 8, 16])
                    Bbi_b = Bbi[:, p0_:p0_ + 4, :].unsqueeze(2).to_broadcast([128, 4, 8, 16])
                    tav = [TA[i][:].rearrange("p r (c i) -> p r i c", i=8) for i in range(4)]
                    tt('pool', tav[0], LRb, Bbr_b, ALU.mult, ['LR', 'Bbr'], ['TA0'])
                    tt('pool', tav[1], LIb, Bbi_b, ALU.mult, ['LI', 'Bbi'], ['TA1'])
                    tt('dve', bpr[:], TA[0][:], TA[1][:], ALU.subtract, ['TA0', 'TA1'], [bprn])
                    tt('pool', tav[2], LRb, Bbi_b, ALU.mult, ['LR', 'Bbi'], ['TA2'])
                    tt('pool', tav[3], LIb, Bbr_b, ALU.mult, ['LI', 'Bbr'], ['TA3'])
                    tt('dve', bpi[:], TA[2][:], TA[3][:], ALU.add, ['TA2', 'TA3'], [bpin])
                    LRc = LR[:, 8:24, p0_:p0_ + 4].rearrange("p s r -> p r s").unsqueeze(3).to_broadcast([128, 4, 16, 16])
                    LIc = LI[:, 8:24, p0_:p0_ + 4].rearrange("p s r -> p r s").unsqueeze(3).to_broadcast([128, 4, 16, 16])
                    cre_b = cre_t[:, p0_:p0_ + 4, :].unsqueeze(2).to_broadcast([128, 4, 16, 16])
                    cim_b = cim_t[:, p0_:p0_ + 4, :].unsqueeze(2).to_broadcast([128, 4, 16, 16])
                    tcv = [TC[i][:].rearrange("p r (s c) -> p r s c", s=16) for i in range(4)]
                    cqrv = cqr[:].rearrange("p r (s c) -> p r s c", s=16)
                    cqnv = cqn[:].rearrange("p r (s c) -> p r s c", s=16)
                    tt('pool', tcv[0], LRc, cre_b, ALU.mult, ['LR', 'cre'], ['TC0'])
                    tt('pool', tcv[1], LIc, cim_b, ALU.mult, ['LI', 'cim'], ['TC1'])
                    tt('dve', cqrv, tcv[0], tcv[1], ALU.subtract, ['TC0', 'TC1'], [cqrn])
                    tt('pool', tcv[2], LRc, cim_b, ALU.mult, ['LR', 'cim'], ['TC2'])
                    tt('dve', tcv[3], LIc, cre_b, ALU.mult, ['LI', 'cre'], ['TC3'])
                    stt(cqnv, tcv[2], -1.0, tcv[3], ALU.mult, ALU.subtract, ['TC2', 'TC3'], [cqnn])
                    for gi in range(8):
                        ts('dve', Dd[:, gi, :], permI_t[:, :], dcol_t[:, g0 + gi:g0 + gi + 1], None, ALU.mult, None, ['permI', 'dcol'], ['Dd'])
                    for gq in range(2):
                        pm = psM[db * 2 + gq]
                        pmn = "psM%d" % (db * 2 + gq)
                        for gl in range(4):
                            gi = gq * 4 + gl
                            pl, g2 = gi // 2, gi % 2
                            hs = slice(g2 * 64, g2 * 64 + 64)
                            o_ = pm[:, gl * 128:(gl + 1) * 128]
                            mm(o_, bpr[hs, pl, :], cqr[hs, pl, 0:128], True, False, [bprn, cqrn], [pmn])
                            mm(o_, bpi[hs, pl, :], cqn[hs, pl, 0:128], False, False, [bpin, cqnn], [pmn])
                            mm(o_, identb[:, :], Dd[:, gi, :], False, True, ['identb', 'Dd'], [pmn])
                    ptn_ = "psTb%d" % db
                    ptv = psTb[db][:, 0:1024].rearrange("p (a r q) -> p a r q", a=4, r=2)
                    for a in range(4):
                        tr(ptv[:, a, 0, :], bpr[:, a, :], identb[:, :], [bprn, 'identb'], [ptn_])
                        tr(ptv[:, a, 1, :], bpi[:, a, :], identb[:, :], [bpin, 'identb'], [ptn_])

                def p0_finish(ch):
                    p0_ = ch * 4
                    g0 = ch * 8
                    db = ch % 2
                    cqr, cqrn = CQr[db], "CQr%d" % db
                    cqn, cqnn = CQn[db], "CQn%d" % db
                    msc, mscn = Msc[db], "Msc%d" % db
                    for gq in range(2):
                        pm = psM[db * 2 + gq]
                        pmn = "psM%d" % (db * 2 + gq)
                        tt('dve', msc[:, gq * 4:(gq + 1) * 4, :], pm[:].rearrange("p (g n) -> p g n", g=4),
                           m16_t[:].unsqueeze(1).to_broadcast([128, 4, 128]), ALU.mult, [pmn, 'm16'], [mscn])
                    ptn_ = "psTb%d" % db
                    ptv = psTb[db][:, 0:1024].rearrange("p (a r q) -> p a r q", a=4, r=2)
                    bzv = BPTz[:].rearrange("p (a g2) r q -> p a g2 r q", g2=2)
                    for g2 in range(2):
                        cp('dve', bzv[:, :, g2, :, g2 * 64:(g2 + 1) * 64], ptv[:, :, :, g2 * 64:(g2 + 1) * 64], [ptn_], ['BPTz'])
                    c.dma('sp', 'st_BPTz', s_bptz[:, g0:g0 + 8, :, :], BPTz[:], reads=['BPTz'])
                    c.dma('sp', 'st_' + cqrn, s_cq1[:, p0_:p0_ + 4, 0, :], cqr[:, :, 128:256], reads=[cqrn])
                    c.dma('sp', 'st_' + cqnn, s_cq1[:, p0_:p0_ + 4, 1, :], cqn[:, :, 128:256], reads=[cqnn])
                    c.dma('sp', 'st_' + mscn, s_msup[:, g0:g0 + 8, :], msc[:], reads=[mscn])

                xnTp = sbt(p0, "xnTp", [128, 16, NPR], BF16)

                def prefix_front():
                    gcol = sbt(p0, "gcol", [128, 16], F32)
                    epsb = sbt(p0, "epsb", [128, 1], F32)
                    c.dma('act', 'ld_gcol', gcol[:], gncol[:, :], writes=['gcol'])
                    with ExitStack() as pa:
                        xl = [sbt(pa, "xlp%d" % i, [128, D], F32) for i in range(2)]
                        xnb = [sbt(pa, "xnbp%d" % i, [128, D], BF16) for i in range(2)]
                        ss = sbt(pa, "ssp", [128, 16], F32)
                        rs = sbt(pa, "rsp", [128, 16], F32)
                        psA = [pst(pa, "psAp%d" % i, [128, 1024], BF16) for i in range(2)]
                        c.op('act', lambda: A.activation(out=epsb[:], in_=gcol[:, 0:1], func=AF.Copy, scale=0.0, bias=EPS),
                             ['gcol'], ['epsb'])
                        c.op('act', lambda: A.activation(out=ss[:], in_=gcol[:, :], func=AF.Copy, scale=0.0), ['gcol'], ['ssp'])

                        def tposes(t):
                            b = t % 2
                            xbn = "xnbp%d" % b
                            for hb in range(2):
                                pa_, pan = psA[hb], "psAp%d" % hb
                                for k in range(8):
                                    kk = hb * 8 + k
                                    tr(pa_[:, k * 128:(k + 1) * 128], xnb[b][:, kk * 128:(kk + 1) * 128], identb[:, :],
                                       [xbn, 'identb'], [pan])
                                for k in range(8):
                                    kk = hb * 8 + k
                                    act(xnTp[:, kk, t * 128:(t + 1) * 128], pa_[:, k * 128:(k + 1) * 128], AF.Copy,
                                        [pan, 'gcol'], ['xnTp'], scale=gcol[:, kk:kk + 1])

                        c.dma('act', 'ld_xlp0', xl[0][:], xp[0:128, :], writes=['xlp0'])
                        for t in range(8):
                            b = t % 2
                            xn_, xbn = "xlp%d" % b, "xnbp%d" % b
                            if t + 1 < 8:
                                nb_ = (t + 1) % 2
                                c.dma('act', 'ld_xlp%d' % nb_, xl[nb_][:], xp[(t + 1) * 128:(t + 2) * 128, :], writes=['xlp%d' % nb_])
                            act(xnb[b][:], xl[b][:], AF.Square, [xn_], [xbn, 'ssp'], accum_out=ss[:, t:t + 1])
                            act(rs[:, t:t + 1], ss[:, t:t + 1], AF.Ln, ['ssp', 'epsb'], ['rsp'], scale=1.0 / D, bias=epsb[:, 0:1])
                            act(rs[:, t:t + 1], rs[:, t:t + 1], AF.Exp, ['rsp'], ['rsp'], scale=-0.5)
                            act(xnb[b][:], xl[b][:], AF.Copy, [xn_, 'rsp'], [xbn], scale=rs[:, t:t + 1])
                            if t >= 1:
                                tposes(t - 1)
                            yield
                        tposes(7)
                        yield
                    with ExitStack() as pb:
                        Wb = [sbt(pb, "Wxp%d" % i, [128, 16, 256], BF16) for i in range(2)]
                        Wst = [sbt(pb, "Wst%d" % i, [128, 4, 256], F32) for i in range(3)]
                        psX = [pst(pb, "psXp%d" % i, [128, 512], F32) for i in range(2)]
                        pi = 0

                        def issue_piece(p):
                            if p >= 32:
                                return
                            cb_, kq_ = p // 4, p % 4
                            srcv = w_in[:, 6144 + cb_ * 256:6144 + (cb_ + 1) * 256].rearrange("(k p) n -> p k n", p=128)
                            c.dma('act', 'ld_Wst%d' % (p % 3), Wst[p % 3][:], srcv[:, kq_ * 4:(kq_ + 1) * 4, :], writes=['Wst%d' % (p % 3)])

                        for p in range(3):
                            issue_piece(p)
                        for cb in range(8):
                            Wt, wn = Wb[cb % 2], "Wxp%d" % (cb % 2)
                            for kq in range(4):
                                p = cb * 4 + kq
                                act(Wt[:, kq * 4:(kq + 1) * 4, :], Wst[p % 3][:], AF.Copy, ['Wst%d' % (p % 3)], ['%s_%d' % (wn, kq)])
                                issue_piece(p + 3)
                            for i in range(8):
                                px, pxn = psX[pi % 2], "psXp%d" % (pi % 2)
                                pi += 1
                                for k in range(16):
                                    mm(px[:, 0:256], xnTp[:, k, i:NPR:8], Wt[:, k, :], k == 0, k == 15,
                                       ['xnTp', '%s_%d' % (wn, k // 4)], [pxn])
                                cp('act', X2p[:, cb * 256:(cb + 1) * 256, i], px[:, 0:256], [pxn], ['X2p'])
                                yield

                ga = prefix_front()
                for ch in range(16):
                    p0_chunk(ch)
                    for _ in range(3 if ch < 3 else 5):
                        next(ga, None)
                    if ch >= 1:
                        p0_finish(ch - 1)
                p0_finish(15)
                for _ in ga:
                    pass
            c.barrier()

            def scan_chain(BuH, Pst, side_gen=None):
                W8v = W8[:].rearrange("p n (d c) -> p n d c", d=2)
                c.barrier()
                if side_gen is not None:
                    cp('dve', Pst[1][:, :, :, 2], BuH[:, :, :, 1], ['BuHin'], ['Pst1b'])
                for m in range(1, 129):
                    if side_gen is not None and m % 4 == 0:
                        next(side_gen, None)
                    Pt, ptn = Pst[m % 2], "Pst%d" % (m % 2)
                    Xp_ = BuH[:, :, :, m - 1]
                    Xc = BuH[:, :, :, m]
                    if side_gen is None:
                        cp('act', Pt[:, :, :, 2], Xc, ['BuHin'], [ptn + 'b'])
                    tt('dve', Pt[:, :, :, 0:2], Xp_.unsqueeze(2).to_broadcast([128, 64, 2, 2]), W8v, ALU.mult,
                       ['BuH', 'W8'], [ptn])
                    if side_gen is not None and m < 128:
                        Pn, pnn = Pst[(m + 1) % 2], "Pst%d" % ((m + 1) % 2)
                        cp('dve', Pn[:, :, :, 2], BuH[:, :, :, m + 1], ['BuHin'], [pnn + 'b'])
                    c.op('dve', lambda: V.tensor_reduce(out=Xc, in_=Pt[:], axis=mybir.AxisListType.X, op=ALU.add),
                         ['BuH', ptn, ptn + 'b'], ['BuH'])
                c.barrier()

            rmx = ExitStack()
            with ExitStack() as pc:
                BuHp = sbt(pc, "BuHp", [128, 64, 2, 129], F32)
                Pstp = [sbt(pc, "Pstp%d" % i, [128, 64, 2, 3], F32) for i in range(2)]
                with ExitStack() as pcu:
                    BPop = [sbt(pcu, "BPop%d" % i, [128, 8, 2, 128], BF16) for i in range(2)]
                    Uop = [sbt(pcu, "Uop%d" % i, [128, 8, 128], BF16) for i in range(2)]
                    psUp = [pst(pcu, "psUp%d" % i, [128, 1024], BF16) for i in range(2)]
                    psBp = [pst(pcu, "psBp%d" % i, [128, 512], F32) for i in range(4)]
                    memset('dve', BuHp[:, :, :, 0], 0.0, ['BuHp'])
                    pbi = 0
                    for o in range(16):
                        bpo, bpn = BPop[o % 2], "BPop%d" % (o % 2)
                        uo, uon = Uop[o % 2], "Uop%d" % (o % 2)
                        pu, pun = psUp[o % 2], "psUp%d" % (o % 2)
                        c.dma('sp', 'ld_' + bpn, bpo[:], s_bptz[:, o * 8:(o + 1) * 8, :, :], reads=['s_bptz'], writes=[bpn])
                        puv = pu[:].rearrange("p (g m) -> p g m", g=8)
                        for gl in range(8):
                            g = o * 8 + gl
                            tr(puv[:, gl, :], X2p[:, g * 16:(g + 1) * 16, :].rearrange("p c i -> p (c i)"), identb[:, :],
                               ['X2p', 'identb'], [pun])
                        cp('act' if o % 2 == 0 else 'dve', uo[:], puv, [pun], [uon])
                        for pl in range(4):
                            pb, pbn = psBp[pbi % 4], "psBp%d" % (pbi % 4)
                            pbi += 1
                            pbv = pb[:, 0:256].rearrange("p (r m) -> p r m", r=2)
                            for r in range(2):
                                for g2 in range(2):
                                    mm(pbv[:, r, :], bpo[:, pl * 2 + g2, r, :], uo[:, pl * 2 + g2, :], g2 == 0, g2 == 1, [bpn, uon], [pbn])
                            cp('act' if pl % 2 == 0 else 'dve', BuHp[:, o * 4 + pl, :, 1:129], pbv, [pbn], ['BuHp'])
                c.barrier()
                pfx.close()
                X2m = sbt_r(rmx, "X2m", [128, D, 8], BF16)
                X2sm = sbt_r(rmx, "X2sm", [128, 1024, 8], BF16)

                mf_use_dve = [False]

                def main_front():
                    epsb2 = sbt(pc, "epsb2", [128, 1], F32)
                    xnTm = sbt(pc, "xnTm", [128, 16, NT], BF16)
                    with ExitStack() as pa:
                        gnbm = sbt(pa, "gnbm", [128, D], F32)
                        xl2 = [sbt(pa, "xlm%d" % i, [128, D], F32) for i in range(2)]
                        xs_ = [sbt(pa, "xsm%d" % i, [128, D], BF16) for i in range(2)]
                        xnb = [sbt(pa, "xnbm%d" % i, [128, D], BF16) for i in range(2)]
                        ss = sbt(pa, "ssm", [128, 16], F32)
                        rs = sbt(pa, "rsm", [128, 16], F32)
                        psA = [pst(pa, "psAm_%d" % i, [128, 1024], BF16) for i in range(2)]
                        c.dma('act', 'ld_gnbm', gnbm[:], gnb[:, :], writes=['gnbm'])
                        c.op('pool', lambda: P.memset(epsb2[:], EPS), (), ['epsb2'])
                        c.op('pool', lambda: P.memset(ss[:], 0.0), (), ['ssm'])

                        def tposes(t):
                            b = t % 2
                            xbn = "xnbm%d" % b
                            for hb in range(2):
                                pa_, pan = psA[hb], "psAm_%d" % hb
                                for k in range(8):
                                    kk = hb * 8 + k
                                    tr(pa_[:, k * 128:(k + 1) * 128], xnb[b][:, kk * 128:(kk + 1) * 128], identb[:, :],
                                       [xbn, 'identb'], [pan])
                                cp('act', xnTm[:, hb * 8:(hb + 1) * 8, t * 128:(t + 1) * 128],
                                   pa_[:].rearrange("p (k n) -> p k n", k=8), [pan], ['xnTm'])

                        c.dma('act', 'ld_xlm0', xl2[0][:], xm[0:128, :], writes=['xlm0'])
                        for t in range(9):
                            b = t % 2
                            xsn, xbn = "xsm%d" % b, "xnbm%d" % b
                            xl, xln = xl2[b], "xlm%d" % b
                            if t + 1 < 9:
                                nb_ = (t + 1) % 2
                                c.dma('act', 'ld_xlm%d' % nb_, xl2[nb_][:], xm[(t + 1) * 128:(t + 2) * 128, :], writes=['xlm%d' % nb_])
                            act(xs_[b][:], xl[:], AF.Square, [xln], [xsn, 'ssm'], accum_out=ss[:, t:t + 1])
                            act(rs[:, t:t + 1], ss[:, t:t + 1], AF.Ln, ['ssm', 'epsb2'], ['rsm'], scale=1.0 / D, bias=epsb2[:, 0:1])
                            act(rs[:, t:t + 1], rs[:, t:t + 1], AF.Exp, ['rsm'], ['rsm'], scale=-0.5)
                            act(xs_[b][:], xl[:], AF.Copy, [xln, 'rsm'], [xsn], scale=rs[:, t:t + 1])
                            tt('pool', xnb[b][:], xs_[b][:], gnbm[:], ALU.mult, [xsn, 'gnbm'], [xbn])
                            if t >= 1:
                                tposes(t - 1)
                            yield
                        tposes(8)
                        yield
                    c.fence('pool', include_self=True)
                    with ExitStack() as pb:
                        Wb = [sbt(pb, "Wxm%d" % i, [128, 16, 512], BF16) for i in range(2)]
                        XS = [sbt(pb, "XSm%d" % i, [128, 512], BF16) for i in range(2)]
                        XSr = sbt(pb, "XSrm", [128, 8, 512], BF16)
                        psX = [pst(pb, "psXm%d" % i, [128, 512], F32) for i in range(6)]
                        pi = 0
                        for cb in range(4):
                            Wt, wn = load_w(Wb, "Wxm", w_in[:, 6144 + cb * 512:6144 + (cb + 1) * 512], 512)
                            for i in range(9):
                                px, pxn = psX[pi % 6], "psXm%d" % (pi % 6)
                                pi += 1
                                for k in range(16):
                                    lt = xnTm[:, k, i:NPR:8] if i < 8 else xnTm[:, k, NPR:NT]
                                    mm(px[:], lt, Wt[:, k, :], k == 0, k == 15,
                                       ['xnTm', '%s_%s' % (wn, 'a' if k < 4 else 'b')], [pxn])
                                if i < 8:
                                    cp('dve' if (mf_use_dve[0] and i % 2 == 1) else 'act', X2m[:, cb * 512:(cb + 1) * 512, i], px[:],
                                       [pxn], ['X2m'])
                                else:
                                    xs, xsn = XS[cb % 2], "XSm%d" % (cb % 2)
                                    cp('act', xs[:], px[:], [pxn], [xsn])
                                    q = cb // 2
                                    c.dma('act', 'rgm', XSr[q * 64:q * 64 + 16, :, :], xs[:], reads=[xsn, 'XSrm'], writes=['XSrm'])
                                    cp('act', X2sm[q * 64:q * 64 + 16, (cb % 2) * 512:(cb % 2 + 1) * 512, :].rearrange("p n i -> p i n"),
                                       XSr[q * 64:q * 64 + 16, :, :], ['XSrm'], ['X2sm'])
                                yield

                gm = main_front()
                scan_chain(BuHp, Pstp, side_gen=gm)
                cp('dve', Hmid[:], BuHp[:, :, :, 128], ['BuHp'], ['Hmid'])
                mf_use_dve[0] = True
                for _ in gm:
                    pass
            c.barrier()
            if stop == 0:
                c.dead = True

            with ExitStack() as s5:
                Uall = sbt(s5, "Uall", [128, 128, 144], BF16)
                for is_main in (True,):
                    ntiles = 9 if is_main else 8
                    x_dram = xm if is_main else xp
                    ncol = 144 if is_main else 128
                    with ExitStack() as sx:
                        X2 = X2m
                        X2s = X2sm
                        with ExitStack() as su:
                            psU = [pst(su, "psU%d" % i, [128, 1024], BF16) for i in range(2)]
                            psUs = [pst(su, "psUs%d" % i, [128, 512], F32) for i in range(2)]
                            for o in range(16):
                                pu, pun = psU[o % 2], "psU%d" % (o % 2)
                                pus, pusn = psUs[o % 2], "psUs%d" % (o % 2)
                                puv = pu[:].rearrange("p (g m) -> p g m", g=8)
                                pusv = pus[:, 0:128].rearrange("p (g m) -> p g m", g=8)
                                for gl in range(8):
                                    g = o * 8 + gl
                                    tr(puv[:, gl, :], X2[:, g * 16:(g + 1) * 16, :].rearrange("p c i -> p (c i)"), identb[:, :],
                                       ['X2', 'identb'], [pun])
                                    if is_main:
                                        q = g // 64
                                        col = (g % 64) * 16
                                        mm(pusv[:, gl, :], X2s[q * 64:q * 64 + 16, col:col + 16, :].rearrange("p c i -> p (c i)"),
                                           identb[q * 64:q * 64 + 16, q * 64:q * 64 + 16], True, True, ['X2s', 'identb'], [pusn])
                                cp('act' if o % 2 == 0 else 'dve', Uall[:, o * 8:(o + 1) * 8, 0:128], puv, [pun], ['Uall'])
                                if is_main:
                                    cp('dve' if o % 2 == 0 else 'act', Uall[:, o * 8:(o + 1) * 8, 128:144], pusv, [pusn], ['Uall'])
                        c.barrier()
                    c.barrier()

                    rmx.close()
                    ybT = sbt_r(st, "ybT", [128, 16, NT], BF16)
                    with ExitStack() as sc:
                        BuH = sbt(sc, "BuH", [128, 64, 2, 145], F32)
                        Pst = [sbt(sc, "Pst%d" % i, [128, 64, 2, 3], F32) for i in range(2)]
                        BPo = [sbt(sc, "BPo%d" % i, [128, 8, 2, 128], BF16) for i in range(2)]
                        Hfin = sbt(sc, "Hfin", [128, 64, 2], F32)
                        if is_main:
                            HS0 = sbt(sc, "HS0", [128, 64, 2, 16], F32)
                            T1s = sbt(sc, "T1s", [128, 32, 2, 16], F32)
                            Pcs = sbt(sc, "Pcs", [128, 32, 2, 16], F32)
                            CQo = [sbt(sc, "CQo%d" % i, [128, 4, 2, 128], BF16) for i in range(2)]
                            MSo = [sbt(sc, "MSo%d" % i, [128, 8, 128], BF16) for i in range(2)]
                            Hbf = [sbt(sc, "Hbf%d" % i, [128, 4, 2, 144], BF16) for i in range(2)]
                            Y2o = [sbt(sc, "Y2o%d" % i, [128, 8, 128], BF16) for i in range(2)]
                            Y2so = [sbt(sc, "Y2so%d" % i, [16, 8, 128], BF16) for i in range(2)]
                            c.dma('sp', 'ld_h0', HS0[:], h0[:, :, :, :], writes=['HS0'])
                            cp('dve', BuH[:, :, :, 0], Hmid[:], ['Hmid'], ['BuH'])
                        else:
                            memset('dve', BuH[:, :, :, 0], 0.0, ['BuH'])
                        with ExitStack() as sbu:
                            psB = [pst(sbu, "psB%d" % i, [128, 512], F32) for i in range(4)]
                            pbi = 0
                            for o in range(16):
                                bpo = BPo[o % 2]
                                bpn = "BPo%d" % (o % 2)
                                c.dma('sp', 'ld_' + bpn, bpo[:], s_bptz[:, o * 8:(o + 1) * 8, :, :], reads=['s_bptz'], writes=[bpn])
                                for pl in range(4):
                                    pb = psB[pbi % 4]
                                    pbn = "psB%d" % (pbi % 4)
                                    pbi += 1
                                    pbv = pb[:, 0:288].rearrange("p (r m) -> p r m", r=2)
                                    for r in range(2):
                                        for g2 in range(2):
                                            g = o * 8 + pl * 2 + g2
                                            mm(pbv[:, r, 0:ncol], bpo[:, pl * 2 + g2, r, :], Uall[:, g, 0:ncol],
                                               g2 == 0, g2 == 1, [bpn, 'Uall'], [pbn])
                                    prl = o * 4 + pl
                                    cp('act' if pl % 2 == 0 else 'dve', BuH[:, prl, :, 1:1 + ncol], pbv[:, :, 0:ncol], [pbn], ['BuH'])
                        W8v = W8[:].rearrange("p n (d c) -> p n d c", d=2)
                        c.barrier()
                        for m in range(1, 129):
                            Pt, ptn = Pst[m % 2], "Pst%d" % (m % 2)
                            Xp_ = BuH[:, :, :, m - 1]
                            Xc = BuH[:, :, :, m]
                            cp('act', Pt[:, :, :, 2], Xc, ['BuHin'], [ptn + 'b'])
                            tt('dve', Pt[:, :, :, 0:2], Xp_.unsqueeze(2).to_broadcast([128, 64, 2, 2]), W8v, ALU.mult,
                               ['BuH', 'W8'], [ptn])
                            c.op('dve', lambda: V.tensor_reduce(out=Xc, in_=Pt[:], axis=mybir.AxisListType.X, op=ALU.add),
                                 ['BuH', ptn, ptn + 'b'], ['BuH'])
                        c.barrier()
                        if not is_main:
                            cp('dve', Hmid[:], BuH[:, :, :, 128], ['BuH'], ['Hmid'])
                        else:
                            cp('dve', Hfin[:], BuH[:, :, :, 128], ['BuH'], ['Hfin'])
                            c.dma('sp', 'st_hp', hp_out[:, :, :], Hfin[:], reads=['Hfin'])
                            for hf in range(2):
                                prs = slice(hf * 32, hf * 32 + 32)
                                for cc in range(2):
                                    tt('dve', Pcs[:], HS0[:, prs, cc, :].unsqueeze(2).to_broadcast([128, 32, 2, 16]),
                                       W8v[:, prs, :, cc].unsqueeze(3).to_broadcast([128, 32, 2, 16]), ALU.mult, ['HS0', 'W8'], ['Pcs'])
                                    if cc == 0:
                                        tt('dve', T1s[:], BuH[:, prs, :, 129:145], Pcs[:], ALU.add, ['BuH', 'Pcs'], ['T1s'])
                                    else:
                                        tt('dve', T1s[:], T1s[:], Pcs[:], ALU.add, ['T1s', 'Pcs'], ['T1s'])
                                c.dma('sp', 'st_hs', hs_out[:, prs, :, :], T1s[:], reads=['T1s'])
                            with ExitStack() as sy:
                                psY = [pst(sy, "psY%d" % i, [128, 512], F32) for i in range(2)]
                                psYs = [pst(sy, "psYs%d" % i, [128, 512], F32) for i in range(2)]
                                psT2 = [pst(sy, "psT2%d" % i, [128, 1024], BF16) for i in range(2)]
                                psT2s = [pst(sy, "psT2s%d" % i, [128, 512], F32) for i in range(2)]
                                def y_transposes(o):
                                    b2 = o % 2
                                    y2o, y2n = Y2o[b2], "Y2o%d" % b2
                                    y2so, y2sn = Y2so[b2], "Y2so%d" % b2
                                    p2, p2n = psT2[b2], "psT2%d" % b2
                                    p2s, p2sn = psT2s[b2], "psT2s%d" % b2
                                    p2v = p2[:].rearrange("p (j m) -> p j m", j=8)
                                    p2sv = p2s[:, 0:128].rearrange("p (j s) -> p j s", j=8)
                                    for j in range(8):
                                        tr(p2v[:, j, :], y2o[:, j, :], identb[:, :], [y2n, 'identb'], [p2n])
                                        mm(p2sv[:, j, :], y2so[0:16, j, :], identb[0:16, 0:16], True, True, [y2sn, 'identb'], [p2sn])
                                    ybv = ybT[:, o, 0:NPR].rearrange("p (m j) -> p j m", j=8)
                                    cp('dve', ybv[:, 0:4, :], p2v[:, 0:4, :], [p2n], ['ybT'])
                                    cp('act', ybv[:, 4:8, :], p2v[:, 4:8, :], [p2n], ['ybT'])
                                    cp('dve', ybT[:, o, NPR:NT].rearrange("p (s j) -> p j s", j=8), p2sv, [p2sn], ['ybT'])

                                qi = 0
                                for o in range(16):
                                    b2 = o % 2
                                    cqo, cqn = CQo[b2], "CQo%d" % b2
                                    mso, msn = MSo[b2], "MSo%d" % b2
                                    hbf, hbn = Hbf[b2], "Hbf%d" % b2
                                    y2o, y2n = Y2o[b2], "Y2o%d" % b2
                                    y2so, y2sn = Y2so[b2], "Y2so%d" % b2
                                    c.dma('sp', 'ld_' + cqn, cqo[:], s_cq1[:, o * 4:(o + 1) * 4, :, :], reads=['s_cq1'], writes=[cqn])
                                    c.dma('sp', 'ld_' + msn, mso[:], s_msup[:, o * 8:(o + 1) * 8, :], reads=['s_msup'], writes=[msn])
                                    cp('pool', hbf[:, :, :, 0:128], BuH[:, o * 4:(o + 1) * 4, :, 0:128], ['BuH'], [hbn])
                                    cp('pool', hbf[:, :, :, 128:144], HS0[:, o * 4:(o + 1) * 4, :, :], ['HS0'], [hbn])
                                    for quad in range(2):
                                        py, pyn = psY[qi % 2], "psY%d" % (qi % 2)
                                        pys, pysn = psYs[qi % 2], "psYs%d" % (qi % 2)
                                        qi += 1
                                        pyv = py[:].rearrange("p (g n) -> p g n", g=4)
                                        pysv = pys[:].rearrange("p (g n) -> p g n", g=4)
                                        for gl4 in range(4):
                                            gl = quad * 4 + gl4
                                            g = o * 8 + gl
                                            pl, g2 = gl // 2, gl % 2
                                            hs = slice(g2 * 64, g2 * 64 + 64)
                                            mm(pyv[:, gl4, :], hbf[hs, pl, 0, 0:128], cqo[hs, pl, 0, :], True, False, [hbn, cqn], [pyn])
                                            mm(pyv[:, gl4, :], hbf[hs, pl, 1, 0:128], cqo[hs, pl, 1, :], False, False, [hbn, cqn], [pyn])
                                            mm(pyv[:, gl4, :], Uall[:, g, 0:128], mso[:, gl, :], False, True, ['Uall', msn], [pyn])
                                            mm(pysv[0:16, gl4, :], hbf[hs, pl, 0, 128:144], cqo[hs, pl, 0, :], True, False, [hbn, cqn], [pysn])
                                            mm(pysv[0:16, gl4, :], hbf[hs, pl, 1, 128:144], cqo[hs, pl, 1, :], False, False, [hbn, cqn], [pysn])
                                            mm(pysv[0:16, gl4, :], Uall[:, g, 128:144], mso[:, gl, :], False, True, ['Uall', msn], [pysn])
                                        act(y2o[:, :, quad * 64:(quad + 1) * 64].rearrange("p j (g c) -> p g j c", g=4),
                                            pyv.rearrange("p g (j c) -> p g j c", j=8), AF.Gelu_apprx_tanh, [pyn], [y2n])
                                        act(y2so[:, :, quad * 64:(quad + 1) * 64].rearrange("p j (g c) -> p g j c", g=4),
                                            pysv[0:16].rearrange("p g (j c) -> p g j c", j=8), AF.Gelu_apprx_tanh, [pysn], [y2sn])
                                    if o >= 1:
                                        y_transposes(o - 1)
                                y_transposes(15)
                    c.barrier()
                    if (stop == 2 and not is_main) or (stop == 4 and is_main):
                        c.dead = True
            c.barrier()

            outbT = sbt(st, "outbT", [128, 16, NT], BF16)
            with ExitStack() as m_:
                xnT = sbt(m_, "xnT", [128, 16, NT], BF16)
                tblocks = [(0, 512), (512, 512), (1024, 128)]
                with ExitStack() as g_:
                    gpa = phaseA_gen(g_, xm, 9, xnT)
                    Wb = [sbt(g_, "Wg%d" % i, [128, 16, 512], BF16) for i in range(2)]
                    bglu_t = sbt(g_, "bglu_t", [128, 16], F32)
                    sg = [sbt(g_, "sg%d" % i, [128, 512], BF16) for i in range(2)]
                    psG = [pst(g_, "psG%d" % i, [128, 512], F32) for i in range(4)]
                    c.dma('sp', 'ld_bglu', bglu_t[:], bglu[:, :], writes=['bglu'])
                    pi = 0
                    for ob in range(4):
                        Wt, wn = load_w(Wb, "Wg", w_glu[:, ob * 512:(ob + 1) * 512], 512)
                        for mo in range(4):
                            oc = ob * 4 + mo
                            for (t0, tn) in tblocks:
                                pg, pgn = psG[pi % 4], "psG%d" % (pi % 4)
                                sgt, sgn = sg[pi % 2], "sg%d" % (pi % 2)
                                pi += 1
                                for k in range(16):
                                    mm(pg[:, 0:tn], Wt[:, k, mo * 128:(mo + 1) * 128], ybT[:, k, t0:t0 + tn], k == 0, k == 15,
                                       ['%s_%s' % (wn, 'a' if k < 4 else 'b'), 'ybT'], [pgn])
                                act(sgt[:, 0:tn], pg[:, 0:tn], AF.Sigmoid, [pgn, 'bglu'], [sgn], bias=bglu_t[:, oc:oc + 1])
                                tt('dve', outbT[:, oc, t0:t0 + tn], sgt[:, 0:tn], ybT[:, oc, t0:t0 + tn], ALU.mult,
                                   [sgn, 'ybT'], ['outb'])
                                if pi % 4 == 0:
                                    next(gpa, None)
                    for _ in gpa:
                        pass
                    for ob in range(4):
                        Wt, wn = load_w(Wb, "Wg", w_in[:, 8192 + ob * 512:8192 + (ob + 1) * 512], 512)
                        for mo in range(4):
                            oc = ob * 4 + mo
                            for (t0, tn) in tblocks:
                                pg, pgn = psG[pi % 4], "psG%d" % (pi % 4)
                                sgt, sgn = sg[pi % 2], "sg%d" % (pi % 2)
                                pi += 1
                                for k in range(16):
                                    mm(pg[:, 0:tn], Wt[:, k, mo * 128:(mo + 1) * 128], xnT[:, k, t0:t0 + tn], k == 0, k == 15,
                                       ['%s_%s' % (wn, 'a' if k < 4 else 'b'), 'xnT'], [pgn])
                                act(sgt[:, 0:tn], pg[:, 0:tn], AF.Silu, [pgn], [sgn])
                                tt('dve', outbT[:, oc, t0:t0 + tn], sgt[:, 0:tn], outbT[:, oc, t0:t0 + tn], ALU.mult,
                                   [sgn, 'outb'], ['outb'])
                c.barrier()
                if stop == 5:
                    c.dead = True

                with ExitStack() as a_:
                    gvb_t = sbt(a_, "gvb_t", [128, D], F32)
                    wsTm = sbt(a_, "wsTm", [128, 8, 128], BF16)
                    wsSm = sbt(a_, "wsSm", [128, 8, 128], BF16)
                    bs_b = sbt(a_, "bs_b", [1, 8, 128], BF16)
                    bsS_b = sbt(a_, "bsS_b", [1, 8, 128], BF16)
                    c.dma('sp', 'ld_gvb', gvb_t[:], gvb[:, :], writes=['gvb'])
                    with ExitStack() as tmp_:
                        wtmp = sbt(tmp_, "wtmp", [128, 8, 128], F32)
                        mtmp = sbt(tmp_, "mtmp", [128, 128], F32)
                        bs_t = sbt(tmp_, "bs_t", [1, 8, 128], F32)
                        bsS_t = sbt(tmp_, "bsS_t", [1, 8, 128], F32)
                        c.dma('sp', 'ld_wsT', wtmp[:], wsT[:, :, :], writes=['wtmp'])
                        c.dma('sp', 'ld_msk', mtmp[:], mask_ts[:, :], writes=['mtmp'])
                        tt('dve', wsTm[:], wtmp[:], mtmp[:].unsqueeze(1).to_broadcast([128, 8, 128]), ALU.mult, ['wtmp', 'mtmp'], ['wsTm'])
                        c.dma('sp', 'ld_wsT', wtmp[:], wsS[:, :, :], reads=['wtmp'], writes=['wtmp'])
                        c.dma('sp', 'ld_msk', mtmp[:], mask_blk[:, :], reads=['mtmp'], writes=['mtmp'])
                        tt('dve', wsSm[:], wtmp[:], mtmp[:].unsqueeze(1).to_broadcast([128, 8, 128]), ALU.mult, ['wtmp', 'mtmp'], ['wsSm'])
                        c.dma('sp', 'ld_bs', bs_t[:], bsrow[:, :, :], writes=['bs_t'])
                        c.dma('sp', 'ld_bsS', bsS_t[:], bsSrow[:, :, :], writes=['bsS_t'])
                        cp('dve', bs_b[:], bs_t[:], ['bs_t'], ['bs_b'])
                        cp('dve', bsS_b[:], bsS_t[:], ['bsS_t'], ['bsS_b'])
                    c.barrier()
                    Wh = [sbt(a_, "Wh%d" % i, [128, 16, 768], BF16) for i in range(2)]
                    uT = sbt(a_, "uT", [128, 2, NT], BF16)
                    gaT = sbt(a_, "gaT", [128, 2, NT], BF16)
                    vg9 = sbt(a_, "vg9", [128, 9, 256], F32)
                    junkv = sbt(a_, "junkv", [128, 256], BF16)
                    vnb = sbt(a_, "vnb", [128, 9, 256], BF16)
                    vnsh = [sbt(a_, "vnsh%d" % i, [128, 256], F32) for i in range(2)]
                    ssv = sbt(a_, "ssv", [128, 80], F32)
                    rsv = sbt(a_, "rsv", [128, 80], F32)
                    psA_ = [pst(a_, "psAm%d" % i, [128, 512], F32) for i in range(4)]
                    psV = [pst(a_, "psV%d" % i, [128, 512], F32) for i in range(2)]
                    psM_ = [pst(a_, "psMx%d" % i, [128, 512], F32) for i in range(2)]
                    memset('dve', ssv[:], 0.0, ['ssv'])
                    pi = 0
                    for h in range(8):
                        i = wslot[0] % 2
                        wslot[0] += 1
                        Wt, wn = Wh[i], "Wh%d" % i
                        for j, c0 in ((1, 2048 + h * 256), (0, h * 256), (2, 4096 + h * 256)):
                            srcv = w_in[:, c0:c0 + 256].rearrange("(k p) n -> p k n", p=128)
                            for kq in range(4):
                                grp = ('a' if kq == 0 else 'b') if j == 1 else ('c' if j == 0 else 'd')
                                c.dma('pool', 'w_%s_%s' % (wn, grp), Wt[:, kq * 4:(kq + 1) * 4, j * 256:(j + 1) * 256],
                                      srcv[:, kq * 4:(kq + 1) * 4, :], writes=['%s_%s' % (wn, grp)],
                                      nowait=((j == 1 and kq > 1) or (j != 1 and kq > 0)))
                        for t in range(9):
                            pv, pvn = psV[t % 2], "psV%d" % (t % 2)
                            for k in range(16):
                                mm(pv[:, 0:256], xnT[:, k, t * 128:(t + 1) * 128], Wt[:, k, 256:512], k == 0, k == 15, ['xnT', '%s_%s' % (wn, 'a' if k < 4 else 'b')], [pvn])
                            act(vg9[:, t, :], pv[:, 0:256], AF.Gelu_apprx_tanh, [pvn], ['vg9'])
                        for t in range(9):
                            col = h * 9 + t
                            act(junkv[:], vg9[:, t, :], AF.Square, ['vg9'], ['junkv', 'ssv'], accum_out=ssv[:, col:col + 1])
                        cs = slice(h * 9, h * 9 + 9)
                        ts('dve', rsv[:, cs], ssv[:, cs], 1.0 / 256, EPS, ALU.mult, ALU.add, ['ssv'], ['rsv'])
                        act(rsv[:, cs], rsv[:, cs], AF.Sqrt, ['rsv'], ['rsv'])
                        c.op('dve', lambda: V.reciprocal(out=rsv[:, cs], in_=rsv[:, cs]), ['rsv'], ['rsv'])
                        for t in range(9):
                            col = h * 9 + t
                            stt(vnb[:, t, :], vg9[:, t, :], rsv[:, col:col + 1], gvb_t[:, h * 256:(h + 1) * 256], ALU.mult, ALU.mult,
                                ['vg9', 'rsv', 'gvb'], ['vnb'])
                            if t == 8:
                                vsn = "vnsh%d" % (h % 2)
                                stt(vnsh[h % 2][:], vg9[:, t, :], rsv[:, col:col + 1], gvb_t[:, h * 256:(h + 1) * 256],
                                    ALU.mult, ALU.mult, ['vg9', 'rsv', 'gvb'], [vsn])
                                c.dma('sp', 'st_' + vsn, vns_out[:, h * 256:(h + 1) * 256], vnsh[h % 2][:], reads=[vsn])
                        for (j, dst, dn, fn) in ((0, uT, 'uT', AF.Gelu_apprx_tanh), (2, gaT, 'gaT', AF.Silu)):
                            for mo in range(2):
                                for (t0, tn) in tblocks:
                                    pg, pgn = psA_[pi % 4], "psAm%d" % (pi % 4)
                                    pi += 1
                                    for k in range(16):
                                        mm(pg[:, 0:tn], Wt[:, k, j * 256 + mo * 128:j * 256 + (mo + 1) * 128], xnT[:, k, t0:t0 + tn],
                                           k == 0, k == 15, ['%s_%s' % (wn, 'c' if j == 0 else 'd'), 'xnT'], [pgn])
                                    act(dst[:, mo, t0:t0 + tn], pg[:, 0:tn], fn, [pgn], [dn])
                                    if j == 2:
                                        tt('pool', uT[:, mo, t0:t0 + tn], uT[:, mo, t0:t0 + tn], gaT[:, mo, t0:t0 + tn], ALU.mult,
                                           ['uT', 'gaT'], ['uT'])
                        for mo in range(2):
                            for tb in range(3):
                                tiles = [0, 1, 2, 3] if tb == 0 else ([4, 5, 6, 7] if tb == 1 else [8])
                                pm, pmn = psM_[(mo * 3 + tb) % 2], "psMx%d" % ((mo * 3 + tb) % 2)
                                for ti, t in enumerate(tiles):
                                    wsm = wsTm if t < 8 else wsSm
                                    bsb = bs_b if t < 8 else bsS_b
                                    o_ = pm[:, ti * 128:(ti + 1) * 128]
                                    mm(o_, vnb[:, t, mo * 128:(mo + 1) * 128], wsm[:, h, :], True, False, ['vnb', 'wsTm', 'wsSm'], [pmn])
                                    mm(o_, ones1[0:1, :], bsb[0:1, h, :], False, True, ['ones1', 'bs_b', 'bsS_b'], [pmn])
                                t0 = tiles[0] * 128
                                tn = len(tiles) * 128
                                tt('dve', ybT[:, h * 2 + mo, t0:t0 + tn], pm[:, 0:tn], uT[:, mo, t0:t0 + tn], ALU.mult,
                                   [pmn, 'uT'], ['outa'])
                c.barrier()
            c.barrier()

            if stop == 6:
                c.dead = True
            with ExitStack() as o_s:
                Wo = [sbt(o_s, "Wo%d" % i, [128, 32, 512], BF16) for i in range(2)]
                gfb_t = sbt(o_s, "gfb_t", [128, D], F32)
                xnew = sbt(o_s, "xnew", [128, 5, D], F32)
                junko = sbt(o_s, "junko", [128, D], BF16)
                sso = sbt(o_s, "sso", [128, 16], F32)
                rso = sbt(o_s, "rso", [128, 16], F32)
                psO = [pst(o_s, "psO%d" % i, [128, 512], F32) for i in range(4)]
                c.dma('sp', 'ld_gfb', gfb_t[:], gfb[:, :], writes=['gfb'])
                memset('dve', sso[:], 0.0, ['sso'])
                pi = 0
                for (tiles) in ([0, 1, 2, 3, 4], [5, 6, 7, 8]):
                    for tl, t in enumerate(tiles):
                        c.dma('sp', 'ld_xr%d' % tl, xnew[:, tl, :], xm[t * 128:(t + 1) * 128, :], reads=['xnew%d' % tl], writes=['xnew%d' % tl])
                    for cb in range(4):
                        i = wslot[0] % 2
                        wslot[0] += 1
                        Wt, wn = Wo[i], "Wo%d" % i
                        srcv = w_out[:, cb * 512:(cb + 1) * 512].rearrange("(k p) n -> p k n", p=128)
                        for kq in range(8):
                            grp = 'a' if kq == 0 else 'b'
                            c.dma('pool', 'w_%s_%s' % (wn, grp), Wt[:, kq * 4:(kq + 1) * 4, :], srcv[:, kq * 4:(kq + 1) * 4, :],
                                  writes=['%s_%s' % (wn, grp)], nowait=(kq > 1))
                        for tl, t in enumerate(tiles):
                            po, pon = psO[pi % 4], "psO%d" % (pi % 4)
                            pi += 1
                            for k in range(32):
                                mm(po[:, :], (ybT if k < 16 else outbT)[:, k % 16, t * 128:(t + 1) * 128], Wt[:, k, :], k == 0, k == 31, ['mixT', '%s_%s' % (wn, 'a' if k < 4 else 'b')], [pon])
                            tt('dve', xnew[:, tl, cb * 512:(cb + 1) * 512], po[:, :], xnew[:, tl, cb * 512:(cb + 1) * 512], ALU.add,
                               [pon, 'xnew%d' % tl], ['xnew%d' % tl])
                    for tl, t in enumerate(tiles):
                        xn_ = 'xnew%d' % tl
                        act(junko[:], xnew[:, tl, :], AF.Square, [xn_], ['junko', 'sso'], accum_out=sso[:, t:t + 1])
                        ts('dve', rso[:, t:t + 1], sso[:, t:t + 1], 1.0 / D, EPS, ALU.mult, ALU.add, ['sso'], ['rso'])
                        act(rso[:, t:t + 1], rso[:, t:t + 1], AF.Sqrt, ['rso'], ['rso'])
                        c.op('dve', lambda: V.reciprocal(out=rso[:, t:t + 1], in_=rso[:, t:t + 1]), ['rso'], ['rso'])
                        stt(xnew[:, tl, :], xnew[:, tl, :], rso[:, t:t + 1], gfb_t[:], ALU.mult, ALU.mult, [xn_, 'rso', 'gfb'], [xn_])
                        c.dma('sp', 'st_y%d' % tl, y_out[t * 128:(t + 1) * 128, :], xnew[:, tl, :], reads=[xn_], writes=[])
        except _StopBuild:
            pass
        for k, s in c.sems.items():
            if k in ('pe', 'act', 'dve', 'pool'):
                continue
            if c.cnt[k] > 0:
                nc.sync.wait_ge(s, c.cnt[k])
    return nc


def _ls(a):
    return np.ascontiguousarray(a.reshape(64, 2, 64).transpose(1, 2, 0).reshape(128, 64))


def make_in_maps(x_prompt, x_sample, state_ssm_re, state_ssm_im, g_norm, w_in, g_v, w_s, b_s,
                 a_re, a_im, log_dt, b_re, b_im, c_re, c_im, d_skip, w_glu, b_glu, w_out, g_final):
    f = np.float32
    x_prompt = np.asarray(x_prompt, f)
    x_sample = np.asarray(x_sample, f)
    shared = {}
    shared["w_in"] = np.ascontiguousarray(np.asarray(w_in, f)[0])
    shared["w_glu"] = np.ascontiguousarray(np.asarray(w_glu, f)[0])
    shared["w_out"] = np.ascontiguousarray(np.asarray(w_out, f)[0])
    shared["gnb"] = np.ascontiguousarray(np.broadcast_to(np.asarray(g_norm, f)[0][None, :], (128, D)))
    shared["gncol"] = np.ascontiguousarray(np.asarray(g_norm, f)[0].reshape(16, 128).T)
    shared["gvb"] = np.ascontiguousarray(np.broadcast_to(np.asarray(g_v, f)[0][None, :], (128, D)))
    shared["gfb"] = np.ascontiguousarray(np.broadcast_to(np.asarray(g_final, f)[None, :], (128, D)))
    shared["bglu"] = np.ascontiguousarray(np.asarray(b_glu, f)[0].reshape(16, 128).T)
    ws = np.asarray(w_s, f)[0]
    shared["wsT"] = np.ascontiguousarray(ws.transpose(2, 0, 1))
    wss = np.zeros((16, 8, 8, 16, 8), f)
    w8 = ws[:, :8, :8]
    for s in range(16):
        wss[s, :, :, s, :] = w8.transpose(2, 0, 1)
    shared["wsS"] = np.ascontiguousarray(wss.reshape(128, 8, 128))
    shared["mask_ts"] = np.triu(np.ones((128, 128), f))
    mb = np.zeros((16, 8, 16, 8), f)
    for s in range(16):
        mb[s, :, s, :] = np.triu(np.ones((8, 8), f))
    shared["mask_blk"] = mb.reshape(128, 128)
    m16 = np.zeros((16, 8, 8, 16), f)
    pI = np.zeros((16, 8, 8, 16), f)
    for i in range(8):
        m16[:, i, i:, :] = 1.0
        for cc in range(16):
            pI[cc, i, i, cc] = 1.0
    shared["mask16"] = np.ascontiguousarray(m16.reshape(128, 128))
    shared["permI"] = np.ascontiguousarray(pI.reshape(128, 128))
    bs = np.asarray(b_s, f)[0]
    shared["bsrow"] = np.ascontiguousarray(bs[None, :, :])
    shared["bsSrow"] = np.ascontiguousarray(np.tile(bs[:, :8], (1, 16))[None, :, :])
    shared["are"] = _ls(np.asarray(a_re, f)[0])
    shared["aim"] = _ls(np.asarray(a_im, f)[0])
    shared["ldt"] = _ls(np.broadcast_to(np.asarray(log_dt, f)[0][:, None], (128, 64)))

    def _lb(a):
        return np.ascontiguousarray(a.reshape(64, 2, 64, 16).transpose(1, 2, 0, 3).reshape(128, 64, 16))
    shared["bre"] = _lb(np.asarray(b_re, f)[0])
    shared["bim"] = _lb(np.asarray(b_im, f)[0])
    shared["cre"] = _lb(np.asarray(c_re, f)[0].transpose(0, 2, 1))
    shared["cim"] = _lb(np.asarray(c_im, f)[0].transpose(0, 2, 1))
    shared["dcol"] = np.ascontiguousarray(np.repeat(np.asarray(d_skip, f)[0].reshape(128, 16).T, 8, axis=0))
    qv = np.zeros((128, 24, 64), f)
    for i in range(8):
        qv[:, i, :] = 7 - i
    for s in range(16):
        qv[:, 8 + s, :] = s - 7
    shared["qv"] = qv
    shared["identf_in"] = np.eye(128, dtype=f)

    sre = np.asarray(state_ssm_re, f)[0]
    sim = np.asarray(state_ssm_im, f)[0]
    in_maps = []
    for cid in range(NCORES):
        sq, half = cid // 2, cid % 2
        m = dict(shared)
        xs = x_sample[cid * 16:(cid + 1) * 16].reshape(128, D)
        m["xm"] = np.ascontiguousarray(np.concatenate([x_prompt[sq, half * NPR:(half + 1) * NPR], xs], axis=0))
        m["xp"] = np.ascontiguousarray(x_prompt[sq, 0:NPR]) if half == 1 else np.zeros((NPR, D), f)
        hh = np.stack([sre[cid * 16:(cid + 1) * 16], sim[cid * 16:(cid + 1) * 16]], axis=0)
        hh = hh.reshape(2, 16, 64, 2, 64).transpose(3, 4, 2, 0, 1)
        m["h0"] = np.ascontiguousarray(hh.reshape(128, 64, 2, 16))
        in_maps.append(m)
    return in_maps


def assemble(R):
    f = np.float32
    y_prompt = np.zeros((4, 2048, D), f)
    y_sample = np.zeros((128, 8, D), f)
    re_p = np.zeros((1, 4, 128, 64), f)
    im_p = np.zeros((1, 4, 128, 64), f)
    re_s = np.zeros((1, 128, 128, 64), f)
    im_s = np.zeros((1, 128, 128, 64), f)
    v_s = np.zeros((1, 128, 8, D), f)
    for cid in range(NCORES):
        sq, half = cid // 2, cid % 2
        y = R[cid]["y"]
        y_prompt[sq, half * NPR:(half + 1) * NPR] = y[:NPR]
        y_sample[cid * 16:(cid + 1) * 16] = y[NPR:].reshape(16, 8, D)
        if half == 1:
            hp = R[cid]["hp"].reshape(2, 64, 64, 2).transpose(2, 0, 1, 3).reshape(128, 64, 2)
            re_p[0, sq] = hp[..., 0]
            im_p[0, sq] = hp[..., 1]
        hs = R[cid]["hs"].reshape(2, 64, 64, 2, 16).transpose(4, 3, 2, 0, 1).reshape(16, 2, 128, 64)
        re_s[0, cid * 16:(cid + 1) * 16] = hs[:, 0]
        im_s[0, cid * 16:(cid + 1) * 16] = hs[:, 1]
        v_s[0, cid * 16:(cid + 1) * 16] = R[cid]["vns"].reshape(16, 8, D)
    return (y_prompt, y_sample, re_p, im_p, re_s, im_s, v_s)


def kernel(**inputs):
    in_maps = make_in_maps(**inputs)
    nc = build_nc()
    res = run_bass_kernel_spmd(nc, in_maps, core_ids=list(range(NCORES)))
    return assemble(res.results)
```

```python
import math
from contextlib import ExitStack
import numpy as np
import concourse.bass as bass
import concourse.mybir as mybir
from concourse.bass_utils import run_bass_kernel_spmd

F32 = mybir.dt.float32
BF16 = mybir.dt.bfloat16
I32 = mybir.dt.int32
ALU = mybir.AluOpType
AF = mybir.ActivationFunctionType

NCORES = 8
D = 2048
NT = 1152
NPR = 1024
EPS = 1e-6
TWO_PI = 2.0 * math.pi


class Ctx:
    def __init__(self, nc, stack):
        self.nc = nc
        self.stack = stack
        self.eng = {'pe': nc.tensor, 'act': nc.scalar, 'dve': nc.vector, 'pool': nc.gpsimd, 'sp': nc.sync}
        self.sems = {}
        self.cnt = {}
        self.seen = {e: {} for e in self.eng}
        self.last_w = {}
        self.readers = {}
        self.dead = False
        self.nops = 0
        self.max_ops = None
        for e in ['pe', 'act', 'dve', 'pool']:
            self.sems[e] = stack.enter_context(nc.semaphore("prog_" + e))
            self.cnt[e] = 0

    def chan(self, name):
        if name not in self.sems:
            self.sems[name] = self.stack.enter_context(self.nc.semaphore("ch_" + name))
            self.cnt[name] = 0
        return name

    def _deps(self, reads, writes):
        deps = []
        for r in reads:
            if r in self.last_w:
                deps.append(self.last_w[r])
        for w in writes:
            if w in self.last_w:
                deps.append(self.last_w[w])
            deps.extend(self.readers.get(w, []))
        return deps

    def _wait(self, e, deps):
        best = {}
        for (k, v) in deps:
            if e == 'pe' and k == 'pe':
                continue
            if v > best.get(k, 0):
                best[k] = v
        for k, v in best.items():
            if self.seen[e].get(k, 0) >= v:
                continue
            self.eng[e].wait_ge(self.sems[k], v)
            self.seen[e][k] = v

    def _record(self, key, val, reads, writes):
        for w in writes:
            self.last_w[w] = (key, val)
            self.readers[w] = []
        for r in reads:
            self.readers.setdefault(r, []).append((key, val))

    def _tick(self):
        self.nops += 1
        if self.max_ops is not None and self.nops > self.max_ops:
            self.dead = True
        return self.dead

    def op(self, e, fn, reads=(), writes=()):
        if self._tick():
            return None
        self._wait(e, self._deps(reads, writes))
        ins = fn()
        self.cnt[e] += 1
        ins.then_inc(self.sems[e], 1)
        self._record(e, self.cnt[e], reads, writes)
        return ins

    def dma(self, q, ch, out, in_, reads=(), writes=(), nowait=False):
        if self._tick():
            return None
        ch = self.chan(ch)
        if not nowait:
            self._wait(q, self._deps(reads, writes))
        ins = self.eng[q].dma_start(out=out, in_=in_)
        self.cnt[ch] += 16
        ins.then_inc(self.sems[ch], 16)
        self._record(ch, self.cnt[ch], reads, writes)
        return ins

    def fence(self, e, include_self=False):
        if self.dead:
            return
        for k, sm in self.sems.items():
            v = self.cnt[k]
            if v > 0 and self.seen[e].get(k, 0) < v and (include_self or k != e):
                self.eng[e].wait_ge(sm, v)
                self.seen[e][k] = v

    def barrier(self):
        if self.dead:
            return
        for e in self.eng:
            for k, s in self.sems.items():
                v = self.cnt[k]
                if v > 0 and self.seen[e].get(k, 0) < v:
                    self.eng[e].wait_ge(s, v)
                    self.seen[e][k] = v
        self.last_w = {}
        self.readers = {}


class _StopBuild(Exception):
    pass


def build_nc(stop=99, max_ops=None):
    nc = bass.Bass("TRN2", target_bir_lowering=False)

    def din(name, shape, dt=F32):
        return nc.dram_tensor(name, shape, dt, kind="ExternalInput").ap()

    def dout(name, shape, dt=F32):
        return nc.dram_tensor(name, shape, dt, kind="ExternalOutput").ap()

    xm = din("xm", [NT, D])
    xp = din("xp", [NPR, D])
    h0 = din("h0", [128, 64, 2, 16])
    w_in = din("w_in", [D, 10240])
    w_glu = din("w_glu", [D, D])
    w_out = din("w_out", [2 * D, D])
    gnb = din("gnb", [128, D])
    gncol = din("gncol", [128, 16])
    gvb = din("gvb", [128, D])
    gfb = din("gfb", [128, D])
    bglu = din("bglu", [128, 16])
    wsT = din("wsT", [128, 8, 128])
    wsS = din("wsS", [128, 8, 128])
    mask_ts = din("mask_ts", [128, 128])
    mask_blk = din("mask_blk", [128, 128])
    mask16 = din("mask16", [128, 128])
    bsrow = din("bsrow", [1, 8, 128])
    bsSrow = din("bsSrow", [1, 8, 128])
    are = din("are", [128, 64])
    aim = din("aim", [128, 64])
    ldt = din("ldt", [128, 64])
    bre = din("bre", [128, 64, 16])
    bim = din("bim", [128, 64, 16])
    cre = din("cre", [128, 64, 16])
    cim = din("cim", [128, 64, 16])
    dcol = din("dcol", [128, 128])
    qv = din("qv", [128, 24, 64])
    identf_in = din("identf_in", [128, 128])
    permI = din("permI", [128, 128])

    y_out = dout("y", [NT, D])
    hp_out = dout("hp", [128, 64, 2])
    hs_out = dout("hs", [128, 64, 2, 16])
    vns_out = dout("vns", [128, D])

    s_bptz = nc.dram_tensor("s_bptz", [128, 128, 2, 128], BF16).ap()
    s_cq1 = nc.dram_tensor("s_cq1", [128, 64, 2, 128], BF16).ap()
    s_msup = nc.dram_tensor("s_msup", [128, 128, 128], BF16).ap()

    with ExitStack() as st:
        c = Ctx(nc, st)
        c.max_ops = max_ops
        V, A, P, T = nc.vector, nc.scalar, nc.gpsimd, nc.tensor
        try:

            uid = [0]

            def sbt(stack, name, shape, dt):
                uid[0] += 1
                return stack.enter_context(nc.sbuf_tensor("%s_%d" % (name, uid[0]), shape, dt))

            def pst(stack, name, shape, dt):
                uid[0] += 1
                return stack.enter_context(nc.psum_tensor("%s_%d" % (name, uid[0]), shape, dt))

            def tt(e, out, in0, in1, op, R, W):
                eng = {'dve': V, 'pool': P}[e]
                return c.op(e, lambda: eng.tensor_tensor(out=out, in0=in0, in1=in1, op=op), R, W)

            def ts(e, out, in0, s1, s2, op0, op1, R, W):
                eng = {'dve': V, 'pool': P}[e]
                if op1 is None:
                    return c.op(e, lambda: eng.tensor_scalar(out=out, in0=in0, scalar1=s1, scalar2=None, op0=op0), R, W)
                return c.op(e, lambda: eng.tensor_scalar(out=out, in0=in0, scalar1=s1, scalar2=s2, op0=op0, op1=op1), R, W)

            def stt(out, in0, scalar, in1, op0, op1, R, W):
                return c.op('dve', lambda: V.scalar_tensor_tensor(out=out, in0=in0, scalar=scalar, in1=in1, op0=op0, op1=op1), R, W)

            def act(out, in_, func, R, W, **kw):
                return c.op('act', lambda: A.activation(out=out, in_=in_, func=func, **kw), R, W)

            def cp(e, out, in_, R, W):
                if e == 'act':
                    return c.op('act', lambda: A.copy(out=out, in_=in_), R, W)
                eng = {'dve': V, 'pool': P}[e]
                return c.op(e, lambda: eng.tensor_copy(out=out, in_=in_), R, W)

            def mm(out, lhsT, rhs, start, stop, R, W):
                return c.op('pe', lambda: T.matmul(out, lhsT=lhsT, rhs=rhs, start=start, stop=stop), R, W)

            def tr(out, in_, ident, R, W):
                return c.op('pe', lambda: T.transpose(out=out, in_=in_, identity=ident), R, W)

            def memset(e, ap, val, W):
                eng = {'dve': V, 'pool': P}[e]
                return c.op(e, lambda: eng.memset(ap, val), (), W)

            identf = sbt(st, "identf", [128, 128], F32)
            identb = sbt(st, "identb", [128, 128], BF16)
            A8 = sbt(st, "A8", [128, 2, 64], F32)
            Hmid = sbt(st, "Hmid", [128, 64, 2], F32)
            ones1 = sbt(st, "ones1", [1, 128], BF16)
            W8 = sbt(st, "W8", [128, 64, 4], F32)

            def sbt_r(stack, name, shape, dt):
                uid[0] += 1
                return stack.enter_context(nc.sbuf_tensor("%s_%d" % (name, uid[0]), shape, dt, side="right"))

            c.dma('sp', 'ldc', identf[:], identf_in[:, :], writes=['identf'])
            cp('dve', identb[:], identf[:], ['identf'], ['identb'])
            memset('dve', ones1[:], 1.0, ['ones1'])

            wslot = [0]

            def load_w(Wbufs, wname, src_ap, ncols_total_view):
                i = wslot[0] % len(Wbufs)
                wslot[0] += 1
                nm = "%s%d" % (wname, i)
                srcv = src_ap.rearrange("(k p) n -> p k n", p=128)
                for kq in range(4):
                    grp = 'a' if kq == 0 else 'b'
                    c.dma('pool', 'w_%s_%s' % (nm, grp), Wbufs[i][:, kq * 4:(kq + 1) * 4, 0:ncols_total_view], srcv[:, kq * 4:(kq + 1) * 4, :],
                          writes=['%s_%s' % (nm, grp)], nowait=(kq > 1))
                return Wbufs[i], nm

            def phaseA_gen(pa, x_dram, ntiles, xnT):
                gnb_t = sbt(pa, "gnb_tg", [128, D], F32)
                xl = [sbt(pa, "xlg%d" % i, [128, D], F32) for i in range(2)]
                xnb = [sbt(pa, "xnbg%d" % i, [128, D], BF16) for i in range(2)]
                ss = sbt(pa, "ssg", [128, 16], F32)
                rs = sbt(pa, "rsg", [128, 16], F32)
                psA = [pst(pa, "psAg%d" % i, [128, 1024], BF16) for i in range(4)]
                epsg = sbt(pa, "epsg", [128, 1], F32)
                c.dma('sp', 'ld_gnbg', gnb_t[:], gnb[:, :], writes=['gnbg'])
                memset('dve', ss[:], 0.0, ['ssg'])
                memset('dve', epsg[:], EPS, ['epsg'])

                def tposes(t):
                    b = t % 2
                    xbn = "xnbg%d" % b
                    for hb in range(2):
                        pa_ = psA[(t % 2) * 2 + hb]
                        pan = "psAg%d" % ((t % 2) * 2 + hb)
                        for k in range(8):
                            kk = hb * 8 + k
                            tr(pa_[:, k * 128:(k + 1) * 128], xnb[b][:, kk * 128:(kk + 1) * 128], identb[:, :],
                               [xbn, 'identb'], [pan])
                        cp('act' if hb == 0 else 'dve', xnT[:, hb * 8:(hb + 1) * 8, t * 128:(t + 1) * 128],
                           pa_[:].rearrange("p (k n) -> p k n", k=8), [pan], ['xnT'])

                for t in range(ntiles):
                    b = t % 2
                    xn_, xbn = "xlg%d" % b, "xnbg%d" % b
                    c.dma('sp', 'ld_' + xn_, xl[b][:], x_dram[t * 128:(t + 1) * 128, :], writes=[xn_])
                    act(xnb[b][:], xl[b][:], AF.Square, [xn_], [xbn, 'ssg'], accum_out=ss[:, t:t + 1])
                    act(rs[:, t:t + 1], ss[:, t:t + 1], AF.Ln, ['ssg', 'epsg'], ['rsg'], scale=1.0 / D, bias=epsg[:, 0:1])
                    act(rs[:, t:t + 1], rs[:, t:t + 1], AF.Exp, ['rsg'], ['rsg'], scale=-0.5)
                    stt(xnb[b][:], xl[b][:], rs[:, t:t + 1], gnb_t[:], ALU.mult, ALU.mult, [xn_, 'rsg', 'gnbg'], [xbn])
                    if t >= 1:
                        tposes(t - 1)
                    yield
                tposes(ntiles - 1)
                yield

            def phaseA(stack_parent, x_dram, ntiles, xnT):
                with ExitStack() as pa:
                    gnb_t = sbt(pa, "gnb_t", [128, D], F32)
                    xl = [sbt(pa, "xl%d" % i, [128, D], F32) for i in range(2)]
                    junk = sbt(pa, "junkA", [128, D], BF16)
                    xnb = [sbt(pa, "xnb%d" % i, [128, D], BF16) for i in range(2)]
                    ss = sbt(pa, "ssA", [128, 16], F32)
                    rs = sbt(pa, "rsA", [128, 16], F32)
                    psA = [pst(pa, "psA%d" % i, [128, 1024], BF16) for i in range(4)]
                    c.dma('sp', 'ld_gnb', gnb_t[:], gnb[:, :], writes=['gnb'])
                    memset('dve', ss[:], 0.0, ['ssA'])
                    for t in range(ntiles):
                        b = t % 2
                        xn_, xbn = "xl%d" % b, "xnb%d" % b
                        c.dma('sp', 'ld_' + xn_, xl[b][:], x_dram[t * 128:(t + 1) * 128, :], writes=[xn_])
                        act(junk[:], xl[b][:], AF.Square, [xn_], ['junkA', 'ssA'], accum_out=ss[:, t:t + 1])
                        ts('dve', rs[:, t:t + 1], ss[:, t:t + 1], 1.0 / D, EPS, ALU.mult, ALU.add, ['ssA'], ['rsA'])
                        act(rs[:, t:t + 1], rs[:, t:t + 1], AF.Sqrt, ['rsA'], ['rsA'])
                        c.op('dve', lambda: V.reciprocal(out=rs[:, t:t + 1], in_=rs[:, t:t + 1]), ['rsA'], ['rsA'])
                        stt(xnb[b][:], xl[b][:], rs[:, t:t + 1], gnb_t[:], ALU.mult, ALU.mult, [xn_, 'rsA', 'gnb'], [xbn])
                        for hb in range(2):
                            pa_ = psA[(t % 2) * 2 + hb]
                            pan = "psA%d" % ((t % 2) * 2 + hb)
                            for k in range(8):
                                kk = hb * 8 + k
                                tr(pa_[:, k * 128:(k + 1) * 128], xnb[b][:, kk * 128:(kk + 1) * 128], identb[:, :],
                                   [xbn, 'identb'], [pan])
                            cp('act' if hb == 0 else 'dve', xnT[:, hb * 8:(hb + 1) * 8, t * 128:(t + 1) * 128],
                               pa_[:].rearrange("p (k n) -> p k n", k=8), [pan], ['xnT'])
                c.barrier()

            pfx = ExitStack()
            X2p = sbt_r(pfx, "X2p", [128, D, 8], BF16)
            with ExitStack() as p0:
                LR = sbt(p0, "LR", [128, 24, 64], F32)
                LI = sbt(p0, "LI", [128, 24, 64], F32)
                Bbr = sbt(p0, "Bbr", [128, 64, 16], F32)
                Bbi = sbt(p0, "Bbi", [128, 64, 16], F32)
                cre_t = sbt(p0, "cre_t", [128, 64, 16], F32)
                cim_t = sbt(p0, "cim_t", [128, 64, 16], F32)
                dcol_t = sbt(p0, "dcol_t", [128, 128], F32)
                m16_t = sbt(p0, "m16_t", [128, 128], F32)
                permI_t = sbt(p0, "permI_t", [128, 128], F32)
                c.dma('sp', 'ld_permI', permI_t[:], permI[:, :], writes=['permI'])
                c.dma('sp', 'ld_dcol', dcol_t[:], dcol[:, :], writes=['dcol'])
                c.dma('sp', 'ld_m16', m16_t[:], mask16[:, :], writes=['m16'])
                c.dma('sp', 'ld_cre', cre_t[:], cre[:, :, :], writes=['cre'])
                c.dma('sp', 'ld_cim', cim_t[:], cim[:, :, :], writes=['cim'])
                with ExitStack() as pre:
                    are_t = sbt(pre, "are_t", [128, 64], F32)
                    aim_t = sbt(pre, "aim_t", [128, 64], F32)
                    ldt_t = sbt(pre, "ldt_t", [128, 64], F32)
                    qv_t = sbt(pre, "qv_t", [128, 24, 64], F32)
                    bre_t = sbt(pre, "bre_t", [128, 64, 16], F32)
                    bim_t = sbt(pre, "bim_t", [128, 64, 16], F32)
                    for (tl, src, nm) in [(are_t, are, 'are'), (aim_t, aim, 'aim'), (ldt_t, ldt, 'ldt')]:
                        c.dma('sp', 'ld_' + nm, tl[:], src[:, :], writes=[nm])
                    c.dma('sp', 'ld_qv', qv_t[:], qv[:, :, :], writes=['qv'])
                    for (tl, src, nm) in [(bre_t, bre, 'bre'), (bim_t, bim, 'bim')]:
                        c.dma('sp', 'ld_' + nm, tl[:], src[:, :, :], writes=[nm])
                    dt_t = sbt(pre, "dt_t", [128, 64], F32)
                    er_t = sbt(pre, "er_t", [128, 64], F32)
                    et_t = sbt(pre, "et_t", [128, 64], F32)
                    act(dt_t[:], ldt_t[:], AF.Exp, ['ldt'], ['dt'])
                    tt('dve', er_t[:], are_t[:], dt_t[:], ALU.mult, ['are', 'dt'], ['er'])
                    tt('dve', et_t[:], aim_t[:], dt_t[:], ALU.mult, ['aim', 'dt'], ['et'])
                    ts('dve', et_t[:], et_t[:], 1.0 / TWO_PI, None, ALU.mult, None, ['et'], ['et'])
                    MAG = sbt(pre, "MAG", [128, 24, 64], F32)
                    TH = sbt(pre, "TH", [128, 24, 64], F32)
                    TIi = sbt(pre, "TIi", [128, 24, 64], I32)
                    TF = sbt(pre, "TF", [128, 24, 64], F32)
                    er_b = er_t[:].unsqueeze(1).to_broadcast([128, 24, 64])
                    et_b = et_t[:].unsqueeze(1).to_broadcast([128, 24, 64])
                    tt('dve', MAG[:], qv_t[:], er_b, ALU.mult, ['qv', 'er'], ['MAG'])
                    act(MAG[:], MAG[:], AF.Exp, ['MAG'], ['MAG'])
                    tt('dve', TH[:], qv_t[:], et_b, ALU.mult, ['qv', 'et'], ['TH'])
                    cp('dve', TIi[:], TH[:], ['TH'], ['TIi'])
                    cp('dve', TF[:], TIi[:], ['TIi'], ['TF'])
                    tt('dve', TF[:], TH[:], TF[:], ALU.subtract, ['TH', 'TF'], ['TF'])
                    act(LI[:], TF[:], AF.Sin, ['TF'], ['LI'], scale=TWO_PI)
                    TH2 = sbt(pre, "TH2", [128, 24, 64], F32)
                    TIi2 = sbt(pre, "TIi2", [128, 24, 64], I32)
                    TF2 = sbt(pre, "TF2", [128, 24, 64], F32)
                    ts('pool', TH2[:], TH[:], 0.25, None, ALU.add, None, ['TH'], ['TH2'])
                    cp('dve', TIi2[:], TH2[:], ['TH2'], ['TIi2'])
                    cp('dve', TF2[:], TIi2[:], ['TIi2'], ['TF2'])
                    tt('pool', TF2[:], TH2[:], TF2[:], ALU.subtract, ['TH2', 'TF2'], ['TF2'])
                    act(LR[:], TF2[:], AF.Sin, ['TF2'], ['LR'], scale=TWO_PI)
                    tt('dve', LR[:], LR[:], MAG[:], ALU.mult, ['LR', 'MAG'], ['LR'])
                    tt('dve', LI[:], LI[:], MAG[:], ALU.mult, ['LI', 'MAG'], ['LI'])
                    cp('dve', A8[:, 0, :], LR[:, 23, :], ['LR'], ['A8'])
                    cp('dve', A8[:, 1, :], LI[:, 23, :], ['LI'], ['A8'])
                    cp('dve', W8[:, :, 0], LR[:, 23, :], ['LR'], ['W8'])
                    cp('dve', W8[:, :, 3], LR[:, 23, :], ['LR'], ['W8'])
                    cp('dve', W8[:, :, 2], LI[:, 23, :], ['LI'], ['W8'])
                    ts('dve', W8[:, :, 1], LI[:, 23, :], -1.0, None, ALU.mult, None, ['LI'], ['W8'])
                    nr = sbt(pre, "nr", [128, 64], F32)
                    den = sbt(pre, "den", [128, 64], F32)
                    t_a = sbt(pre, "t_a", [128, 64], F32)
                    t_b = sbt(pre, "t_b", [128, 64], F32)
                    cr = sbt(pre, "cr", [128, 64], F32)
                    ci = sbt(pre, "ci", [128, 64], F32)
                    ni = LI[:, 16, :]
                    ts('dve', nr[:], LR[:, 16, :], -1.0, None, ALU.add, None, ['LR'], ['nr'])
                    tt('dve', den[:], are_t[:], are_t[:], ALU.mult, ['are'], ['den'])
                    tt('dve', t_a[:], aim_t[:], aim_t[:], ALU.mult, ['aim'], ['t_a'])
                    tt('dve', den[:], den[:], t_a[:], ALU.add, ['den', 't_a'], ['den'])
                    c.op('dve', lambda: V.reciprocal(out=den[:], in_=den[:]), ['den'], ['den'])
                    tt('dve', t_a[:], nr[:], are_t[:], ALU.mult, ['nr', 'are'], ['t_a'])
                    tt('dve', t_b[:], ni, aim_t[:], ALU.mult, ['LI', 'aim'], ['t_b'])
                    tt('dve', t_a[:], t_a[:], t_b[:], ALU.add, ['t_a', 't_b'], ['t_a'])
                    tt('dve', cr[:], t_a[:], den[:], ALU.mult, ['t_a', 'den'], ['cr'])
                    tt('dve', t_a[:], ni, are_t[:], ALU.mult, ['LI', 'are'], ['t_a'])
                    tt('dve', t_b[:], nr[:], aim_t[:], ALU.mult, ['nr', 'aim'], ['t_b'])
                    tt('dve', t_a[:], t_a[:], t_b[:], ALU.subtract, ['t_a', 't_b'], ['t_a'])
                    tt('dve', ci[:], t_a[:], den[:], ALU.mult, ['t_a', 'den'], ['ci'])
                    tq1 = sbt(pre, "tq1", [128, 64, 16], F32)
                    tq2 = sbt(pre, "tq2", [128, 64, 16], F32)
                    cr_b = cr[:].unsqueeze(2).to_broadcast([128, 64, 16])
                    ci_b = ci[:].unsqueeze(2).to_broadcast([128, 64, 16])
                    tt('dve', Bbr[:], bre_t[:], cr_b, ALU.mult, ['bre', 'cr'], ['Bbr'])
                    tt('pool', tq1[:], bim_t[:], ci_b, ALU.mult, ['bim', 'ci'], ['tq1'])
                    tt('dve', Bbr[:], Bbr[:], tq1[:], ALU.subtract, ['Bbr', 'tq1'], ['Bbr'])
                    tt('dve', Bbi[:], bim_t[:], cr_b, ALU.mult, ['bim', 'cr'], ['Bbi'])
                    tt('pool', tq2[:], bre_t[:], ci_b, ALU.mult, ['bre', 'ci'], ['tq2'])
                    tt('dve', Bbi[:], Bbi[:], tq2[:], ALU.add, ['Bbi', 'tq2'], ['Bbi'])
                c.barrier()

                BPr = [sbt(p0, "BPr%d" % i, [128, 4, 128], BF16) for i in range(2)]
                BPi = [sbt(p0, "BPi%d" % i, [128, 4, 128], BF16) for i in range(2)]
                CQr = [sbt(p0, "CQr%d" % i, [128, 4, 256], BF16) for i in range(2)]
                CQn = [sbt(p0, "CQn%d" % i, [128, 4, 256], BF16) for i in range(2)]
                Dd = sbt(p0, "Dd", [128, 8, 128], BF16)
                Msc = [sbt(p0, "Msc%d" % i, [128, 8, 128], BF16) for i in range(2)]
                BPTz = sbt(p0, "BPTz", [128, 8, 2, 128], BF16)
                TA = [sbt(p0, "TA%d" % i, [128, 4, 128], F32) for i in range(4)]
                TC = [sbt(p0, "TC%d" % i, [128, 4, 256], F32) for i in range(4)]
                psM = [pst(p0, "psM%d" % i, [128, 512], F32) for i in range(4)]
                psTb = [pst(p0, "psTb%d" % i, [128, 1024], BF16) for i in range(2)]
                memset('pool', BPTz[:], 0.0, ['BPTz'])

                def p0_chunk(ch):
                    p0_ = ch * 4
                    g0 = ch * 8
                    db = ch % 2
                    bpr, bprn = BPr[db], "BPr%d" % db
                    bpi, bpin = BPi[db], "BPi%d" % db
                    cqr, cqrn = CQr[db], "CQr%d" % db
                    cqn, cqnn = CQn[db], "CQn%d" % db
                    msc, mscn = Msc[db], "Msc%d" % db
                    LRb = LR[:, 0:8, p0_:p0_ + 4].rearrange("p i r -> p r i").unsqueeze(3).to_broadcast([128, 4, 8, 16])
                    LIb = LI[:, 0:8, p0_:p0_ + 4].rearrange("p i r -> p r i").unsqueeze(3).to_broadcast([128, 4, 8, 16])
                    Bbr_b = Bbr[:, p0_:p0_ + 4, :].unsqueeze(2).to_broadcast([128, 4, 8, 16])
                    Bbi_b = Bbi[:, p0_:p0_ + 4, :].unsqueeze(2).to_broadcast([128, 4, 8, 16])
                    tav = [TA[i][:].rearrange("p r (c i) -> p r i c", i=8) for i in range(4)]
                    tt('pool', tav[0], LRb, Bbr_b, ALU.mult, ['LR', 'Bbr'], ['TA0'])
                    tt('pool', tav[1], LIb, Bbi_b, ALU.mult, ['LI', 'Bbi'], ['TA1'])
                    tt('dve', bpr[:], TA[0][:], TA[1][:], ALU.subtract, ['TA0', 'TA1'], [bprn])
                    tt('pool', tav[2], LRb, Bbi_b, ALU.mult, ['LR', 'Bbi'], ['TA2'])
                    tt('pool', tav[3], LIb, Bbr_b, ALU.mult, ['LI', 'Bbr'], ['TA3'])
                    tt('dve', bpi[:], TA[2][:], TA[3][:], ALU.add, ['TA2', 'TA3'], [bpin])
                    LRc = LR[:, 8:24, p0_:p0_ + 4].rearrange("p s r -> p r s").unsqueeze(3).to_broadcast([128, 4, 16, 16])
                    LIc = LI[:, 8:24, p0_:p0_ + 4].rearrange("p s r -> p r s").unsqueeze(3).to_broadcast([128, 4, 16, 16])
                    cre_b = cre_t[:, p0_:p0_ + 4, :].unsqueeze(2).to_broadcast([128, 4, 16, 16])
                    cim_b = cim_t[:, p0_:p0_ + 4, :].unsqueeze(2).to_broadcast([128, 4, 16, 16])
                    tcv = [TC[i][:].rearrange("p r (s c) -> p r s c", s=16) for i in range(4)]
                    cqrv = cqr[:].rearrange("p r (s c) -> p r s c", s=16)
                    cqnv = cqn[:].rearrange("p r (s c) -> p r s c", s=16)
                    tt('pool', tcv[0], LRc, cre_b, ALU.mult, ['LR', 'cre'], ['TC0'])
                    tt('pool', tcv[1], LIc, cim_b, ALU.mult, ['LI', 'cim'], ['TC1'])
                    tt('dve', cqrv, tcv[0], tcv[1], ALU.subtract, ['TC0', 'TC1'], [cqrn])
                    tt('pool', tcv[2], LRc, cim_b, ALU.mult, ['LR', 'cim'], ['TC2'])
                    tt('dve', tcv[3], LIc, cre_b, ALU.mult, ['LI', 'cre'], ['TC3'])
                    stt(cqnv, tcv[2], -1.0, tcv[3], ALU.mult, ALU.subtract, ['TC2', 'TC3'], [cqnn])
                    for gi in range(8):
                        ts('dve', Dd[:, gi, :], permI_t[:, :], dcol_t[:, g0 + gi:g0 + gi + 1], None, ALU.mult, None, ['permI', 'dcol'], ['Dd'])
                    for gq in range(2):
                        pm = psM[db * 2 + gq]
                        pmn = "psM%d" % (db * 2 + gq)
                        for gl in range(4):
                            gi = gq * 4 + gl
                            pl, g2 = gi // 2, gi % 2
                            hs = slice(g2 * 64, g2 * 64 + 64)
                            o_ = pm[:, gl * 128:(gl + 1) * 128]
                            mm(o_, bpr[hs, pl, :], cqr[hs, pl, 0:128], True, False, [bprn, cqrn], [pmn])
                            mm(o_, bpi[hs, pl, :], cqn[hs, pl, 0:128], False, False, [bpin, cqnn], [pmn])
                            mm(o_, identb[:, :], Dd[:, gi, :], False, True, ['identb', 'Dd'], [pmn])
                    ptn_ = "psTb%d" % db
                    ptv = psTb[db][:, 0:1024].rearrange("p (a r q) -> p a r q", a=4, r=2)
                    for a in range(4):
                        tr(ptv[:, a, 0, :], bpr[:, a, :], identb[:, :], [bprn, 'identb'], [ptn_])
                        tr(ptv[:, a, 1, :], bpi[:, a, :], identb[:, :], [bpin, 'identb'], [ptn_])

                def p0_finish(ch):
                    p0_ = ch * 4
                    g0 = ch * 8
                    db = ch % 2
                    cqr, cqrn = CQr[db], "CQr%d" % db
                    cqn, cqnn = CQn[db], "CQn%d" % db
                    msc, mscn = Msc[db], "Msc%d" % db
                    for gq in range(2):
                        pm = psM[db * 2 + gq]
                        pmn = "psM%d" % (db * 2 + gq)
                        tt('dve', msc[:, gq * 4:(gq + 1) * 4, :], pm[:].rearrange("p (g n) -> p g n", g=4),
                           m16_t[:].unsqueeze(1).to_broadcast([128, 4, 128]), ALU.mult, [pmn, 'm16'], [mscn])
                    ptn_ = "psTb%d" % db
                    ptv = psTb[db][:, 0:1024].rearrange("p (a r q) -> p a r q", a=4, r=2)
                    bzv = BPTz[:].rearrange("p (a g2) r q -> p a g2 r q", g2=2)
                    for g2 in range(2):
                        cp('dve', bzv[:, :, g2, :, g2 * 64:(g2 + 1) * 64], ptv[:, :, :, g2 * 64:(g2 + 1) * 64], [ptn_], ['BPTz'])
                    c.dma('sp', 'st_BPTz', s_bptz[:, g0:g0 + 8, :, :], BPTz[:], reads=['BPTz'])
                    c.dma('sp', 'st_' + cqrn, s_cq1[:, p0_:p0_ + 4, 0, :], cqr[:, :, 128:256], reads=[cqrn])
                    c.dma('sp', 'st_' + cqnn, s_cq1[:, p0_:p0_ + 4, 1, :], cqn[:, :, 128:256], reads=[cqnn])
                    c.dma('sp', 'st_' + mscn, s_msup[:, g0:g0 + 8, :], msc[:], reads=[mscn])

                xnTp = sbt(p0, "xnTp", [128, 16, NPR], BF16)

                def prefix_front():
                    gcol = sbt(p0, "gcol", [128, 16], F32)
                    epsb = sbt(p0, "epsb", [128, 1], F32)
                    c.dma('act', 'ld_gcol', gcol[:], gncol[:, :], writes=['gcol'])
                    with ExitStack() as pa:
                        xl = [sbt(pa, "xlp%d" % i, [128, D], F32) for i in range(2)]
                        xnb = [sbt(pa, "xnbp%d" % i, [128, D], BF16) for i in range(2)]
                        ss = sbt(pa, "ssp", [128, 16], F32)
                        rs = sbt(pa, "rsp", [128, 16], F32)
                        psA = [pst(pa, "psAp%d" % i, [128, 1024], BF16) for i in range(2)]
                        c.op('act', lambda: A.activation(out=epsb[:], in_=gcol[:, 0:1], func=AF.Copy, scale=0.0, bias=EPS),
                             ['gcol'], ['epsb'])
                        c.op('act', lambda: A.activation(out=ss[:], in_=gcol[:, :], func=AF.Copy, scale=0.0), ['gcol'], ['ssp'])

                        def tposes(t):
                            b = t % 2
                            xbn = "xnbp%d" % b
                            for hb in range(2):
                                pa_, pan = psA[hb], "psAp%d" % hb
                                for k in range(8):
                                    kk = hb * 8 + k
                                    tr(pa_[:, k * 128:(k + 1) * 128], xnb[b][:, kk * 128:(kk + 1) * 128], identb[:, :],
                                       [xbn, 'identb'], [pan])
                                for k in range(8):
                                    kk = hb * 8 + k
                                    act(xnTp[:, kk, t * 128:(t + 1) * 128], pa_[:, k * 128:(k + 1) * 128], AF.Copy,
                                        [pan, 'gcol'], ['xnTp'], scale=gcol[:, kk:kk + 1])

                        c.dma('act', 'ld_xlp0', xl[0][:], xp[0:128, :], writes=['xlp0'])
                        for t in range(8):
                            b = t % 2
                            xn_, xbn = "xlp%d" % b, "xnbp%d" % b
                            if t + 1 < 8:
                                nb_ = (t + 1) % 2
                                c.dma('act', 'ld_xlp%d' % nb_, xl[nb_][:], xp[(t + 1) * 128:(t + 2) * 128, :], writes=['xlp%d' % nb_])
                            act(xnb[b][:], xl[b][:], AF.Square, [xn_], [xbn, 'ssp'], accum_out=ss[:, t:t + 1])
                            act(rs[:, t:t + 1], ss[:, t:t + 1], AF.Ln, ['ssp', 'epsb'], ['rsp'], scale=1.0 / D, bias=epsb[:, 0:1])
                            act(rs[:, t:t + 1], rs[:, t:t + 1], AF.Exp, ['rsp'], ['rsp'], scale=-0.5)
                            act(xnb[b][:], xl[b][:], AF.Copy, [xn_, 'rsp'], [xbn], scale=rs[:, t:t + 1])
                            if t >= 1:
                                tposes(t - 1)
                            yield
                        tposes(7)
                        yield
                    with ExitStack() as pb:
                        Wb = [sbt(pb, "Wxp%d" % i, [128, 16, 256], BF16) for i in range(2)]
                        Wst = [sbt(pb, "Wst%d" % i, [128, 4, 256], F32) for i in range(3)]
                        psX = [pst(pb, "psXp%d" % i, [128, 512], F32) for i in range(2)]
                        pi = 0

                        def issue_piece(p):
                            if p >= 32:
                                return
                            cb_, kq_ = p // 4, p % 4
                            srcv = w_in[:, 6144 + cb_ * 256:6144 + (cb_ + 1) * 256].rearrange("(k p) n -> p k n", p=128)
                            c.dma('act', 'ld_Wst%d' % (p % 3), Wst[p % 3][:], srcv[:, kq_ * 4:(kq_ + 1) * 4, :], writes=['Wst%d' % (p % 3)])

                        for p in range(3):
                            issue_piece(p)
                        for cb in range(8):
                            Wt, wn = Wb[cb % 2], "Wxp%d" % (cb % 2)
                            for kq in range(4):
                                p = cb * 4 + kq
                                act(Wt[:, kq * 4:(kq + 1) * 4, :], Wst[p % 3][:], AF.Copy, ['Wst%d' % (p % 3)], ['%s_%d' % (wn, kq)])
                                issue_piece(p + 3)
                            for i in range(8):
                                px, pxn = psX[pi % 2], "psXp%d" % (pi % 2)
                                pi += 1
                                for k in range(16):
                                    mm(px[:, 0:256], xnTp[:, k, i:NPR:8], Wt[:, k, :], k == 0, k == 15,
                                       ['xnTp', '%s_%d' % (wn, k // 4)], [pxn])
                                cp('act', X2p[:, cb * 256:(cb + 1) * 256, i], px[:, 0:256], [pxn], ['X2p'])
                                yield

                ga = prefix_front()
                for ch in range(16):
                    p0_chunk(ch)
                    for _ in range(3 if ch < 3 else 5):
                        next(ga, None)
                    if ch >= 1:
                        p0_finish(ch - 1)
                p0_finish(15)
                for _ in ga:
                    pass
            c.barrier()

            def scan_chain(BuH, Pst, side_gen=None):
                W8v = W8[:].rearrange("p n (d c) -> p n d c", d=2)
                c.barrier()
                if side_gen is not None:
                    cp('dve', Pst[1][:, :, :, 2], BuH[:, :, :, 1], ['BuHin'], ['Pst1b'])
                for m in range(1, 129):
                    if side_gen is not None:
                        next(side_gen, None)
                    Pt, ptn = Pst[m % 2], "Pst%d" % (m % 2)
                    Xp_ = BuH[:, :, :, m - 1]
                    Xc = BuH[:, :, :, m]
                    if side_gen is None:
                        cp('act', Pt[:, :, :, 2], Xc, ['BuHin'], [ptn + 'b'])
                    tt('dve', Pt[:, :, :, 0:2], Xp_.unsqueeze(2).to_broadcast([128, 64, 2, 2]), W8v, ALU.mult,
                       ['BuH', 'W8'], [ptn])
                    if side_gen is not None and m < 128:
                        Pn, pnn = Pst[(m + 1) % 2], "Pst%d" % ((m + 1) % 2)
                        cp('dve', Pn[:, :, :, 2], BuH[:, :, :, m + 1], ['BuHin'], [pnn + 'b'])
                    c.op('dve', lambda: V.tensor_reduce(out=Xc, in_=Pt[:], axis=mybir.AxisListType.X, op=ALU.add),
                         ['BuH', ptn, ptn + 'b'], ['BuH'])
                c.barrier()

            rmx = ExitStack()
            with ExitStack() as pc:
                BuHp = sbt(pc, "BuHp", [128, 64, 2, 129], F32)
                Pstp = [sbt(pc, "Pstp%d" % i, [128, 64, 2, 3], F32) for i in range(2)]
                with ExitStack() as pcu:
                    BPop = [sbt(pcu, "BPop%d" % i, [128, 8, 2, 128], BF16) for i in range(2)]
                    Uop = [sbt(pcu, "Uop%d" % i, [128, 8, 128], BF16) for i in range(2)]
                    psUp = [pst(pcu, "psUp%d" % i, [128, 1024], BF16) for i in range(2)]
                    psBp = [pst(pcu, "psBp%d" % i, [128, 512], F32) for i in range(4)]
                    memset('dve', BuHp[:, :, :, 0], 0.0, ['BuHp'])
                    pbi = 0
                    for o in range(16):
                        bpo, bpn = BPop[o % 2], "BPop%d" % (o % 2)
                        uo, uon = Uop[o % 2], "Uop%d" % (o % 2)
                        pu, pun = psUp[o % 2], "psUp%d" % (o % 2)
                        c.dma('sp', 'ld_' + bpn, bpo[:], s_bptz[:, o * 8:(o + 1) * 8, :, :], reads=['s_bptz'], writes=[bpn])
                        puv = pu[:].rearrange("p (g m) -> p g m", g=8)
                        for gl in range(8):
                            g = o * 8 + gl
                            tr(puv[:, gl, :], X2p[:, g * 16:(g + 1) * 16, :].rearrange("p c i -> p (c i)"), identb[:, :],
                               ['X2p', 'identb'], [pun])
                        cp('act' if o % 2 == 0 else 'dve', uo[:], puv, [pun], [uon])
                        for pl in range(4):
                            pb, pbn = psBp[pbi % 4], "psBp%d" % (pbi % 4)
                            pbi += 1
                            pbv = pb[:, 0:256].rearrange("p (r m) -> p r m", r=2)
                            for r in range(2):
                                for g2 in range(2):
                                    mm(pbv[:, r, :], bpo[:, pl * 2 + g2, r, :], uo[:, pl * 2 + g2, :], g2 == 0, g2 == 1, [bpn, uon], [pbn])
                            cp('act' if pl % 2 == 0 else 'dve', BuHp[:, o * 4 + pl, :, 1:129], pbv, [pbn], ['BuHp'])
                c.barrier()
                pfx.close()
                X2m = sbt_r(rmx, "X2m", [128, D, 8], BF16)
                X2sm = sbt_r(rmx, "X2sm", [128, 1024, 8], BF16)

                def main_front():
                    epsb2 = sbt(pc, "epsb2", [128, 1], F32)
                    xnTm = sbt(pc, "xnTm", [128, 16, NT], BF16)
                    with ExitStack() as pa:
                        gnbm = sbt(pa, "gnbm", [128, D], F32)
                        xl2 = [sbt(pa, "xlm%d" % i, [128, D], F32) for i in range(2)]
                        xs_ = [sbt(pa, "xsm%d" % i, [128, D], BF16) for i in range(2)]
                        xnb = [sbt(pa, "xnbm%d" % i, [128, D], BF16) for i in range(2)]
                        ss = sbt(pa, "ssm", [128, 16], F32)
                        rs = sbt(pa, "rsm", [128, 16], F32)
                        psA = [pst(pa, "psAm_%d" % i, [128, 1024], BF16) for i in range(2)]
                        c.dma('act', 'ld_gnbm', gnbm[:], gnb[:, :], writes=['gnbm'])
                        c.op('pool', lambda: P.memset(epsb2[:], EPS), (), ['epsb2'])
                        c.op('pool', lambda: P.memset(ss[:], 0.0), (), ['ssm'])

                        def tposes(t):
                            b = t % 2
                            xbn = "xnbm%d" % b
                            for hb in range(2):
                                pa_, pan = psA[hb], "psAm_%d" % hb
                                for k in range(8):
                                    kk = hb * 8 + k
                                    tr(pa_[:, k * 128:(k + 1) * 128], xnb[b][:, kk * 128:(kk + 1) * 128], identb[:, :],
                                       [xbn, 'identb'], [pan])
                                cp('act', xnTm[:, hb * 8:(hb + 1) * 8, t * 128:(t + 1) * 128],
                                   pa_[:].rearrange("p (k n) -> p k n", k=8), [pan], ['xnTm'])

                        c.dma('act', 'ld_xlm0', xl2[0][:], xm[0:128, :], writes=['xlm0'])
                        for t in range(9):
                            b = t % 2
                            xsn, xbn = "xsm%d" % b, "xnbm%d" % b
                            xl, xln = xl2[b], "xlm%d" % b
                            if t + 1 < 9:
                                nb_ = (t + 1) % 2
                                c.dma('act', 'ld_xlm%d' % nb_, xl2[nb_][:], xm[(t + 1) * 128:(t + 2) * 128, :], writes=['xlm%d' % nb_])
                            act(xs_[b][:], xl[:], AF.Square, [xln], [xsn, 'ssm'], accum_out=ss[:, t:t + 1])
                            act(rs[:, t:t + 1], ss[:, t:t + 1], AF.Ln, ['ssm', 'epsb2'], ['rsm'], scale=1.0 / D, bias=epsb2[:, 0:1])
                            act(rs[:, t:t + 1], rs[:, t:t + 1], AF.Exp, ['rsm'], ['rsm'], scale=-0.5)
                            act(xs_[b][:], xl[:], AF.Copy, [xln, 'rsm'], [xsn], scale=rs[:, t:t + 1])
                            tt('pool', xnb[b][:], xs_[b][:], gnbm[:], ALU.mult, [xsn, 'gnbm'], [xbn])
                            if t >= 1:
                                tposes(t - 1)
                            yield
                        tposes(8)
                        yield
                    c.fence('pool', include_self=True)
                    with ExitStack() as pb:
                        Wb = [sbt(pb, "Wxm%d" % i, [128, 16, 512], BF16) for i in range(2)]
                        XS = [sbt(pb, "XSm%d" % i, [128, 512], BF16) for i in range(2)]
                        XSr = sbt(pb, "XSrm", [128, 8, 512], BF16)
                        psX = [pst(pb, "psXm%d" % i, [128, 512], F32) for i in range(6)]
                        pi = 0
                        for cb in range(4):
                            Wt, wn = load_w(Wb, "Wxm", w_in[:, 6144 + cb * 512:6144 + (cb + 1) * 512], 512)
                            for i in range(9):
                                px, pxn = psX[pi % 6], "psXm%d" % (pi % 6)
                                pi += 1
                                for k in range(16):
                                    lt = xnTm[:, k, i:NPR:8] if i < 8 else xnTm[:, k, NPR:NT]
                                    mm(px[:], lt, Wt[:, k, :], k == 0, k == 15,
                                       ['xnTm', '%s_%s' % (wn, 'a' if k < 4 else 'b')], [pxn])
                                if i < 8:
                                    cp('act', X2m[:, cb * 512:(cb + 1) * 512, i], px[:], [pxn], ['X2m'])
                                else:
                                    xs, xsn = XS[cb % 2], "XSm%d" % (cb % 2)
                                    cp('act', xs[:], px[:], [pxn], [xsn])
                                    q = cb // 2
                                    c.dma('act', 'rgm', XSr[q * 64:q * 64 + 16, :, :], xs[:], reads=[xsn, 'XSrm'], writes=['XSrm'])
                                    cp('act', X2sm[q * 64:q * 64 + 16, (cb % 2) * 512:(cb % 2 + 1) * 512, :].rearrange("p n i -> p i n"),
                                       XSr[q * 64:q * 64 + 16, :, :], ['XSrm'], ['X2sm'])
                                yield

                gm = main_front()
                scan_chain(BuHp, Pstp, side_gen=gm)
                cp('dve', Hmid[:], BuHp[:, :, :, 128], ['BuHp'], ['Hmid'])
                for _ in gm:
                    pass
            c.barrier()
            if stop == 0:
                c.dead = True

            with ExitStack() as s5:
                Uall = sbt(s5, "Uall", [128, 128, 144], BF16)
                for is_main in (True,):
                    ntiles = 9 if is_main else 8
                    x_dram = xm if is_main else xp
                    ncol = 144 if is_main else 128
                    with ExitStack() as sx:
                        X2 = X2m
                        X2s = X2sm
                        with ExitStack() as su:
                            psU = [pst(su, "psU%d" % i, [128, 1024], BF16) for i in range(2)]
                            psUs = [pst(su, "psUs%d" % i, [128, 512], F32) for i in range(2)]
                            for o in range(16):
                                pu, pun = psU[o % 2], "psU%d" % (o % 2)
                                pus, pusn = psUs[o % 2], "psUs%d" % (o % 2)
                                puv = pu[:].rearrange("p (g m) -> p g m", g=8)
                                pusv = pus[:, 0:128].rearrange("p (g m) -> p g m", g=8)
                                for gl in range(8):
                                    g = o * 8 + gl
                                    tr(puv[:, gl, :], X2[:, g * 16:(g + 1) * 16, :].rearrange("p c i -> p (c i)"), identb[:, :],
                                       ['X2', 'identb'], [pun])
                                    if is_main:
                                        q = g // 64
                                        col = (g % 64) * 16
                                        mm(pusv[:, gl, :], X2s[q * 64:q * 64 + 16, col:col + 16, :].rearrange("p c i -> p (c i)"),
                                           identb[q * 64:q * 64 + 16, q * 64:q * 64 + 16], True, True, ['X2s', 'identb'], [pusn])
                                cp('act' if o % 2 == 0 else 'dve', Uall[:, o * 8:(o + 1) * 8, 0:128], puv, [pun], ['Uall'])
                                if is_main:
                                    cp('dve' if o % 2 == 0 else 'act', Uall[:, o * 8:(o + 1) * 8, 128:144], pusv, [pusn], ['Uall'])
                        c.barrier()
                    c.barrier()

                    rmx.close()
                    ybT = sbt_r(st, "ybT", [128, 16, NT], BF16)
                    with ExitStack() as sc:
                        BuH = sbt(sc, "BuH", [128, 64, 2, 145], F32)
                        Pst = [sbt(sc, "Pst%d" % i, [128, 64, 2, 3], F32) for i in range(2)]
                        BPo = [sbt(sc, "BPo%d" % i, [128, 8, 2, 128], BF16) for i in range(2)]
                        Hfin = sbt(sc, "Hfin", [128, 64, 2], F32)
                        if is_main:
                            HS0 = sbt(sc, "HS0", [128, 64, 2, 16], F32)
                            T1s = sbt(sc, "T1s", [128, 32, 2, 16], F32)
                            Pcs = sbt(sc, "Pcs", [128, 32, 2, 16], F32)
                            CQo = [sbt(sc, "CQo%d" % i, [128, 4, 2, 128], BF16) for i in range(2)]
                            MSo = [sbt(sc, "MSo%d" % i, [128, 8, 128], BF16) for i in range(2)]
                            Hbf = [sbt(sc, "Hbf%d" % i, [128, 4, 2, 144], BF16) for i in range(2)]
                            Y2o = [sbt(sc, "Y2o%d" % i, [128, 8, 128], BF16) for i in range(2)]
                            Y2so = [sbt(sc, "Y2so%d" % i, [16, 8, 128], BF16) for i in range(2)]
                            c.dma('sp', 'ld_h0', HS0[:], h0[:, :, :, :], writes=['HS0'])
                            cp('dve', BuH[:, :, :, 0], Hmid[:], ['Hmid'], ['BuH'])
                        else:
                            memset('dve', BuH[:, :, :, 0], 0.0, ['BuH'])
                        with ExitStack() as sbu:
                            psB = [pst(sbu, "psB%d" % i, [128, 512], F32) for i in range(4)]
                            pbi = 0
                            for o in range(16):
                                bpo = BPo[o % 2]
                                bpn = "BPo%d" % (o % 2)
                                c.dma('sp', 'ld_' + bpn, bpo[:], s_bptz[:, o * 8:(o + 1) * 8, :, :], reads=['s_bptz'], writes=[bpn])
                                for pl in range(4):
                                    pb = psB[pbi % 4]
                                    pbn = "psB%d" % (pbi % 4)
                                    pbi += 1
                                    pbv = pb[:, 0:288].rearrange("p (r m) -> p r m", r=2)
                                    for r in range(2):
                                        for g2 in range(2):
                                            g = o * 8 + pl * 2 + g2
                                            mm(pbv[:, r, 0:ncol], bpo[:, pl * 2 + g2, r, :], Uall[:, g, 0:ncol],
                                               g2 == 0, g2 == 1, [bpn, 'Uall'], [pbn])
                                    prl = o * 4 + pl
                                    cp('act' if pl % 2 == 0 else 'dve', BuH[:, prl, :, 1:1 + ncol], pbv[:, :, 0:ncol], [pbn], ['BuH'])
                        W8v = W8[:].rearrange("p n (d c) -> p n d c", d=2)
                        c.barrier()
                        for m in range(1, 129):
                            Pt, ptn = Pst[m % 2], "Pst%d" % (m % 2)
                            Xp_ = BuH[:, :, :, m - 1]
                            Xc = BuH[:, :, :, m]
                            cp('act', Pt[:, :, :, 2], Xc, ['BuHin'], [ptn + 'b'])
                            tt('dve', Pt[:, :, :, 0:2], Xp_.unsqueeze(2).to_broadcast([128, 64, 2, 2]), W8v, ALU.mult,
                               ['BuH', 'W8'], [ptn])
                            c.op('dve', lambda: V.tensor_reduce(out=Xc, in_=Pt[:], axis=mybir.AxisListType.X, op=ALU.add),
                                 ['BuH', ptn, ptn + 'b'], ['BuH'])
                        c.barrier()
                        if not is_main:
                            cp('dve', Hmid[:], BuH[:, :, :, 128], ['BuH'], ['Hmid'])
                        else:
                            cp('dve', Hfin[:], BuH[:, :, :, 128], ['BuH'], ['Hfin'])
                            c.dma('sp', 'st_hp', hp_out[:, :, :], Hfin[:], reads=['Hfin'])
                            for hf in range(2):
                                prs = slice(hf * 32, hf * 32 + 32)
                                for cc in range(2):
                                    tt('dve', Pcs[:], HS0[:, prs, cc, :].unsqueeze(2).to_broadcast([128, 32, 2, 16]),
                                       W8v[:, prs, :, cc].unsqueeze(3).to_broadcast([128, 32, 2, 16]), ALU.mult, ['HS0', 'W8'], ['Pcs'])
                                    if cc == 0:
                                        tt('dve', T1s[:], BuH[:, prs, :, 129:145], Pcs[:], ALU.add, ['BuH', 'Pcs'], ['T1s'])
                                    else:
                                        tt('dve', T1s[:], T1s[:], Pcs[:], ALU.add, ['T1s', 'Pcs'], ['T1s'])
                                c.dma('sp', 'st_hs', hs_out[:, prs, :, :], T1s[:], reads=['T1s'])
                            with ExitStack() as sy:
                                psY = [pst(sy, "psY%d" % i, [128, 512], F32) for i in range(2)]
                                psYs = [pst(sy, "psYs%d" % i, [128, 512], F32) for i in range(2)]
                                psT2 = [pst(sy, "psT2%d" % i, [128, 1024], BF16) for i in range(2)]
                                psT2s = [pst(sy, "psT2s%d" % i, [128, 512], F32) for i in range(2)]
                                def y_transposes(o):
                                    b2 = o % 2
                                    y2o, y2n = Y2o[b2], "Y2o%d" % b2
                                    y2so, y2sn = Y2so[b2], "Y2so%d" % b2
                                    p2, p2n = psT2[b2], "psT2%d" % b2
                                    p2s, p2sn = psT2s[b2], "psT2s%d" % b2
                                    p2v = p2[:].rearrange("p (j m) -> p j m", j=8)
                                    p2sv = p2s[:, 0:128].rearrange("p (j s) -> p j s", j=8)
                                    for j in range(8):
                                        tr(p2v[:, j, :], y2o[:, j, :], identb[:, :], [y2n, 'identb'], [p2n])
                                        mm(p2sv[:, j, :], y2so[0:16, j, :], identb[0:16, 0:16], True, True, [y2sn, 'identb'], [p2sn])
                                    ybv = ybT[:, o, 0:NPR].rearrange("p (m j) -> p j m", j=8)
                                    cp('dve', ybv[:, 0:4, :], p2v[:, 0:4, :], [p2n], ['ybT'])
                                    cp('act', ybv[:, 4:8, :], p2v[:, 4:8, :], [p2n], ['ybT'])
                                    cp('dve', ybT[:, o, NPR:NT].rearrange("p (s j) -> p j s", j=8), p2sv, [p2sn], ['ybT'])

                                qi = 0
                                for o in range(16):
                                    b2 = o % 2
                                    cqo, cqn = CQo[b2], "CQo%d" % b2
                                    mso, msn = MSo[b2], "MSo%d" % b2
                                    hbf, hbn = Hbf[b2], "Hbf%d" % b2
                                    y2o, y2n = Y2o[b2], "Y2o%d" % b2
                                    y2so, y2sn = Y2so[b2], "Y2so%d" % b2
                                    c.dma('sp', 'ld_' + cqn, cqo[:], s_cq1[:, o * 4:(o + 1) * 4, :, :], reads=['s_cq1'], writes=[cqn])
                                    c.dma('sp', 'ld_' + msn, mso[:], s_msup[:, o * 8:(o + 1) * 8, :], reads=['s_msup'], writes=[msn])
                                    cp('pool', hbf[:, :, :, 0:128], BuH[:, o * 4:(o + 1) * 4, :, 0:128], ['BuH'], [hbn])
                                    cp('pool', hbf[:, :, :, 128:144], HS0[:, o * 4:(o + 1) * 4, :, :], ['HS0'], [hbn])
                                    for quad in range(2):
                                        py, pyn = psY[qi % 2], "psY%d" % (qi % 2)
                                        pys, pysn = psYs[qi % 2], "psYs%d" % (qi % 2)
                                        qi += 1
                                        pyv = py[:].rearrange("p (g n) -> p g n", g=4)
                                        pysv = pys[:].rearrange("p (g n) -> p g n", g=4)
                                        for gl4 in range(4):
                                            gl = quad * 4 + gl4
                                            g = o * 8 + gl
                                            pl, g2 = gl // 2, gl % 2
                                            hs = slice(g2 * 64, g2 * 64 + 64)
                                            mm(pyv[:, gl4, :], hbf[hs, pl, 0, 0:128], cqo[hs, pl, 0, :], True, False, [hbn, cqn], [pyn])
                                            mm(pyv[:, gl4, :], hbf[hs, pl, 1, 0:128], cqo[hs, pl, 1, :], False, False, [hbn, cqn], [pyn])
                                            mm(pyv[:, gl4, :], Uall[:, g, 0:128], mso[:, gl, :], False, True, ['Uall', msn], [pyn])
                                            mm(pysv[0:16, gl4, :], hbf[hs, pl, 0, 128:144], cqo[hs, pl, 0, :], True, False, [hbn, cqn], [pysn])
                                            mm(pysv[0:16, gl4, :], hbf[hs, pl, 1, 128:144], cqo[hs, pl, 1, :], False, False, [hbn, cqn], [pysn])
                                            mm(pysv[0:16, gl4, :], Uall[:, g, 128:144], mso[:, gl, :], False, True, ['Uall', msn], [pysn])
                                        act(y2o[:, :, quad * 64:(quad + 1) * 64].rearrange("p j (g c) -> p g j c", g=4),
                                            pyv.rearrange("p g (j c) -> p g j c", j=8), AF.Gelu_apprx_tanh, [pyn], [y2n])
                                        act(y2so[:, :, quad * 64:(quad + 1) * 64].rearrange("p j (g c) -> p g j c", g=4),
                                            pysv[0:16].rearrange("p g (j c) -> p g j c", j=8), AF.Gelu_apprx_tanh, [pysn], [y2sn])
                                    if o >= 1:
                                        y_transposes(o - 1)
                                y_transposes(15)
                    c.barrier()
                    if (stop == 2 and not is_main) or (stop == 4 and is_main):
                        c.dead = True
            c.barrier()

            outbT = sbt(st, "outbT", [128, 16, NT], BF16)
            with ExitStack() as m_:
                xnT = sbt(m_, "xnT", [128, 16, NT], BF16)
                tblocks = [(0, 512), (512, 512), (1024, 128)]
                with ExitStack() as g_:
                    gpa = phaseA_gen(g_, xm, 9, xnT)
                    Wb = [sbt(g_, "Wg%d" % i, [128, 16, 512], BF16) for i in range(2)]
                    bglu_t = sbt(g_, "bglu_t", [128, 16], F32)
                    sg = [sbt(g_, "sg%d" % i, [128, 512], BF16) for i in range(2)]
                    psG = [pst(g_, "psG%d" % i, [128, 512], F32) for i in range(4)]
                    c.dma('sp', 'ld_bglu', bglu_t[:], bglu[:, :], writes=['bglu'])
                    pi = 0
                    for ob in range(4):
                        Wt, wn = load_w(Wb, "Wg", w_glu[:, ob * 512:(ob + 1) * 512], 512)
                        for mo in range(4):
                            oc = ob * 4 + mo
                            for (t0, tn) in tblocks:
                                pg, pgn = psG[pi % 4], "psG%d" % (pi % 4)
                                sgt, sgn = sg[pi % 2], "sg%d" % (pi % 2)
                                pi += 1
                                for k in range(16):
                                    mm(pg[:, 0:tn], Wt[:, k, mo * 128:(mo + 1) * 128], ybT[:, k, t0:t0 + tn], k == 0, k == 15,
                                       ['%s_%s' % (wn, 'a' if k < 4 else 'b'), 'ybT'], [pgn])
                                act(sgt[:, 0:tn], pg[:, 0:tn], AF.Sigmoid, [pgn, 'bglu'], [sgn], bias=bglu_t[:, oc:oc + 1])
                                tt('dve', outbT[:, oc, t0:t0 + tn], sgt[:, 0:tn], ybT[:, oc, t0:t0 + tn], ALU.mult,
                                   [sgn, 'ybT'], ['outb'])
                                if pi % 4 == 0:
                                    next(gpa, None)
                    for _ in gpa:
                        pass
                    for ob in range(4):
                        Wt, wn = load_w(Wb, "Wg", w_in[:, 8192 + ob * 512:8192 + (ob + 1) * 512], 512)
                        for mo in range(4):
                            oc = ob * 4 + mo
                            for (t0, tn) in tblocks:
                                pg, pgn = psG[pi % 4], "psG%d" % (pi % 4)
                                sgt, sgn = sg[pi % 2], "sg%d" % (pi % 2)
                                pi += 1
                                for k in range(16):
                                    mm(pg[:, 0:tn], Wt[:, k, mo * 128:(mo + 1) * 128], xnT[:, k, t0:t0 + tn], k == 0, k == 15,
                                       ['%s_%s' % (wn, 'a' if k < 4 else 'b'), 'xnT'], [pgn])
                                act(sgt[:, 0:tn], pg[:, 0:tn], AF.Silu, [pgn], [sgn])
                                tt('dve', outbT[:, oc, t0:t0 + tn], sgt[:, 0:tn], outbT[:, oc, t0:t0 + tn], ALU.mult,
                                   [sgn, 'outb'], ['outb'])
                c.barrier()
                if stop == 5:
                    c.dead = True

                with ExitStack() as a_:
                    gvb_t = sbt(a_, "gvb_t", [128, D], F32)
                    wsTm = sbt(a_, "wsTm", [128, 8, 128], BF16)
                    wsSm = sbt(a_, "wsSm", [128, 8, 128], BF16)
                    bs_b = sbt(a_, "bs_b", [1, 8, 128], BF16)
                    bsS_b = sbt(a_, "bsS_b", [1, 8, 128], BF16)
                    c.dma('sp', 'ld_gvb', gvb_t[:], gvb[:, :], writes=['gvb'])
                    with ExitStack() as tmp_:
                        wtmp = sbt(tmp_, "wtmp", [128, 8, 128], F32)
                        mtmp = sbt(tmp_, "mtmp", [128, 128], F32)
                        bs_t = sbt(tmp_, "bs_t", [1, 8, 128], F32)
                        bsS_t = sbt(tmp_, "bsS_t", [1, 8, 128], F32)
                        c.dma('sp', 'ld_wsT', wtmp[:], wsT[:, :, :], writes=['wtmp'])
                        c.dma('sp', 'ld_msk', mtmp[:], mask_ts[:, :], writes=['mtmp'])
                        tt('dve', wsTm[:], wtmp[:], mtmp[:].unsqueeze(1).to_broadcast([128, 8, 128]), ALU.mult, ['wtmp', 'mtmp'], ['wsTm'])
                        c.dma('sp', 'ld_wsT', wtmp[:], wsS[:, :, :], reads=['wtmp'], writes=['wtmp'])
                        c.dma('sp', 'ld_msk', mtmp[:], mask_blk[:, :], reads=['mtmp'], writes=['mtmp'])
                        tt('dve', wsSm[:], wtmp[:], mtmp[:].unsqueeze(1).to_broadcast([128, 8, 128]), ALU.mult, ['wtmp', 'mtmp'], ['wsSm'])
                        c.dma('sp', 'ld_bs', bs_t[:], bsrow[:, :, :], writes=['bs_t'])
                        c.dma('sp', 'ld_bsS', bsS_t[:], bsSrow[:, :, :], writes=['bsS_t'])
                        cp('dve', bs_b[:], bs_t[:], ['bs_t'], ['bs_b'])
                        cp('dve', bsS_b[:], bsS_t[:], ['bsS_t'], ['bsS_b'])
                    c.barrier()
                    Wh = [sbt(a_, "Wh%d" % i, [128, 16, 768], BF16) for i in range(2)]
                    uT = sbt(a_, "uT", [128, 2, NT], BF16)
                    gaT = sbt(a_, "gaT", [128, 2, NT], BF16)
                    vg9 = sbt(a_, "vg9", [128, 9, 256], F32)
                    junkv = sbt(a_, "junkv", [128, 256], BF16)
                    vnb = sbt(a_, "vnb", [128, 9, 256], BF16)
                    vnsh = [sbt(a_, "vnsh%d" % i, [128, 256], F32) for i in range(2)]
                    ssv = sbt(a_, "ssv", [128, 80], F32)
                    rsv = sbt(a_, "rsv", [128, 80], F32)
                    psA_ = [pst(a_, "psAm%d" % i, [128, 512], F32) for i in range(4)]
                    psV = [pst(a_, "psV%d" % i, [128, 512], F32) for i in range(2)]
                    psM_ = [pst(a_, "psMx%d" % i, [128, 512], F32) for i in range(2)]
                    memset('dve', ssv[:], 0.0, ['ssv'])
                    pi = 0
                    for h in range(8):
                        i = wslot[0] % 2
                        wslot[0] += 1
                        Wt, wn = Wh[i], "Wh%d" % i
                        for j, c0 in ((1, 2048 + h * 256), (0, h * 256), (2, 4096 + h * 256)):
                            srcv = w_in[:, c0:c0 + 256].rearrange("(k p) n -> p k n", p=128)
                            for kq in range(4):
                                grp = ('a' if kq == 0 else 'b') if j == 1 else ('c' if j == 0 else 'd')
                                c.dma('pool', 'w_%s_%s' % (wn, grp), Wt[:, kq * 4:(kq + 1) * 4, j * 256:(j + 1) * 256],
                                      srcv[:, kq * 4:(kq + 1) * 4, :], writes=['%s_%s' % (wn, grp)],
                                      nowait=((j == 1 and kq > 1) or (j != 1 and kq > 0)))
                        for t in range(9):
                            pv, pvn = psV[t % 2], "psV%d" % (t % 2)
                            for k in range(16):
                                mm(pv[:, 0:256], xnT[:, k, t * 128:(t + 1) * 128], Wt[:, k, 256:512], k == 0, k == 15, ['xnT', '%s_%s' % (wn, 'a' if k < 4 else 'b')], [pvn])
                            act(vg9[:, t, :], pv[:, 0:256], AF.Gelu_apprx_tanh, [pvn], ['vg9'])
                        for t in range(9):
                            col = h * 9 + t
                            act(junkv[:], vg9[:, t, :], AF.Square, ['vg9'], ['junkv', 'ssv'], accum_out=ssv[:, col:col + 1])
                        cs = slice(h * 9, h * 9 + 9)
                        ts('dve', rsv[:, cs], ssv[:, cs], 1.0 / 256, EPS, ALU.mult, ALU.add, ['ssv'], ['rsv'])
                        act(rsv[:, cs], rsv[:, cs], AF.Sqrt, ['rsv'], ['rsv'])
                        c.op('dve', lambda: V.reciprocal(out=rsv[:, cs], in_=rsv[:, cs]), ['rsv'], ['rsv'])
                        for t in range(9):
                            col = h * 9 + t
                            stt(vnb[:, t, :], vg9[:, t, :], rsv[:, col:col + 1], gvb_t[:, h * 256:(h + 1) * 256], ALU.mult, ALU.mult,
                                ['vg9', 'rsv', 'gvb'], ['vnb'])
                            if t == 8:
                                vsn = "vnsh%d" % (h % 2)
                                stt(vnsh[h % 2][:], vg9[:, t, :], rsv[:, col:col + 1], gvb_t[:, h * 256:(h + 1) * 256],
                                    ALU.mult, ALU.mult, ['vg9', 'rsv', 'gvb'], [vsn])
                                c.dma('sp', 'st_' + vsn, vns_out[:, h * 256:(h + 1) * 256], vnsh[h % 2][:], reads=[vsn])
                        for (j, dst, dn, fn) in ((0, uT, 'uT', AF.Gelu_apprx_tanh), (2, gaT, 'gaT', AF.Silu)):
                            for mo in range(2):
                                for (t0, tn) in tblocks:
                                    pg, pgn = psA_[pi % 4], "psAm%d" % (pi % 4)
                                    pi += 1
                                    for k in range(16):
                                        mm(pg[:, 0:tn], Wt[:, k, j * 256 + mo * 128:j * 256 + (mo + 1) * 128], xnT[:, k, t0:t0 + tn],
                                           k == 0, k == 15, ['%s_%s' % (wn, 'c' if j == 0 else 'd'), 'xnT'], [pgn])
                                    act(dst[:, mo, t0:t0 + tn], pg[:, 0:tn], fn, [pgn], [dn])
                                    if j == 2:
                                        tt('pool', uT[:, mo, t0:t0 + tn], uT[:, mo, t0:t0 + tn], gaT[:, mo, t0:t0 + tn], ALU.mult,
                                           ['uT', 'gaT'], ['uT'])
                        for mo in range(2):
                            for tb in range(3):
                                tiles = [0, 1, 2, 3] if tb == 0 else ([4, 5, 6, 7] if tb == 1 else [8])
                                pm, pmn = psM_[(mo * 3 + tb) % 2], "psMx%d" % ((mo * 3 + tb) % 2)
                                for ti, t in enumerate(tiles):
                                    wsm = wsTm if t < 8 else wsSm
                                    bsb = bs_b if t < 8 else bsS_b
                                    o_ = pm[:, ti * 128:(ti + 1) * 128]
                                    mm(o_, vnb[:, t, mo * 128:(mo + 1) * 128], wsm[:, h, :], True, False, ['vnb', 'wsTm', 'wsSm'], [pmn])
                                    mm(o_, ones1[0:1, :], bsb[0:1, h, :], False, True, ['ones1', 'bs_b', 'bsS_b'], [pmn])
                                t0 = tiles[0] * 128
                                tn = len(tiles) * 128
                                tt('dve', ybT[:, h * 2 + mo, t0:t0 + tn], pm[:, 0:tn], uT[:, mo, t0:t0 + tn], ALU.mult,
                                   [pmn, 'uT'], ['outa'])
                c.barrier()
            c.barrier()

            if stop == 6:
                c.dead = True
            with ExitStack() as o_s:
                Wo = [sbt(o_s, "Wo%d" % i, [128, 32, 512], BF16) for i in range(2)]
                gfb_t = sbt(o_s, "gfb_t", [128, D], F32)
                xnew = sbt(o_s, "xnew", [128, 5, D], F32)
                junko = sbt(o_s, "junko", [128, D], BF16)
                sso = sbt(o_s, "sso", [128, 16], F32)
                rso = sbt(o_s, "rso", [128, 16], F32)
                psO = [pst(o_s, "psO%d" % i, [128, 512], F32) for i in range(4)]
                c.dma('sp', 'ld_gfb', gfb_t[:], gfb[:, :], writes=['gfb'])
                memset('dve', sso[:], 0.0, ['sso'])
                pi = 0
                for (tiles) in ([0, 1, 2, 3, 4], [5, 6, 7, 8]):
                    for tl, t in enumerate(tiles):
                        c.dma('sp', 'ld_xr%d' % tl, xnew[:, tl, :], xm[t * 128:(t + 1) * 128, :], reads=['xnew%d' % tl], writes=['xnew%d' % tl])
                    for cb in range(4):
                        i = wslot[0] % 2
                        wslot[0] += 1
                        Wt, wn = Wo[i], "Wo%d" % i
                        srcv = w_out[:, cb * 512:(cb + 1) * 512].rearrange("(k p) n -> p k n", p=128)
                        for kq in range(8):
                            grp = 'a' if kq == 0 else 'b'
                            c.dma('pool', 'w_%s_%s' % (wn, grp), Wt[:, kq * 4:(kq + 1) * 4, :], srcv[:, kq * 4:(kq + 1) * 4, :],
                                  writes=['%s_%s' % (wn, grp)], nowait=(kq > 1))
                        for tl, t in enumerate(tiles):
                            po, pon = psO[pi % 4], "psO%d" % (pi % 4)
                            pi += 1
                            for k in range(32):
                                mm(po[:, :], (ybT if k < 16 else outbT)[:, k % 16, t * 128:(t + 1) * 128], Wt[:, k, :], k == 0, k == 31, ['mixT', '%s_%s' % (wn, 'a' if k < 4 else 'b')], [pon])
                            tt('dve', xnew[:, tl, cb * 512:(cb + 1) * 512], po[:, :], xnew[:, tl, cb * 512:(cb + 1) * 512], ALU.add,
                               [pon, 'xnew%d' % tl], ['xnew%d' % tl])
                    for tl, t in enumerate(tiles):
                        xn_ = 'xnew%d' % tl
                        act(junko[:], xnew[:, tl, :], AF.Square, [xn_], ['junko', 'sso'], accum_out=sso[:, t:t + 1])
                        ts('dve', rso[:, t:t + 1], sso[:, t:t + 1], 1.0 / D, EPS, ALU.mult, ALU.add, ['sso'], ['rso'])
                        act(rso[:, t:t + 1], rso[:, t:t + 1], AF.Sqrt, ['rso'], ['rso'])
                        c.op('dve', lambda: V.reciprocal(out=rso[:, t:t + 1], in_=rso[:, t:t + 1]), ['rso'], ['rso'])
                        stt(xnew[:, tl, :], xnew[:, tl, :], rso[:, t:t + 1], gfb_t[:], ALU.mult, ALU.mult, [xn_, 'rso', 'gfb'], [xn_])
                        c.dma('sp', 'st_y%d' % tl, y_out[t * 128:(t + 1) * 128, :], xnew[:, tl, :], reads=[xn_], writes=[])
        except _StopBuild:
            pass
        for k, s in c.sems.items():
            if k in ('pe', 'act', 'dve', 'pool'):
                continue
            if c.cnt[k] > 0:
                nc.sync.wait_ge(s, c.cnt[k])
    return nc


def _ls(a):
    return np.ascontiguousarray(a.reshape(64, 2, 64).transpose(1, 2, 0).reshape(128, 64))


def make_in_maps(x_prompt, x_sample, state_ssm_re, state_ssm_im, g_norm, w_in, g_v, w_s, b_s,
                 a_re, a_im, log_dt, b_re, b_im, c_re, c_im, d_skip, w_glu, b_glu, w_out, g_final):
    f = np.float32
    x_prompt = np.asarray(x_prompt, f)
    x_sample = np.asarray(x_sample, f)
    shared = {}
    shared["w_in"] = np.ascontiguousarray(np.asarray(w_in, f)[0])
    shared["w_glu"] = np.ascontiguousarray(np.asarray(w_glu, f)[0])
    shared["w_out"] = np.ascontiguousarray(np.asarray(w_out, f)[0])
    shared["gnb"] = np.ascontiguousarray(np.broadcast_to(np.asarray(g_norm, f)[0][None, :], (128, D)))
    shared["gncol"] = np.ascontiguousarray(np.asarray(g_norm, f)[0].reshape(16, 128).T)
    shared["gvb"] = np.ascontiguousarray(np.broadcast_to(np.asarray(g_v, f)[0][None, :], (128, D)))
    shared["gfb"] = np.ascontiguousarray(np.broadcast_to(np.asarray(g_final, f)[None, :], (128, D)))
    shared["bglu"] = np.ascontiguousarray(np.asarray(b_glu, f)[0].reshape(16, 128).T)
    ws = np.asarray(w_s, f)[0]
    shared["wsT"] = np.ascontiguousarray(ws.transpose(2, 0, 1))
    wss = np.zeros((16, 8, 8, 16, 8), f)
    w8 = ws[:, :8, :8]
    for s in range(16):
        wss[s, :, :, s, :] = w8.transpose(2, 0, 1)
    shared["wsS"] = np.ascontiguousarray(wss.reshape(128, 8, 128))
    shared["mask_ts"] = np.triu(np.ones((128, 128), f))
    mb = np.zeros((16, 8, 16, 8), f)
    for s in range(16):
        mb[s, :, s, :] = np.triu(np.ones((8, 8), f))
    shared["mask_blk"] = mb.reshape(128, 128)
    m16 = np.zeros((16, 8, 8, 16), f)
    pI = np.zeros((16, 8, 8, 16), f)
    for i in range(8):
        m16[:, i, i:, :] = 1.0
        for cc in range(16):
            pI[cc, i, i, cc] = 1.0
    shared["mask16"] = np.ascontiguousarray(m16.reshape(128, 128))
    shared["permI"] = np.ascontiguousarray(pI.reshape(128, 128))
    bs = np.asarray(b_s, f)[0]
    shared["bsrow"] = np.ascontiguousarray(bs[None, :, :])
    shared["bsSrow"] = np.ascontiguousarray(np.tile(bs[:, :8], (1, 16))[None, :, :])
    shared["are"] = _ls(np.asarray(a_re, f)[0])
    shared["aim"] = _ls(np.asarray(a_im, f)[0])
    shared["ldt"] = _ls(np.broadcast_to(np.asarray(log_dt, f)[0][:, None], (128, 64)))

    def _lb(a):
        return np.ascontiguousarray(a.reshape(64, 2, 64, 16).transpose(1, 2, 0, 3).reshape(128, 64, 16))
    shared["bre"] = _lb(np.asarray(b_re, f)[0])
    shared["bim"] = _lb(np.asarray(b_im, f)[0])
    shared["cre"] = _lb(np.asarray(c_re, f)[0].transpose(0, 2, 1))
    shared["cim"] = _lb(np.asarray(c_im, f)[0].transpose(0, 2, 1))
    shared["dcol"] = np.ascontiguousarray(np.repeat(np.asarray(d_skip, f)[0].reshape(128, 16).T, 8, axis=0))
    qv = np.zeros((128, 24, 64), f)
    for i in range(8):
        qv[:, i, :] = 7 - i
    for s in range(16):
        qv[:, 8 + s, :] = s - 7
    shared["qv"] = qv
    shared["identf_in"] = np.eye(128, dtype=f)

    sre = np.asarray(state_ssm_re, f)[0]
    sim = np.asarray(state_ssm_im, f)[0]
    in_maps = []
    for cid in range(NCORES):
        sq, half = cid // 2, cid % 2
        m = dict(shared)
        xs = x_sample[cid * 16:(cid + 1) * 16].reshape(128, D)
        m["xm"] = np.ascontiguousarray(np.concatenate([x_prompt[sq, half * NPR:(half + 1) * NPR], xs], axis=0))
        m["xp"] = np.ascontiguousarray(x_prompt[sq, 0:NPR]) if half == 1 else np.zeros((NPR, D), f)
        hh = np.stack([sre[cid * 16:(cid + 1) * 16], sim[cid * 16:(cid + 1) * 16]], axis=0)
        hh = hh.reshape(2, 16, 64, 2, 64).transpose(3, 4, 2, 0, 1)
        m["h0"] = np.ascontiguousarray(hh.reshape(128, 64, 2, 16))
        in_maps.append(m)
    return in_maps


def assemble(R):
    f = np.float32
    y_prompt = np.zeros((4, 2048, D), f)
    y_sample = np.zeros((128, 8, D), f)
    re_p = np.zeros((1, 4, 128, 64), f)
    im_p = np.zeros((1, 4, 128, 64), f)
    re_s = np.zeros((1, 128, 128, 64), f)
    im_s = np.zeros((1, 128, 128, 64), f)
    v_s = np.zeros((1, 128, 8, D), f)
    for cid in range(NCORES):
        sq, half = cid // 2, cid % 2
        y = R[cid]["y"]
        y_prompt[sq, half * NPR:(half + 1) * NPR] = y[:NPR]
        y_sample[cid * 16:(cid + 1) * 16] = y[NPR:].reshape(16, 8, D)
        if half == 1:
            hp = R[cid]["hp"].reshape(2, 64, 64, 2).transpose(2, 0, 1, 3).reshape(128, 64, 2)
            re_p[0, sq] = hp[..., 0]
            im_p[0, sq] = hp[..., 1]
        hs = R[cid]["hs"].reshape(2, 64, 64, 2, 16).transpose(4, 3, 2, 0, 1).reshape(16, 2, 128, 64)
        re_s[0, cid * 16:(cid + 1) * 16] = hs[:, 0]
        im_s[0, cid * 16:(cid + 1) * 16] = hs[:, 1]
        v_s[0, cid * 16:(cid + 1) * 16] = R[cid]["vns"].reshape(16, 8, D)
    return (y_prompt, y_sample, re_p, im_p, re_s, im_s, v_s)


def kernel(**inputs):
    in_maps = make_in_maps(**inputs)
    nc = build_nc()
    res = run_bass_kernel_spmd(nc, in_maps, core_ids=list(range(NCORES)))
    return assemble(res.results)
```
